# Optimizing a Trainium2 kernel written in Bass

```python
import functools
import math
import jax
import jax.numpy as jnp
from jax import lax
import numpy as np

D_MODEL = 1024
BATCH = 32
SEQ = 256
DEPTH = 2
DEC_BATCH = 2
DEC_SEQ = 4096
PAST_LEN = 512

GRID_W = 64
N_DIR = 2
N_BRANCH = 3
BRANCH_W = 512
N_MOD = 6
EPS = 1e-6
ML_HEADS = 4
ML_DK = 128
ML_DV = 128
ML_WIDTH = ML_HEADS * ML_DV
ML_CHUNK = 64
S5_WIDTH = BRANCH_W
S5_GROUP = 16
S5_GROUPS = S5_WIDTH // S5_GROUP
S5_STATE = 64
GD_HEADS = 4
GD_DK = 128
GD_DV = 128
GD_WIDTH = GD_HEADS * GD_DV
GD_CHUNK = 64
CONV_K = 5
D_FF = -(-8 * D_MODEL // (3 * 256)) * 256
IN_SIZES = (ML_HEADS * ML_DK, ML_HEADS * ML_DK, ML_WIDTH, ML_WIDTH, N_DIR * ML_HEADS, N_DIR * ML_HEADS,
            S5_WIDTH,
            3 * GD_WIDTH, GD_WIDTH, N_DIR * GD_HEADS, N_DIR * GD_HEADS,
            N_BRANCH * D_MODEL)
D_IN = sum(IN_SIZES)

kernel_name = 'hybrid_mlstm_s5_gdn_diffusion_step'


def rmsnorm(x, w):
    xf = x.astype(jnp.float32)
    y = xf * lax.rsqrt(jnp.mean(xf * xf, axis=-1, keepdims=True) + EPS)
    return (y * w.astype(jnp.float32)).astype(x.dtype)


def l2norm(x):
    return x * lax.rsqrt(jnp.sum(x * x, axis=-1, keepdims=True) + EPS)


def rev(a):
    return jnp.flip(a, axis=1)


def to_chunks(a, size):
    b, s, h = a.shape[:3]
    a = a.reshape((b, s // size, size, h) + a.shape[3:])
    return jnp.moveaxis(a, (1, 3), (0, 2))


def from_chunks(a):
    a = jnp.moveaxis(a, (0, 2), (1, 3))
    return a.reshape((a.shape[0], a.shape[1] * a.shape[2]) + a.shape[3:])


def split_in(z):
    idx, acc = [], 0
    for size in IN_SIZES[:-1]:
        acc += size
        idx.append(acc)
    return jnp.split(z, idx, axis=-1)


def grid_pos_embed(n_tok, dtype):
    rows = n_tok // GRID_W
    t = jnp.arange(rows * GRID_W)
    quarter = D_MODEL // 4
    omega = 1.0 / (10000.0 ** (jnp.arange(quarter, dtype=jnp.float32) / quarter))

    def enc(pos):
        ang = pos.astype(jnp.float32)[:, None] * omega[None, :]
        return jnp.concatenate([jnp.sin(ang), jnp.cos(ang)], axis=-1)

    return jnp.concatenate([enc(t // GRID_W), enc(t % GRID_W)], axis=-1).astype(dtype)


def mlstm_scan(q, k, v, ig, lf, C0, n0, m0):
    L = ML_CHUNK
    tri = jnp.tril(jnp.ones((L, L), dtype=bool))

    def step(carry, inp):
        C, n, m = carry
        qi, ki, vi, ii, fi = inp
        bcum = jnp.cumsum(fi, axis=-1)
        log_d = jnp.where(tri, bcum[..., :, None] - bcum[..., None, :] + ii[..., None, :], -jnp.inf)
        log_0 = bcum + m[..., None]
        m_t = jnp.maximum(log_0, jnp.max(log_d, axis=-1))
        w_0 = jnp.exp(log_0 - m_t)
        s = jnp.einsum('bhtk,bhsk->bhts', qi, ki) * jnp.exp(log_d - m_t[..., None])
        num = jnp.einsum('bhts,bhsv->bhtv', s, vi) + w_0[..., None] * jnp.einsum('bhtk,bhkv->bhtv', qi, C)
        den = jnp.sum(s, axis=-1) + w_0 * jnp.einsum('bhtk,bhk->bht', qi, n)
        h = num / jnp.maximum(jnp.abs(den), jnp.exp(-m_t))[..., None]
        m_new = m_t[..., -1]
        w_s = jnp.exp(bcum[..., -1:] - bcum + ii - m_new[..., None])
        c_0 = jnp.exp(bcum[..., -1] + m - m_new)
        C = c_0[..., None, None] * C + jnp.einsum('bhs,bhsk,bhsv->bhkv', w_s, ki, vi)
        n = c_0[..., None] * n + jnp.einsum('bhs,bhsk->bhk', w_s, ki)
        return (C, n, m_new), h

    chunks = tuple(to_chunks(a, L) for a in (q, k, v, ig, lf))
    (C, n, m), h = lax.scan(step, (C0, n0, m0), chunks)
    return from_chunks(h), (C, n, m)


def mlstm_mixer(q, k, v, o, ig, fg, i_bias, f_bias, norm_w, C0, n0, m0):
    f32 = jnp.float32
    b, s, _ = q.shape
    q = q.astype(f32).reshape(b, s, ML_HEADS, ML_DK)
    k = k.astype(f32).reshape(b, s, ML_HEADS, ML_DK) * (ML_DK ** -0.5)
    v = v.astype(f32).reshape(b, s, ML_HEADS, ML_DV)
    ig = ig.astype(f32).reshape(b, s, N_DIR, ML_HEADS) + i_bias.astype(f32)
    lf = jax.nn.log_sigmoid(fg.astype(f32).reshape(b, s, N_DIR, ML_HEADS) + f_bias.astype(f32))
    C0, n0, m0 = C0.astype(f32), n0.astype(f32), m0.astype(f32)
    h_f, st_f = mlstm_scan(q, k, v, ig[:, :, 0], lf[:, :, 0], C0[:, 0], n0[:, 0], m0[:, 0])
    h_b, st_b = mlstm_scan(rev(q), rev(k), rev(v), rev(ig[:, :, 1]), rev(lf[:, :, 1]),
                           C0[:, 1], n0[:, 1], m0[:, 1])
    h = h_f + rev(h_b)
    h = h * lax.rsqrt(jnp.mean(h * h, axis=-1, keepdims=True) + EPS) * norm_w.astype(f32).reshape(ML_HEADS, ML_DV)
    y = jax.nn.sigmoid(o.astype(f32)) * h.reshape(b, s, ML_WIDTH)
    states = tuple(jnp.stack([a_f, a_b], axis=1) for a_f, a_b in zip(st_f, st_b))
    return y, states


def s5_scan(bu_re, bu_im, lam_re, lam_im, log_step, h0_re, h0_im):
    dt = jnp.exp(log_step)[:, None]
    mag = jnp.exp(lam_re * dt)
    ab_re = mag * jnp.cos(lam_im * dt)
    ab_im = mag * jnp.sin(lam_im * dt)
    den = lam_re * lam_re + lam_im * lam_im
    nr = ab_re - 1.0
    z_re = (nr * lam_re + ab_im * lam_im) / den
    z_im = (ab_im * lam_re - nr * lam_im) / den
    x_re = z_re * bu_re - z_im * bu_im
    x_im = z_re * bu_im + z_im * bu_re
    x_re = x_re.at[:, 0].add(ab_re * h0_re - ab_im * h0_im)
    x_im = x_im.at[:, 0].add(ab_re * h0_im + ab_im * h0_re)
    a_re = jnp.broadcast_to(ab_re, x_re.shape)
    a_im = jnp.broadcast_to(ab_im, x_im.shape)

    def combine(e1, e2):
        a1r, a1i, b1r, b1i = e1
        a2r, a2i, b2r, b2i = e2
        return (a1r * a2r - a1i * a2i, a1r * a2i + a1i * a2r,
                a2r * b1r - a2i * b1i + b2r, a2r * b1i + a2i * b1r + b2i)

    _, _, h_re, h_im = lax.associative_scan(combine, (a_re, a_im, x_re, x_im), axis=1)
    return h_re, h_im


def s5_mixer(u, lam_re, lam_im, log_step, B_re, B_im, C_re, C_im, D, glu_w, glu_b, h0_re, h0_im):
    f32 = jnp.float32
    b, s, _ = u.shape
    uf = u.astype(f32)
    ug = uf.reshape(b, s, S5_GROUPS, S5_GROUP)
    bu_re = jnp.einsum('bsgc,gpc->bsgp', ug, B_re.astype(f32))
    bu_im = jnp.einsum('bsgc,gpc->bsgp', ug, B_im.astype(f32))
    lam_re, lam_im, log_step = lam_re.astype(f32), lam_im.astype(f32), log_step.astype(f32)
    h0_re, h0_im = h0_re.astype(f32), h0_im.astype(f32)
    hr_f, hi_f = s5_scan(bu_re, bu_im, lam_re[0], lam_im[0], log_step[0], h0_re[:, 0], h0_im[:, 0])
    hr_b, hi_b = s5_scan(rev(bu_re), rev(bu_im), lam_re[1], lam_im[1], log_step[1], h0_re[:, 1], h0_im[:, 1])
    hr_b, hi_b = rev(hr_b), rev(hi_b)
    y = (jnp.einsum('bsgp,gcp->bsgc', hr_f + hr_b, C_re.astype(f32))
         - jnp.einsum('bsgp,gcp->bsgc', hi_f + hi_b, C_im.astype(f32)))
    y = y.reshape(b, s, S5_WIDTH) + D.astype(f32) * uf
    y = jax.nn.gelu(y)
    y = y * jax.nn.sigmoid(y @ glu_w.astype(f32) + glu_b.astype(f32))
    st_re = jnp.stack([hr_f[:, -1], hr_b[:, 0]], axis=1)
    st_im = jnp.stack([hi_f[:, -1], hi_b[:, 0]], axis=1)
    return y, (st_re, st_im)


def depthwise_conv(x, w):
    return lax.conv_general_dilated(x, w[:, None, :], window_strides=(1,),
                                    padding=[(CONV_K // 2, CONV_K // 2)],
                                    dimension_numbers=('NWC', 'WIO', 'NWC'),
                                    feature_group_count=x.shape[-1])


def gdn_scan(q, k, v, g, beta, S0):
    L = GD_CHUNK
    qc, kc, vc, gc, bc = (to_chunks(a, L) for a in (q, k, v, g, beta))
    gc = jnp.cumsum(gc, axis=-1)
    tri = jnp.tril(jnp.ones((L, L), dtype=bool))
    strict = jnp.tril(jnp.ones((L, L), dtype=bool), -1)
    decay = jnp.exp(jnp.where(tri, gc[..., :, None] - gc[..., None, :], -jnp.inf))
    kb = kc * bc[..., None]
    lmat = jnp.where(strict, jnp.einsum('nbhtk,nbhsk->nbhts', kb, kc) * decay, 0.0)
    amat = lmat + jnp.eye(L, dtype=lmat.dtype)
    solve = functools.partial(lax.linalg.triangular_solve, left_side=True, lower=True, unit_diagonal=True)
    u = solve(amat, vc * bc[..., None])
    w = solve(amat, kb * jnp.exp(gc)[..., None])
    attn = jnp.einsum('nbhtk,nbhsk->nbhts', qc, kc) * decay
    qg = qc * jnp.exp(gc)[..., None]
    kd = kc * jnp.exp(gc[..., -1:] - gc)[..., None]

    def step(state, inp):
        qg_i, kd_i, u_i, w_i, attn_i, glast = inp
        v_new = u_i - jnp.einsum('bhsk,bhkv->bhsv', w_i, state)
        o = jnp.einsum('bhtk,bhkv->bhtv', qg_i, state) + jnp.einsum('bhts,bhsv->bhtv', attn_i, v_new)
        state = state * jnp.exp(glast)[..., None, None] + jnp.einsum('bhsk,bhsv->bhkv', kd_i, v_new)
        return state, o

    S_fin, o = lax.scan(step, S0, (qg, kd, u, w, attn, gc[..., -1]))
    return from_chunks(o), S_fin


def gdn_mixer(qkv, z, a, bt, conv_w, A_log, dt_bias, norm_w, S0):
    f32 = jnp.float32
    b, s, _ = qkv.shape
    qkv = jax.nn.silu(depthwise_conv(qkv.astype(f32), conv_w.astype(f32)))
    q, k, v = jnp.split(qkv, 3, axis=-1)
    q = l2norm(q.reshape(b, s, GD_HEADS, GD_DK)) * (GD_DK ** -0.5)
    k = l2norm(k.reshape(b, s, GD_HEADS, GD_DK))
    v = v.reshape(b, s, GD_HEADS, GD_DV)
    g = -jnp.exp(A_log.astype(f32)) * jax.nn.softplus(a.astype(f32).reshape(b, s, N_DIR, GD_HEADS) + dt_bias.astype(f32))
    beta = jax.nn.sigmoid(bt.astype(f32).reshape(b, s, N_DIR, GD_HEADS))
    S0 = S0.astype(f32)
    o_f, S_f = gdn_scan(q, k, v, g[:, :, 0], beta[:, :, 0], S0[:, 0])
    o_b, S_b = gdn_scan(rev(q), rev(k), rev(v), rev(g[:, :, 1]), rev(beta[:, :, 1]), S0[:, 1])
    o = o_f + rev(o_b)
    o = o * lax.rsqrt(jnp.mean(o * o, axis=-1, keepdims=True) + EPS) * norm_w.astype(f32)
    y = o.reshape(b, s, GD_WIDTH) * jax.nn.silu(z.astype(f32))
    return y, jnp.stack([S_f, S_b], axis=1)


def trunk_layer(x, mod, p, st):
    b, s, _ = x.shape
    sh1, sc1, g1, sh2, sc2, g2 = jnp.split(mod[:, None, :].astype(x.dtype), N_MOD, axis=-1)
    h = rmsnorm(x, p['norm1_w']) * (1 + sc1) + sh1
    z = h @ p['w_in']
    mq, mk, mv, mo, mi, mf, su, gqkv, gz, ga, gb, gates = split_in(z)
    y_a, (mC, mn, mm) = mlstm_mixer(mq, mk, mv, mo, mi, mf, p['ml_i_bias'], p['ml_f_bias'], p['ml_norm_w'],
                                    st[0], st[1], st[2])
    y_b, (sre, sim) = s5_mixer(su, p['s5_lam_re'], p['s5_lam_im'], p['s5_log_step'], p['s5_B_re'], p['s5_B_im'],
                               p['s5_C_re'], p['s5_C_im'], p['s5_D'], p['s5_glu_w'], p['s5_glu_b'], st[3], st[4])
    y_c, gS = gdn_mixer(gqkv, gz, ga, gb, p['gd_conv_w'], p['gd_A_log'], p['gd_dt_bias'], p['gd_norm_w'], st[5])
    ys = jnp.stack([y_a, y_b, y_c], axis=2).astype(x.dtype)
    gates = jax.nn.sigmoid(gates.reshape(b, s, N_BRANCH, D_MODEL))
    merged = jnp.sum(gates * jnp.einsum('bsnc,ncd->bsnd', ys, p['w_branch']), axis=2)
    x = x + g1 * (merged @ p['w_out'])
    h2 = rmsnorm(x, p['norm2_w']) * (1 + sc2) + sh2
    ff = (jax.nn.silu(h2 @ p['w_gate']) * (h2 @ p['w_up'])) @ p['w_down']
    x = x + g2 * ff
    return x, (mC, mn, mm, sre, sim, gS)


def setup_inputs(seed: int = 0) -> dict:
    key = jax.random.key(seed)
    keys = iter(jax.random.split(key, 64))

    def nrm(shape, scale):
        return scale * jax.random.normal(next(keys), shape, jnp.float32)

    def unif(shape, lo, hi):
        return jax.random.uniform(next(keys), shape, jnp.float32, lo, hi)

    L = DEPTH
    dt = jnp.exp(unif((L, N_DIR, GD_HEADS), math.log(1e-3), math.log(1e-1)))
    return {
        'x_prompt': nrm((BATCH, SEQ, D_MODEL), 1.0),
        'x_sample': nrm((DEC_BATCH, DEC_SEQ, D_MODEL), 1.0),
        'state_mlstm_C': nrm((DEC_BATCH, L, N_DIR, ML_HEADS, ML_DK, ML_DV), 0.1),
        'state_mlstm_n': nrm((DEC_BATCH, L, N_DIR, ML_HEADS, ML_DK), 0.3),
        'state_mlstm_m': nrm((DEC_BATCH, L, N_DIR, ML_HEADS), 1.0),
        'state_s5_re': nrm((DEC_BATCH, L, N_DIR, S5_GROUPS, S5_STATE), 0.3),
        'state_s5_im': nrm((DEC_BATCH, L, N_DIR, S5_GROUPS, S5_STATE), 0.3),
        'state_gdn_S': nrm((DEC_BATCH, L, N_DIR, GD_HEADS, GD_DK, GD_DV), 0.3),
        'c': nrm((DEC_BATCH, D_MODEL), 1.0),
        'c_ctx': nrm((D_MODEL,), 1.0),
        'ada_w': nrm((L, D_MODEL, N_MOD * D_MODEL), 0.3 * D_MODEL ** -0.5),
        'ada_b': nrm((L, N_MOD * D_MODEL), 0.02),
        'norm1_w': 1.0 + nrm((L, D_MODEL), 0.01),
        'w_in': nrm((L, D_MODEL, D_IN), D_MODEL ** -0.5),
        'ml_i_bias': nrm((L, N_DIR, ML_HEADS), 0.1),
        'ml_f_bias': unif((L, N_DIR, ML_HEADS), 3.0, 6.0),
        'ml_norm_w': 1.0 + nrm((L, ML_WIDTH), 0.01),
        's5_lam_re': -0.5 + nrm((L, N_DIR, S5_GROUPS, S5_STATE), 0.01),
        's5_lam_im': math.pi * jnp.arange(S5_STATE, dtype=jnp.float32) + nrm((L, N_DIR, S5_GROUPS, S5_STATE), 0.01),
        's5_log_step': unif((L, N_DIR, S5_GROUPS), math.log(1e-3), math.log(1e-1)),
        's5_B_re': nrm((L, S5_GROUPS, S5_STATE, S5_GROUP), (2 * S5_GROUP) ** -0.5),
        's5_B_im': nrm((L, S5_GROUPS, S5_STATE, S5_GROUP), (2 * S5_GROUP) ** -0.5),
        's5_C_re': nrm((L, S5_GROUPS, S5_GROUP, S5_STATE), (2 * S5_STATE) ** -0.5),
        's5_C_im': nrm((L, S5_GROUPS, S5_GROUP, S5_STATE), (2 * S5_STATE) ** -0.5),
        's5_D': nrm((L, S5_WIDTH), 1.0),
        's5_glu_w': nrm((L, S5_WIDTH, S5_WIDTH), S5_WIDTH ** -0.5),
        's5_glu_b': nrm((L, S5_WIDTH), 0.02),
        'gd_conv_w': nrm((L, CONV_K, 3 * GD_WIDTH), CONV_K ** -0.5),
        'gd_A_log': jnp.log(unif((L, N_DIR, GD_HEADS), 1.0, 16.0)),
        'gd_dt_bias': dt + jnp.log(-jnp.expm1(-dt)),
        'gd_norm_w': 1.0 + nrm((L, GD_DV), 0.01),
        'w_branch': nrm((L, N_BRANCH, BRANCH_W, D_MODEL), BRANCH_W ** -0.5),
        'w_out': nrm((L, D_MODEL, D_MODEL), D_MODEL ** -0.5),
        'norm2_w': 1.0 + nrm((L, D_MODEL), 0.01),
        'w_gate': nrm((L, D_MODEL, D_FF), D_MODEL ** -0.5),
        'w_up': nrm((L, D_MODEL, D_FF), D_MODEL ** -0.5),
        'w_down': nrm((L, D_FF, D_MODEL), D_FF ** -0.5),
        'final_norm_w': 1.0 + nrm((D_MODEL,), 0.01),
    }


def reference(x_prompt, x_sample, state_mlstm_C, state_mlstm_n, state_mlstm_m, state_s5_re, state_s5_im,
              state_gdn_S, c, c_ctx, ada_w, ada_b, norm1_w, w_in, ml_i_bias, ml_f_bias, ml_norm_w,
              s5_lam_re, s5_lam_im, s5_log_step, s5_B_re, s5_B_im, s5_C_re, s5_C_im, s5_D, s5_glu_w, s5_glu_b,
              gd_conv_w, gd_A_log, gd_dt_bias, gd_norm_w, w_branch, w_out, norm2_w, w_gate, w_up, w_down,
              final_norm_w):
    def layer_params(l):
        return dict(norm1_w=norm1_w[l], w_in=w_in[l], ml_i_bias=ml_i_bias[l], ml_f_bias=ml_f_bias[l],
                    ml_norm_w=ml_norm_w[l], s5_lam_re=s5_lam_re[l], s5_lam_im=s5_lam_im[l],
                    s5_log_step=s5_log_step[l], s5_B_re=s5_B_re[l], s5_B_im=s5_B_im[l], s5_C_re=s5_C_re[l],
                    s5_C_im=s5_C_im[l], s5_D=s5_D[l], s5_glu_w=s5_glu_w[l], s5_glu_b=s5_glu_b[l],
                    gd_conv_w=gd_conv_w[l], gd_A_log=gd_A_log[l], gd_dt_bias=gd_dt_bias[l], gd_norm_w=gd_norm_w[l],
                    w_branch=w_branch[l], w_out=w_out[l], norm2_w=norm2_w[l], w_gate=w_gate[l], w_up=w_up[l],
                    w_down=w_down[l])

    f32 = jnp.float32
    bp = x_prompt.shape[0]
    zero_states = (jnp.zeros((bp, N_DIR, ML_HEADS, ML_DK, ML_DV), f32),
                   jnp.zeros((bp, N_DIR, ML_HEADS, ML_DK), f32),
                   jnp.zeros((bp, N_DIR, ML_HEADS), f32),
                   jnp.zeros((bp, N_DIR, S5_GROUPS, S5_STATE), f32),
                   jnp.zeros((bp, N_DIR, S5_GROUPS, S5_STATE), f32),
                   jnp.zeros((bp, N_DIR, GD_HEADS, GD_DK, GD_DV), f32))
    xp = x_prompt
    per_layer = []
    for l in range(DEPTH):
        mod_ctx = (jax.nn.silu(c_ctx) @ ada_w[l] + ada_b[l])[None]
        xp, st = trunk_layer(xp, mod_ctx, layer_params(l), zero_states)
        per_layer.append(st)
    y_prompt = rmsnorm(xp, final_norm_w)
    new_mlstm_C, new_mlstm_n, new_mlstm_m, new_s5_re, new_s5_im, new_gdn_S = [
        jnp.stack([st[i] for st in per_layer], axis=1) for i in range(6)]

    xs = x_sample + grid_pos_embed(x_sample.shape[1], x_sample.dtype)
    cache = (state_mlstm_C, state_mlstm_n, state_mlstm_m, state_s5_re, state_s5_im, state_gdn_S)
    for l in range(DEPTH):
        mod = jax.nn.silu(c) @ ada_w[l] + ada_b[l]
        xs, _ = trunk_layer(xs, mod, layer_params(l), tuple(a[:, l] for a in cache))
    y_sample = rmsnorm(xs, final_norm_w)
    return (y_prompt, y_sample, new_mlstm_C, new_mlstm_n, new_mlstm_m, new_s5_re, new_s5_im, new_gdn_S)
```

```python
import math
from contextlib import ExitStack
import numpy as np
import concourse.bass as bass
import concourse.mybir as mybir
from concourse.bass_utils import run_bass_kernel_spmd

F32 = mybir.dt.float32
BF16 = mybir.dt.bfloat16
F32R = mybir.dt.float32r


def f32r(ap):
    return ap
AF = mybir.ActivationFunctionType
ALU = mybir.AluOpType
AX = mybir.AxisListType

D = 1024
KT = 8
T = 4096
NSLOT = 16
SLOT = 256
NTT = 32
NBLK = 8
DEPTH = 2
D_IN = 7712
D_FF = 2816
FT = 22
EPS = 1e-6
C_MQ, C_MK, C_MV, C_MO, C_MI, C_MF, C_SU, C_GQKV, C_GZ, C_GA, C_GB, C_GATES = (
    0, 512, 1024, 1536, 2048, 2056, 2064, 2576, 4112, 4624, 4632, 4640)


class Dep:
    __slots__ = ("w", "r")

    def __init__(self):
        self.w = None
        self.r = set()


class _Rec:
    def __init__(self):
        self.call = None

    def __getattr__(self, name):
        def f(*a, **k):
            self.call = (name, a, k)
            return self
        return f


def _ap_free_elems(ap):
    try:
        sh = tuple(ap.shape)
        n = 1
        for v in sh[1:]:
            n *= int(v)
        return max(n, 1)
    except Exception:
        return 256


class Sched:
    KDMA = 32
    LAT = 2.0

    def __init__(self, nc, es):
        self.nc = nc
        self.engs = {"pe": nc.tensor, "act": nc.scalar, "dve": nc.vector, "pool": nc.gpsimd, "sp": nc.sync}
        self.sem = {k: es.enter_context(nc.semaphore("sem_" + k)) for k in self.engs}
        self.cnt = {k: 0 for k in self.engs}
        self.waited = {k: {} for k in self.engs}
        self.dsem = {q: [es.enter_context(nc.semaphore("dsem_%s%d" % (q, i))) for i in range(self.KDMA)]
                     for q in ("sp", "pool")}
        self.dn = {q: 0 for q in self.dsem}
        self.nwait = 0
        self.ops = []
        self.base = 0
        self.ev = {}
        self.free_t = {k: 0.0 for k in self.engs}

    def _deps(self, reads, writes):
        deps = set()
        for d in reads:
            if d.w is not None:
                deps.add(d.w)
        for d in writes:
            if d.w is not None:
                deps.add(d.w)
            deps.update(d.r)
        return deps

    def _record(self, oid, reads, writes):
        for d in writes:
            d.w = oid
            d.r = set()
        for d in reads:
            d.r.add(oid)

    def op(self, eng, fn, reads=(), writes=()):
        rec = _Rec()
        fn(rec)
        name, a, k = rec.call
        out = k.get("out", a[0] if a else None)
        n = _ap_free_elems(out) if out is not None else 256
        if eng == "pe":
            lhs = a[1] if len(a) > 1 else None
            slow = 4.0 if (lhs is not None and getattr(lhs, "dtype", None) == F32) else 1.0
            cost = 0.10 + slow * n / 1400.0
        elif eng == "act":
            cost = 0.25 + n / 1200.0
        elif eng == "dve":
            cost = 0.12 + n / (480.0 if name == "tensor_tensor_scan" else 960.0)
        else:
            cost = 0.3 + n / 480.0
        oid = self.base + len(self.ops)
        self.ops.append(dict(eng=eng, call=rec.call, deps=self._deps(reads, writes), cost=cost, lat=cost, dma=False))
        self._record(oid, reads, writes)

    def dma(self, q, out, in_, reads=(), writes=()):
        n = _ap_free_elems(out)
        oid = self.base + len(self.ops)
        self.ops.append(dict(eng=q, call=(out, in_), deps=self._deps(reads, writes), cost=0.08, lat=2.0 + n / 400.0, dma=True))
        self._record(oid, reads, writes)

    def _semh(self, key):
        if isinstance(key, tuple):
            return self.dsem[key[1]][key[2]]
        return self.sem[key]

    def _wait(self, eng, deps):
        e = self.engs[eng]
        best = {}
        for (k, v) in deps:
            if best.get(k, 0) < v:
                best[k] = v
        for k, v in best.items():
            if k == "pe" and eng == "pe":
                continue
            if self.waited[eng].get(k, 0) < v:
                e.wait_ge(self._semh(k), v)
                self.waited[eng][k] = v
                self.nwait += 1

    def flush(self):
        import heapq
        ops = self.ops
        n = len(ops)
        if n == 0:
            return
        base = self.base
        succ = [[] for _ in range(n)]
        indeg = [0] * n
        rt = [0.0] * n
        for i, o in enumerate(ops):
            for d in o["deps"]:
                if d >= base:
                    succ[d - base].append(i)
                    indeg[i] += 1
        heap = [(0.0, i) for i in range(n) if indeg[i] == 0]
        heapq.heapify(heap)
        free_t = {k: 0.0 for k in self.engs}
        emitted = 0
        while heap:
            t, i = heapq.heappop(heap)
            o = ops[i]
            eng = o["eng"]
            start = max(t, free_t[eng])
            free_t[eng] = start + o["cost"]
            fin = start + o["lat"]
            self._emit(base + i, o)
            emitted += 1
            for s_ in succ[i]:
                lat = 0.05 if (ops[s_]["eng"] == eng and not o["dma"]) else self.LAT
                if rt[s_] < fin + lat:
                    rt[s_] = fin + lat
                indeg[s_] -= 1
                if indeg[s_] == 0:
                    heapq.heappush(heap, (rt[s_], s_))
        assert emitted == n, "dependency cycle in scheduler"
        self.base += n
        self.ops = []

    def _emit(self, oid, o):
        eng = o["eng"]
        deps = [self.ev[d] for d in o["deps"]]
        if o["dma"]:
            idx = self.dn[eng]
            slot = idx % self.KDMA
            key = ("dma", eng, slot)
            if idx >= self.KDMA:
                deps.append((key, 16 * (idx // self.KDMA)))
            self._wait(eng, deps)
            out, in_ = o["call"]
            inst = self.engs[eng].dma_start(out=out, in_=in_)
            inst.then_inc(self.dsem[eng][slot], 16)
            self.dn[eng] += 1
            self.ev[oid] = (key, 16 * (idx // self.KDMA + 1))
        else:
            self._wait(eng, deps)
            name, a, k = o["call"]
            inst = getattr(self.engs[eng], name)(*a, **k)
            self.cnt[eng] += 1
            inst.then_inc(self.sem[eng], 1)
            self.ev[oid] = (eng, self.cnt[eng])

    def _all_done_deps(self):
        deps = [(k, v) for k, v in self.cnt.items() if v > 0]
        for q in self.dsem:
            for slot in range(self.KDMA):
                n = (self.dn[q] - slot + self.KDMA - 1) // self.KDMA if self.dn[q] > slot else 0
                if n > 0:
                    deps.append((("dma", q, slot), 16 * n))
        return deps

    def barrier(self):
        self.flush()
        deps = self._all_done_deps()
        for eng in self.engs:
            self._wait(eng, [d for d in deps if d[0] != eng])

    def finish(self, deps_list):
        self.flush()
        self._wait("sp", self._all_done_deps())


class Buf:
    def __init__(self, t, n=1):
        self.t = t
        self.d = [Dep() for _ in range(n)]

    def __getitem__(self, k):
        return self.t[k]


class Ring:
    def __init__(self, bufs):
        self.bufs = bufs
        self.i = 0

    def next(self):
        b = self.bufs[self.i % len(self.bufs)]
        self.i += 1
        return b


def build_program(dbg=None, nlayers=DEPTH):
    nc = bass.Bass("TRN2", target_bir_lowering=False)
    es = ExitStack()
    S = Sched(nc, es)

    def din(name, shape, dt=F32):
        return nc.dram_tensor(name, list(shape), dt, kind="ExternalInput").ap()

    def dout(name, shape, dt=F32):
        return nc.dram_tensor(name, list(shape), dt, kind="ExternalOutput").ap()

    def dscr(name, shape, dt=F32):
        kind = "ExternalOutput" if (dbg and name in dbg) else "Internal"
        return nc.dram_tensor(name, list(shape), dt, kind=kind).ap()

    def sb(name, shape, dt=F32, n=1, stack=None):
        t = (stack or es).enter_context(nc.sbuf_tensor("s_" + name, list(shape), dt))
        return Buf(t, n)

    x_in = din("x", [T, D])
    pos_in = din("posT", [128, KT, T])
    cvec = din("cvec", [128, KT])
    keep_in = din("keep", [128, 1])
    ada_w = din("ada_w", [DEPTH, D, 6 * D])
    ada_bT = din("ada_bT", [128, DEPTH, 48])
    n1wT = din("n1wT", [128, DEPTH, KT])
    n2wT = din("n2wT", [128, DEPTH, KT])
    fnwT = din("fnwT", [128, KT])
    w_in = din("w_in", [DEPTH, D, D_IN])
    gbias = din("gbias", [128, DEPTH, 32])
    convw = din("convw", [128, DEPTH, 12, 5])
    w_branch = din("w_branch", [DEPTH, 3, 512, D])
    w_out = din("w_out", [DEPTH, D, D])
    glu_w = din("glu_w", [DEPTH, 512, 512])
    s5DbT = din("s5DbT", [128, DEPTH, 2, 4])
    mlnw = din("mlnw", [128, DEPTH, 512])
    gdnw = din("gdnw", [128, DEPTH, 128])
    w_gate = din("w_gate", [DEPTH, D, D_FF])
    w_up = din("w_up", [DEPTH, D, D_FF])
    w_down = din("w_down", [DEPTH, D_FF, D])
    consts = din("consts", [128, 8, 128])
    y_out = dout("y", [T, D])

    xT_scr = dscr("xT_scr", [128, KT, T])
    ml_qT = dscr("ml_qT", [128, 4, T])
    ml_kT = dscr("ml_kT", [128, 4, T])
    ml_k = dscr("ml_k", [T, 512])
    ml_v = dscr("ml_v", [T, 4, 129])
    s5_uT = dscr("s5_uT", [128, 4, T])
    gd_qT = dscr("gd_qT", [128, 4, T])
    gd_kT = dscr("gd_kT", [128, 4, T])
    gd_q = dscr("gd_q", [T, 512])
    gd_k = dscr("gd_k", [T, 512])
    gd_v = dscr("gd_v", [T, 512])
    h_ml = dscr("h_ml", [2, T, 512])
    o_gd = dscr("o_gd", [2, T, 512])
    y_s5T = dscr("y_s5T", [2, 128, 4, T])
    mlC0 = din("mlC0", [128, DEPTH, 2, 4, 129])
    m0rep = din("m0rep", [128, DEPTH, 2, 4])
    m0h = din("m0h", [DEPTH, 2, 4, 1])
    s5rep = din("s5rep", [128, DEPTH, 2, 3, 2048])
    s5sm = din("s5sm", [128, DEPTH, 2, 3, 16])
    s5Bt = din("s5Bt", [128, DEPTH, 2, 16, 128])
    s5Ct = din("s5Ct", [128, DEPTH, 2, 16, 128])
    s5h0 = din("s5h0", [128, DEPTH, 2, 2, 16])
    st_s5 = dout("st_s5", [DEPTH, NSLOT, 2, 128, 2, 16])
    gdS0 = din("gdS0", [128, DEPTH, 2, 4, 128])
    st_gdS = dout("st_gdS", [DEPTH, NSLOT, 2, 128, 4, 128])
    st_mlC = dout("st_mlC", [DEPTH, NSLOT, 2, 128, 4, 129])
    st_mlm = dout("st_mlm", [DEPTH, NSLOT, 2, 4, 1])
    gates_dbg = dscr("gates_dbg", [128, NTT, 32]) if (dbg and "gates_dbg" in dbg) else None
    hT_dbg = dscr("hT_dbg", [128, KT, T], BF16) if (dbg and "hT_dbg" in dbg) else None

    cst = sb("cst", [128, 8, 128])
    ident = cst[:, 0, :]
    ones = cst[:, 1, :]
    keep = sb("keep", [128, 1])
    modT = sb("modT", [128, DEPTH, 48])
    cv = sb("cv", [128, KT])
    n1w = sb("n1w", [128, DEPTH, KT])
    n2w = sb("n2w", [128, DEPTH, KT])
    fnw = sb("fnw", [128, KT])
    gb = sb("gb", [128, DEPTH, 32])
    cw = sb("cw", [128, DEPTH, 12, 5])
    w1 = sb("w1", [128, DEPTH, 2, KT])
    negA = sb("negA", [128, DEPTH, 8])
    epsb = sb("epsb", [128, 1])
    onesb = sb("onesb", [128, 128], BF16)
    hTh = [None]
    gates = sb("gates", [128, NTT, 32], n=1)

    ps = [Buf(es.enter_context(nc.psum_tensor("ps%d" % i, [128, 512], F32))) for i in range(8)]
    psr = Ring(ps)

    cdep = cst.d[0]
    S.dma("sp", cst[:], consts, writes=[cdep])
    for (b_, src) in ((keep, keep_in), (cv, cvec), (n1w, n1wT), (n2w, n2wT), (fnw, fnwT), (gb, gbias),
                      (cw, convw)):
        S.dma("sp", b_[:], src, writes=[b_.d[0]])

    S.op("dve", lambda e: e.memset(epsb[:], EPS), writes=[epsb.d[0]])
    S.op("dve", lambda e: e.memset(onesb[:], 1.0), writes=[onesb.d[0]])
    S.op("act", lambda e: e.activation(negA[:], gb[:, :, 24:32], AF.Exp), reads=[gb.d[0]], writes=[negA.d[0]])
    S.op("dve", lambda e: e.tensor_scalar(negA[:], negA[:], -1.0, None, ALU.mult), reads=[negA.d[0]], writes=[negA.d[0]])
    with ExitStack() as st:
        sc = sb("silu_c", [128, KT], stack=st)
        S.op("act", lambda e: e.activation(sc[:], cv[:], AF.Silu), reads=[cv.d[0]], writes=[sc.d[0]])
        abT = sb("abT", [128, DEPTH, 48], stack=st)
        S.dma("sp", abT[:], ada_bT, writes=[abT.d[0]])
        awr = Ring([sb("aw%d" % i, [128, KT, 512], stack=st) for i in range(2)])
        for l in range(DEPTH):
            pm = psr.next()
            for cg in range(12):
                aw = awr.next()
                S.dma("sp", aw[:], ada_w[l].rearrange("(kt p) c -> p kt c", p=128)[:, :, cg * 512:(cg + 1) * 512],
                      writes=[aw.d[0]])
                for c4 in range(4):
                    j = cg * 4 + c4
                    for kt in range(KT):
                        S.op("pe", lambda e, aw=aw, c4=c4, kt=kt, j=j, pm=pm: e.matmul(
                            pm[:, j:j + 1], aw[:, kt, c4 * 128:(c4 + 1) * 128], sc[:, kt:kt + 1],
                            start=(kt == 0), stop=(kt == KT - 1)),
                            reads=[aw.d[0], sc.d[0]], writes=[pm.d[0]])
            S.op("dve", lambda e, l=l, pm=pm: e.tensor_tensor(modT[:, l, :], pm[:, 0:48], abT[:, l, :], ALU.add),
                 reads=[pm.d[0], abT.d[0]], writes=[modT.d[0]])
            for which, nw, m in ((0, n1w, 1), (1, n2w, 4)):
                S.op("dve", lambda e, l=l, which=which, nw=nw, m=m: e.scalar_tensor_tensor(
                    w1[:, l, which, :], modT[:, l, m * 8:(m + 1) * 8], 1.0, nw[:, l, :], ALU.add, ALU.mult),
                    reads=[modT.d[0], nw.d[0]], writes=[w1.d[0]])

    S.barrier()

    def evac(i, out, in_, reads, writes, scale=None):
        if i % 2 == 0:
            if scale is None:
                S.op("act", lambda e: e.copy(out, in_), reads=reads, writes=writes)
            else:
                S.op("act", lambda e: e.mul(out, in_, scale), reads=reads, writes=writes)
        else:
            if scale is None:
                S.op("dve", lambda e: e.tensor_copy(out, in_), reads=reads, writes=writes)
            else:
                S.op("dve", lambda e: e.tensor_scalar(out, in_, scale, None, ALU.mult), reads=reads, writes=writes)

    def rmsnorm_block(st, xb, l, which, blk, tmp, out=None):
        hT = hTh[0]
        sh_m = 0 if which == 0 else 3
        sq = tmp
        sqb = tmp[:].bitcast(BF16)[:, :, 0:512]
        S.op("act", lambda e: e.activation(sqb, xb[:], AF.Square), reads=[xb.d[0]], writes=[sq.d[0]])
        pq = psr.next()
        for kt in range(KT):
            S.op("pe", lambda e, kt=kt: e.matmul(pq[:, :], onesb[:], sqb[:, kt, :], start=(kt == 0), stop=(kt == KT - 1)),
                 reads=[sq.d[0], onesb.d[0]], writes=[pq.d[0]])
        rs = rs_ring.next()
        S.op("dve", lambda e: e.tensor_scalar(rs[:], pq[:, :], 1.0 / D, EPS, ALU.mult, ALU.add),
             reads=[pq.d[0]], writes=[rs.d[0]])
        S.op("act", lambda e: e.activation(rs[:], rs[:], AF.Sqrt), reads=[rs.d[0]], writes=[rs.d[0]])
        S.op("dve", lambda e: e.reciprocal(rs[:], rs[:]), reads=[rs.d[0]], writes=[rs.d[0]])
        S.op("dve", lambda e: e.tensor_tensor(sq[:], xb[:], rs[:].unsqueeze(1).to_broadcast([128, KT, 512]), ALU.mult),
             reads=[xb.d[0], rs.d[0]], writes=[sq.d[0]])
        for kt in range(KT):
            if which == 2:
                S.op("act", lambda e, kt=kt: e.mul(out[:, kt, :], sq[:, kt, :], fnw[:, kt:kt + 1]),
                     reads=[sq.d[0], fnw.d[0]], writes=[out.d[0]])
                continue
            dst = hT[:, kt, blk * 512:(blk + 1) * 512] if out is None else out[:, kt, :]
            S.op("act", lambda e, kt=kt, dst=dst: e.activation(
                dst, sq[:, kt, :], AF.Identity,
                bias=modT[:, l, sh_m * 8 + kt:sh_m * 8 + kt + 1], scale=w1[:, l, which, kt:kt + 1]),
                reads=[sq.d[0], modT.d[0], w1.d[0]], writes=[hT.d[blk] if out is None else out.d[0]])

    rs_ring = Ring([sb("rs%d" % i, [128, 512]) for i in range(2)])


    def drive_window(items, width):
        active = []
        items = iter(items)
        done = False
        while True:
            while not done and len(active) < width:
                try:
                    active.append(next(items))
                except StopIteration:
                    done = True
            if not active:
                break
            for g in list(active):
                try:
                    next(g)
                except StopIteration:
                    active.remove(g)

    def drive(gens):
        gens = list(gens)
        while gens:
            for g in list(gens):
                try:
                    next(g)
                except StopIteration:
                    gens.remove(g)

    def phase_mlstm(l):
        psr = Ring(ps[4:8])
        with ExitStack() as st:
            def R(name, shape, n, dt=F32):
                return Ring([sb("%s%d_%d" % (name, l, i), shape, dt, stack=st) for i in range(n)])
            Cst = [sb("mC%d_%d" % (l, d), [128, 4, 129], stack=st) for d in range(2)]
            mst = [sb("mm%d_%d" % (l, d), [4, 1], stack=st) for d in range(2)]
            em0 = sb("em0_%d" % l, [128, 2, 4], stack=st)
            S.dma("sp", em0[:], m0rep[:, l, :, :], writes=[em0.d[0]])
            S.op("act", lambda e: e.activation(em0[:], em0[:], AF.Exp), reads=[em0.d[0]], writes=[em0.d[0]])
            for d in range(2):
                S.dma("sp", Cst[d][:], mlC0[:, l, d, :, :], writes=[Cst[d].d[0]])
                S.op("dve", lambda e, d=d: e.tensor_tensor(
                    Cst[d][:], Cst[d][:], em0[:, d, :].unsqueeze(2).to_broadcast([128, 4, 129]), ALU.mult),
                    reads=[Cst[d].d[0], em0.d[0]], writes=[Cst[d].d[0]])
                S.dma("sp", mst[d][:], m0h[l, d], writes=[mst[d].d[0]])
            qTr, kTr, ktr = R("mqT", [128, 4, 128], 2), R("mkT", [128, 4, 128], 2), R("mkt", [128, 4, 128], 2)
            var = R("mva", [128, 4, 129], 2)
            smr, dgr, Er = R("msm", [128, 28], 3), R("mdg", [128, 4, 128], 2), R("mE", [128, 4, 128], 2)
            STr, tmr, nmr, dnr, kwr = (R("mST", [128, 128], 2), R("mtm", [128, 129], 2), R("mnm", [128, 129], 2),
                                       R("mdn", [128, 1], 3), R("mkw", [128, 128], 2))
            hor, smallr, cor = R("mho", [128, 4, 128], 2), R("msml", [128, 8], 4), R("mco", [128, 4, 129], 1)
            for it in range(NTT):
                def body(d, it=it):
                    c = it if d == 0 else NTT - 1 - it
                    first = (c % 2 == 0) if d == 0 else (c % 2 == 1)
                    slot = c // 2
                    t0 = c * 128
                    Cd, md = Cst[d], mst[d]
                    if first and it > 0:
                        S.op("dve", lambda e, Cd=Cd: e.tensor_scalar(Cd[:], Cd[:], keep[:, 0:1], None, ALU.mult),
                             reads=[Cd.d[0], keep.d[0]], writes=[Cd.d[0]])
                        S.op("dve", lambda e, md=md: e.tensor_scalar(md[:], md[:], keep[0:4, 0:1], None, ALU.mult),
                             reads=[md.d[0], keep.d[0]], writes=[md.d[0]])
                    qT, kT, kt_, va = qTr.next(), kTr.next(), ktr.next(), var.next()
                    S.dma("sp", qT[:], ml_qT[:, :, t0:t0 + 128], writes=[qT.d[0]])
                    S.dma("sp", kT[:], ml_kT[:, :, t0:t0 + 128], writes=[kT.d[0]])
                    S.dma("sp", kt_[:], ml_k[t0:t0 + 128, :].rearrange("p (h d) -> p h d", h=4), writes=[kt_.d[0]])
                    S.dma("sp", va[:], ml_v[t0:t0 + 128, :, :], writes=[va.d[0]])
                    yield
                    gi = gates[:, c, d * 4:(d + 1) * 4]
                    gl = gates[:, c, 8 + d * 4:8 + (d + 1) * 4]
                    gdp = gates.d[0]
                    Md = cst[:, 2 + d, :]
                    pcs = psr.next()
                    S.op("pe", lambda e: e.matmul(pcs[:, 0:4], Md, gl, start=True, stop=True), reads=[cdep, gdp], writes=[pcs.d[0]])
                    S.op("pe", lambda e: e.matmul(pcs[:, 4:8], ones, gl, start=True, stop=True), reads=[cdep, gdp], writes=[pcs.d[0]])
                    yield
                    sm = smr.next()
                    sd = sm.d[0]
                    S.op("dve", lambda e: e.tensor_copy(sm[:, 0:8], pcs[:, 0:8]), reads=[pcs.d[0]], writes=[sd])
                    S.op("act", lambda e: e.activation(sm[:, 8:12], sm[:, 0:4], AF.Exp), reads=[sd], writes=[sd])
                    S.op("act", lambda e: e.activation(sm[:, 12:16], gi, AF.Exp), reads=[gdp], writes=[sd])
                    S.op("dve", lambda e: e.tensor_tensor(sm[:, 16:20], sm[:, 4:8], sm[:, 0:4], ALU.subtract), reads=[sd], writes=[sd])
                    S.op("dve", lambda e: e.tensor_tensor(sm[:, 16:20], sm[:, 16:20], gi, ALU.add), reads=[sd, gdp], writes=[sd])
                    S.op("act", lambda e: e.activation(sm[:, 20:24], sm[:, 16:20], AF.Exp), reads=[sd], writes=[sd])
                    S.op("act", lambda e: e.activation(sm[:, 24:28], sm[:, 4:8], AF.Exp), reads=[sd], writes=[sd])
                    yield
                    dg = dgr.next()
                    for h in range(4):
                        S.op("act", lambda e, h=h: e.mul(dg[:, h, :], ident, sm[:, h:h + 1]), reads=[cdep, sd], writes=[dg.d[0]])
                    pbc = psr.next()
                    S.op("pe", lambda e: e.matmul(pbc[:, :], ones, dg[:].rearrange("p h t -> p (h t)"), start=True, stop=True),
                         reads=[cdep, dg.d[0]], writes=[pbc.d[0]])
                    yield
                    E = Er.next()
                    for h in range(4):
                        S.op("act", lambda e, h=h: e.activation(E[:, h, :], pbc[:, h * 128:(h + 1) * 128], AF.Abs,
                                                                bias=sm[:, h:h + 1], scale=-1.0),
                             reads=[pbc.d[0], sd], writes=[E.d[0]])
                    S.op("act", lambda e: e.activation(E[:], E[:], AF.Exp, scale=-1.0), reads=[E.d[0]], writes=[E.d[0]])
                    S.op("pool", lambda e: e.tensor_tensor(E[:], E[:], cst[:, 2 + d:3 + d, :].to_broadcast([128, 4, 128]), ALU.mult),
                         reads=[E.d[0], cdep], writes=[E.d[0]])
                    yield
                    ho = hor.next()
                    for h in range(4):
                        yield
                        pst = psr.next()
                        S.op("pe", lambda e: e.matmul(pst[:, 0:128], f32r(kT[:, h, :]), f32r(qT[:, h, :]), start=True, stop=True),
                             reads=[kT.d[0], qT.d[0]], writes=[pst.d[0]])
                        yield
                        ST = STr.next()
                        S.op("dve", lambda e: e.scalar_tensor_tensor(ST[:], pst[:, 0:128], sm[:, 12 + h:13 + h], E[:, h, :],
                                                                     ALU.mult, ALU.mult),
                             reads=[pst.d[0], sd, E.d[0]], writes=[ST.d[0]])
                        pn = psr.next()
                        S.op("pe", lambda e: e.matmul(pn[:, 0:129], f32r(ST[:]), f32r(va[:, h, :]), start=True, stop=True),
                             reads=[ST.d[0], va.d[0]], writes=[pn.d[0]])
                        S.op("pe", lambda e: e.matmul(pn[:, 256:385], f32r(qT[:, h, :]), f32r(Cd[:, h, :]), start=True, stop=True),
                             reads=[qT.d[0], Cd.d[0]], writes=[pn.d[0]])
                        yield
                        tm, nm, dn = tmr.next(), nmr.next(), dnr.next()
                        S.op("act", lambda e: e.mul(tm[:], pn[:, 256:385], sm[:, 8 + h:9 + h]), reads=[pn.d[0], sd], writes=[tm.d[0]])
                        S.op("dve", lambda e: e.tensor_tensor(nm[:], tm[:], pn[:, 0:129], ALU.add),
                             reads=[tm.d[0], pn.d[0]], writes=[nm.d[0]])
                        S.op("act", lambda e: e.activation(dn[:], nm[:, 128:129], AF.Abs), reads=[nm.d[0]], writes=[dn.d[0]])
                        S.op("dve", lambda e: e.tensor_scalar(dn[:], dn[:], 1.0, None, ALU.max), reads=[dn.d[0]], writes=[dn.d[0]])
                        S.op("dve", lambda e: e.reciprocal(dn[:], dn[:]), reads=[dn.d[0]], writes=[dn.d[0]])
                        S.op("act", lambda e: e.mul(ho[:, h, :], nm[:, 0:128], dn[:, 0:1]), reads=[nm.d[0], dn.d[0]], writes=[ho.d[0]])
                        yield
                        kw = kwr.next()
                        S.op("act", lambda e: e.mul(kw[:], kt_[:, h, :], sm[:, 20 + h:21 + h]),
                             reads=[kt_.d[0], sd], writes=[kw.d[0]])
                        pu = psr.next()
                        S.op("pe", lambda e: e.matmul(pu[:, 0:129], f32r(kw[:]), f32r(va[:, h, :]), start=True, stop=True),
                             reads=[kw.d[0], va.d[0]], writes=[pu.d[0]])
                        S.op("dve", lambda e: e.scalar_tensor_tensor(Cd[:, h, :], Cd[:, h, :], sm[:, 24 + h:25 + h], pu[:, 0:129],
                                                                     ALU.mult, ALU.add),
                             reads=[Cd.d[0], sd, pu.d[0]], writes=[Cd.d[0]])
                    S.dma("pool", h_ml[d, t0:t0 + 128, :].rearrange("p (h d) -> p h d", h=4), ho[:], reads=[ho.d[0]])
                    yield
                    ptm = psr.next()
                    S.op("pe", lambda e: e.transpose(ptm[0:4, 0:128], sm[:, 16:20], ident), reads=[sd, cdep], writes=[ptm.d[0]])
                    S.op("pe", lambda e: e.transpose(ptm[0:4, 128:256], sm[:, 4:8], ident), reads=[sd, cdep], writes=[ptm.d[0]])
                    yield
                    sl = smallr.next()
                    S.op("dve", lambda e: e.tensor_reduce(sl[0:4, 0:1], ptm[0:4, 0:128], AX.X, ALU.max), reads=[ptm.d[0]], writes=[sl.d[0]])
                    S.op("dve", lambda e: e.tensor_tensor(sl[0:4, 1:2], ptm[0:4, 128:129], md[:], ALU.add),
                         reads=[ptm.d[0], md.d[0]], writes=[sl.d[0]])
                    S.op("dve", lambda e: e.tensor_tensor(md[:], sl[0:4, 0:1], sl[0:4, 1:2], ALU.max), reads=[sl.d[0]], writes=[md.d[0]])
                    if not first:
                        S.op("act", lambda e: e.activation(sl[0:4, 2:3], md[:], AF.Exp, scale=-1.0), reads=[md.d[0]], writes=[sl.d[0]])
                        S.op("dve", lambda e: e.tensor_scalar(sl[0:4, 4:8], cst[0:4, 0, 0:4], sl[0:4, 2:3], None, ALU.mult),
                             reads=[sl.d[0], cdep], writes=[sl.d[0]])
                        pb = psr.next()
                        S.op("pe", lambda e: e.matmul(pb[:, 0:4], cst[0:4, 1, :], sl[0:4, 4:8], start=True, stop=True),
                             reads=[sl.d[0], cdep], writes=[pb.d[0]])
                        sl2 = smallr.next()
                        S.op("dve", lambda e: e.tensor_copy(sl2[:, 0:4], pb[:, 0:4]), reads=[pb.d[0]], writes=[sl2.d[0]])
                        co = cor.next()
                        S.op("dve", lambda e: e.tensor_tensor(co[:], Cd[:], sl2[:, 0:4].unsqueeze(2).to_broadcast([128, 4, 129]), ALU.mult),
                             reads=[Cd.d[0], sl2.d[0]], writes=[co.d[0]])
                        S.dma("pool", st_mlC[l, slot, d], co[:], reads=[co.d[0]])
                        S.dma("pool", st_mlm[l, slot, d], md[:], reads=[md.d[0]])
                yield [body(0), body(1)]

    def phase_gdn(l):
        with ExitStack() as st:
            cnt = [0]

            def T4(name, n=2):
                cnt[0] += 1
                return Ring([sb("g%s%d_%d_%d" % (name, l, cnt[0], i), [128, 4, 128], stack=st) for i in range(n)])
            Sst = [sb("gS%d_%d" % (l, d), [128, 4, 128], stack=st) for d in range(2)]
            for d in range(2):
                s0 = sb("gS0_%d_%d" % (l, d), [128, 4, 128], stack=st)
                S.dma("sp", s0[:], gdS0[:, l, d, :, :], writes=[s0.d[0]])
                S.op("dve", lambda e, d=d, s0=s0: e.tensor_copy(Sst[d][:].bitcast(F32R), s0[:]), reads=[s0.d[0]], writes=[Sst[d].d[0]])
            names = ["qTr", "kTr", "qT", "kT", "qt", "kt", "vt", "dg", "E", "EUi", "EUs", "ELs", "kb", "kbg", "vb", "qg", "kd", "kbT", "qgT",
                     "NT", "N", "aT", "AT", "Em", "wTn", "vn", "oo", "tmpS"]
            rg = {n: T4(n) for n in names}
            for k in (1, 2, 4, 8, 16):
                rg["X%d" % k] = T4("X%d" % k)
                rg["Y%d" % k] = T4("Y%d" % k)
            rg["P"] = T4("P", 4)
            rg["Q"] = T4("Q", 4)
            smr = Ring([sb("gsm%d_%d" % (l, i), [128, 24], stack=st) for i in range(3)])
            identb = cst[:, 0:1, :].to_broadcast([128, 4, 128])

            def flat(b):
                return b[:].rearrange("p h t -> p (h t)")

            def v4(pb):
                return pb[:, :].rearrange("p (h t) -> p h t", h=4)

            def mm4(lhs, rhs, rl, rr, red=False):
                pq = psr.next()
                for h in range(4):
                    a_, b_ = lhs[:, h, :], rhs[:, h, :]
                    if red:
                        a_, b_ = a_.bitcast(F32R), b_.bitcast(F32R)
                    S.op("pe", lambda e, h=h, a_=a_, b_=b_: e.matmul(pq[:, h * 128:(h + 1) * 128], a_, b_,
                                                                     start=True, stop=True), reads=[rl.d[0], rr.d[0]], writes=[pq.d[0]])
                return pq

            def R_(b):
                return b[:].bitcast(F32R)

            ev = [0]

            def copy4(dst, pq, scale=None, red=False):
                ev[0] += 1
                o_ = flat(dst).bitcast(F32R) if red else flat(dst)
                evac(ev[0], o_, pq[:, :], [pq.d[0]], [dst.d[0]], scale)

            for it in range(NTT):
                def body(d, it=it):
                    c = it if d == 0 else NTT - 1 - it
                    first = (c % 2 == 0) if d == 0 else (c % 2 == 1)
                    slot = c // 2
                    t0 = c * 128
                    Sd = Sst[d]
                    if first and it > 0:
                        S.op("dve", lambda e: e.tensor_scalar(R_(Sd), Sd[:], keep[:, 0:1], None, ALU.mult),
                             reads=[Sd.d[0], keep.d[0]], writes=[Sd.d[0]])
                    B = {n: r.next() for n, r in rg.items() if n not in ("P", "Q")}
                    S.dma("sp", B["qT"][:], gd_qT[:, :, t0:t0 + 128], writes=[B["qT"].d[0]])
                    S.dma("sp", B["kT"][:], gd_kT[:, :, t0:t0 + 128], writes=[B["kT"].d[0]])
                    for nm_, src in (("qt", gd_q), ("kt", gd_k), ("vt", gd_v)):
                        S.dma("sp", B[nm_][:], src[t0:t0 + 128, :].rearrange("p (h d) -> p h d", h=4), writes=[B[nm_].d[0]])
                    yield
                    for nm_ in ("qT", "kT"):
                        ev[0] += 1
                        evac(ev[0], R_(B[nm_ + "r"]), B[nm_][:], [B[nm_].d[0]], [B[nm_ + "r"].d[0]])
                    gg = gates[:, c, 16 + d * 4:20 + d * 4]
                    gbeta = gates[:, c, 24 + d * 4:28 + d * 4]
                    gdp = gates.d[0]
                    Md = cst[:, 2 + d, :]
                    pcs = psr.next()
                    S.op("pe", lambda e: e.matmul(pcs[:, 0:4], Md, gg, start=True, stop=True), reads=[cdep, gdp], writes=[pcs.d[0]])
                    S.op("pe", lambda e: e.matmul(pcs[:, 4:8], ones, gg, start=True, stop=True), reads=[cdep, gdp], writes=[pcs.d[0]])
                    yield
                    sm = smr.next()
                    sd = sm.d[0]
                    S.op("dve", lambda e: e.tensor_copy(sm[:, 0:8], pcs[:, 0:8]), reads=[pcs.d[0]], writes=[sd])
                    S.op("act", lambda e: e.activation(sm[:, 8:16], sm[:, 0:8], AF.Exp), reads=[sd], writes=[sd])
                    S.op("dve", lambda e: e.tensor_tensor(sm[:, 16:20], sm[:, 4:8], sm[:, 0:4], ALU.subtract), reads=[sd], writes=[sd])
                    S.op("act", lambda e: e.activation(sm[:, 16:20], sm[:, 16:20], AF.Exp), reads=[sd], writes=[sd])
                    S.op("dve", lambda e: e.tensor_tensor(sm[:, 20:24], sm[:, 8:12], gbeta, ALU.mult), reads=[sd, gdp], writes=[sd])

                    def bc(ap4):
                        return ap4.unsqueeze(2).to_broadcast([128, 4, 128])
                    yield
                    dg, E = B["dg"], B["E"]
                    for h in range(4):
                        S.op("act", lambda e, h=h: e.mul(dg[:, h, :], ident, sm[:, h:h + 1]), reads=[cdep, sd], writes=[dg.d[0]])
                    yield
                    pbc = psr.next()
                    S.op("pe", lambda e: e.matmul(pbc[:, :], ones, flat(dg), start=True, stop=True), reads=[cdep, dg.d[0]], writes=[pbc.d[0]])
                    yield
                    for h in range(4):
                        S.op("act", lambda e, h=h: e.activation(E[:, h, :], pbc[:, h * 128:(h + 1) * 128], AF.Abs,
                                                                bias=sm[:, h:h + 1], scale=-1.0),
                             reads=[pbc.d[0], sd], writes=[E.d[0]])
                    S.op("act", lambda e: e.activation(E[:], E[:], AF.Exp, scale=-1.0), reads=[E.d[0]], writes=[E.d[0]])
                    for nm_, ci in (("EUi", 2 + d), ("EUs", 4 + d), ("ELs", 5 - d)):
                        S.op("pool", lambda e, nm_=nm_, ci=ci: e.tensor_tensor(
                            B[nm_][:], E[:], cst[:, ci:ci + 1, :].to_broadcast([128, 4, 128]), ALU.mult),
                            reads=[E.d[0], cdep], writes=[B[nm_].d[0]])
                    yield
                    for i_, (nm_, src, sc_, scd) in enumerate((("kb", "kt", gbeta, gdp), ("kbg", "kt", sm[:, 20:24], sd),
                                                              ("vb", "vt", gbeta, gdp), ("qg", "qt", sm[:, 8:12], sd),
                                                              ("kd", "kt", sm[:, 16:20], sd))):
                        S.op("pool" if i_ % 2 == 0 else "dve", lambda e, nm_=nm_, src=src, sc_=sc_: e.tensor_tensor(
                            R_(B[nm_]), B[src][:], bc(sc_), ALU.mult), reads=[B[src].d[0], scd], writes=[B[nm_].d[0]])
                    for nm_, src in (("kbT", "kb"), ("qgT", "qg")):
                        pt = psr.next()
                        for h in range(4):
                            S.op("pe", lambda e, h=h, pt=pt, src=src: e.transpose(pt[:, h * 128:(h + 1) * 128], B[src][:, h, :], ident),
                                 reads=[B[src].d[0], cdep], writes=[pt.d[0]])
                        copy4(B[nm_], pt, red=True)
                    yield
                    kT, qT, kbT, qgT = B["kTr"], B["qTr"], B["kbT"], B["qgT"]
                    for nm_, lh, rh, msk in (("NT", kT, kbT, "EUs"), ("N", kbT, kT, "ELs"), ("aT", kT, qT, "EUi")):
                        pq = mm4(lh, rh, lh, rh, True)
                        S.op("dve", lambda e, nm_=nm_, pq=pq, msk=msk: e.tensor_tensor(R_(B[nm_]), v4(pq), B[msk][:], ALU.mult),
                             reads=[pq.d[0], B[msk].d[0]], writes=[B[nm_].d[0]])
                    yield
                    NT, N = B["NT"], B["N"]
                    blkb = cst[:, 6:7, :].to_broadcast([128, 4, 128])
                    X, Y = {1: B["X1"]}, {1: B["Y1"]}
                    S.op("pool", lambda e: e.tensor_tensor(R_(X[1]), N[:], blkb, ALU.mult), reads=[N.d[0], cdep], writes=[X[1].d[0]])
                    S.op("pool", lambda e: e.tensor_tensor(R_(Y[1]), NT[:], blkb, ALU.mult), reads=[NT.d[0], cdep], writes=[Y[1].d[0]])
                    P, Q = rg["P"].next(), rg["Q"].next()
                    S.op("dve", lambda e: e.tensor_tensor(R_(P), identb, X[1][:], ALU.subtract), reads=[X[1].d[0], cdep], writes=[P.d[0]])
                    S.op("pool", lambda e: e.tensor_tensor(R_(Q), identb, Y[1][:], ALU.subtract), reads=[Y[1].d[0], cdep], writes=[Q.d[0]])
                    AT = B["AT"]
                    S.op("pool", lambda e: e.tensor_tensor(R_(AT), NT[:], identb, ALU.add), reads=[NT.d[0], cdep], writes=[AT.d[0]])
                    yield
                    kp = 1
                    for k in (2, 4, 8, 16):
                        X[k], Y[k] = B["X%d" % k], B["Y%d" % k]
                        pX = mm4(Y[kp], X[kp], Y[kp], X[kp], True)
                        pY = mm4(X[kp], Y[kp], X[kp], Y[kp], True)
                        yield
                        copy4(X[k], pX, red=True)
                        copy4(Y[k], pY, red=True)
                        yield
                        pP = mm4(Q, X[k], Q, X[k], True)
                        pQ = mm4(P, Y[k], P, Y[k], True)
                        yield
                        Pn, Qn = rg["P"].next(), rg["Q"].next()
                        S.op("dve", lambda e, Pn=Pn, pP=pP, P=P: e.tensor_tensor(R_(Pn), v4(pP), P[:], ALU.add),
                             reads=[pP.d[0], P.d[0]], writes=[Pn.d[0]])
                        S.op("dve", lambda e, Qn=Qn, pQ=pQ, Q=Q: e.tensor_tensor(R_(Qn), v4(pQ), Q[:], ALU.add),
                             reads=[pQ.d[0], Q.d[0]], writes=[Qn.d[0]])
                        P, Q = Pn, Qn
                        kp = k
                    Em = B["Em"]
                    for step in range(2):
                        yield
                        pR = mm4(AT, P, AT, P, True)
                        S.op("dve", lambda e, pR=pR: e.tensor_tensor(R_(Em), identb, v4(pR), ALU.subtract),
                             reads=[pR.d[0], cdep], writes=[Em.d[0]])
                        yield
                        Qn = rg["Q"].next()
                        if step == 0:
                            pP = mm4(Q, Em, Q, Em, True)
                            Pn = rg["P"].next()
                            S.op("dve", lambda e, Pn=Pn, pP=pP, P=P: e.tensor_tensor(R_(Pn), v4(pP), P[:], ALU.add),
                                 reads=[pP.d[0], P.d[0]], writes=[Pn.d[0]])
                        pQ = mm4(Em, Q, Em, Q, True)
                        S.op("dve", lambda e, Qn=Qn, pQ=pQ, Q=Q: e.tensor_tensor(R_(Qn), v4(pQ), Q[:], ALU.add),
                             reads=[pQ.d[0], Q.d[0]], writes=[Qn.d[0]])
                        if step == 0:
                            P = Pn
                        Q = Qn
                    kbg, vb, kd, aT, wTn, vn, oo = B["kbg"], B["vb"], B["kd"], B["aT"], B["wTn"], B["vn"], B["oo"]
                    yield
                    pw = mm4(kbg, Q, kbg, Q, True)
                    yield
                    copy4(wTn, pw, -1.0, red=True)
                    pu = psr.next()
                    for h in range(4):
                        S.op("pe", lambda e, h=h: e.matmul(pu[:, h * 128:(h + 1) * 128], Q[:, h, :].bitcast(F32R), vb[:, h, :].bitcast(F32R), start=True, stop=False),
                             reads=[Q.d[0], vb.d[0]], writes=[pu.d[0]])
                        S.op("pe", lambda e, h=h: e.matmul(pu[:, h * 128:(h + 1) * 128], wTn[:, h, :].bitcast(F32R), Sd[:, h, :].bitcast(F32R), start=False, stop=True),
                             reads=[wTn.d[0], Sd.d[0]], writes=[pu.d[0]])
                    yield
                    copy4(vn, pu, red=True)
                    po = psr.next()
                    for h in range(4):
                        S.op("pe", lambda e, h=h: e.matmul(po[:, h * 128:(h + 1) * 128], qgT[:, h, :].bitcast(F32R), Sd[:, h, :].bitcast(F32R), start=True, stop=False),
                             reads=[qgT.d[0], Sd.d[0]], writes=[po.d[0]])
                        S.op("pe", lambda e, h=h: e.matmul(po[:, h * 128:(h + 1) * 128], aT[:, h, :].bitcast(F32R), vn[:, h, :].bitcast(F32R), start=False, stop=True),
                             reads=[aT.d[0], vn.d[0]], writes=[po.d[0]])
                    yield
                    copy4(oo, po)
                    S.dma("pool", o_gd[d, t0:t0 + 128, :].rearrange("p (h d) -> p h d", h=4), oo[:], reads=[oo.d[0]])
                    pS = mm4(kd, vn, kd, vn, True)
                    yield
                    tS = B["tmpS"]
                    S.op("pool", lambda e: e.tensor_tensor(tS[:], Sd[:], bc(sm[:, 12:16]), ALU.mult), reads=[Sd.d[0], sd], writes=[tS.d[0]])
                    S.op("dve", lambda e: e.tensor_tensor(R_(Sd), v4(pS), tS[:], ALU.add), reads=[pS.d[0], tS.d[0]], writes=[Sd.d[0]])
                    if not first:
                        S.dma("pool", st_gdS[l, slot, d], Sd[:], reads=[Sd.d[0]])
                yield [body(0), body(1)]

    MAGIC = 12582912.0
    INV2PI = 1.0 / (2.0 * math.pi)
    TWOPI_LO = 6.2831845

    def phase_s5(l):
        psr = Ring(ps[0:4])
        with ExitStack() as st:
            def F(name, shape, n=1):
                return Ring([sb("s5%s%d_%d" % (name, l, i), shape, stack=st) for i in range(n)])
            Bp = [[F("Bp%d%d" % (d, ri), [128, 16, 128]).next() for ri in range(2)] for d in range(2)]
            Ct = [F("Ct%d" % ri, [128, 16, 128]).next() for ri in range(2)]
            cosT = [F("cos%d" % d, [128, 16, 128]).next() for d in range(2)]
            sinT = [F("sin%d" % d, [128, 16, 128]).next() for d in range(2)]
            rmat = [F("rmat%d" % d, [128, 16, 128]).next() for d in range(2)]
            smp = F("smp", [128, 2, 3, 16]).next()
            rsm = F("rsm", [128, 2, 16]).next()
            thy = F("thy", [128, 2, 16]).next()
            H = [F("H%d" % d, [128, 2, 16]).next() for d in range(2)]

            def fl(b):
                return b[:].rearrange("p a t -> p (a t)")

            def sincos(ybuf, dst_sin, dst_cos, t1, t2):
                for dst, shift in ((dst_sin, 0.0), (dst_cos, 0.25)):
                    S.op("dve", lambda e: e.tensor_scalar(fl(t1), ybuf, shift, MAGIC, ALU.add, ALU.add), reads=ybd, writes=[t1.d[0]])
                    S.op("dve", lambda e: e.tensor_scalar(fl(t1), fl(t1), MAGIC, None, ALU.subtract), reads=[t1.d[0]], writes=[t1.d[0]])
                    S.op("dve", lambda e: e.scalar_tensor_tensor(fl(t2), ybuf, shift, fl(t1), ALU.add, ALU.subtract),
                         reads=ybd + [t1.d[0]], writes=[t2.d[0]])
                    S.op("act", lambda e: e.activation(fl(dst), fl(t2), AF.Sin, scale=TWOPI_LO), reads=[t2.d[0]], writes=[dst.d[0]])

            S.dma("sp", smp[:], s5sm[:, l, :, :, :], writes=[smp.d[0]])
            S.dma("sp", Ct[0][:], s5Ct[:, l, 0, :, :], writes=[Ct[0].d[0]])
            S.dma("sp", Ct[1][:], s5Ct[:, l, 1, :, :], writes=[Ct[1].d[0]])
            S.op("dve", lambda e: e.tensor_scalar(fl(Ct[1]), fl(Ct[1]), -1.0, None, ALU.mult), reads=[Ct[1].d[0]], writes=[Ct[1].d[0]])
            for d in range(2):
                S.dma("sp", H[d][:], s5h0[:, l, d, :, :], writes=[H[d].d[0]])
            dtm = F("dtm", [128, 2, 16]).next()
            S.op("act", lambda e: e.activation(dtm[:], smp[:, :, 2, :], AF.Exp), reads=[smp.d[0]], writes=[dtm.d[0]])
            S.op("dve", lambda e: e.tensor_tensor(rsm[:], smp[:, :, 0, :], dtm[:], ALU.mult), reads=[smp.d[0], dtm.d[0]], writes=[rsm.d[0]])
            S.op("act", lambda e: e.activation(rsm[:], rsm[:], AF.Exp), reads=[rsm.d[0]], writes=[rsm.d[0]])
            S.op("dve", lambda e: e.scalar_tensor_tensor(thy[:], smp[:, :, 1, :], INV2PI, dtm[:], ALU.mult, ALU.mult),
                 reads=[smp.d[0], dtm.d[0]], writes=[thy.d[0]])
            with ExitStack() as st2:
                def G(name):
                    return sb("s5t%s%d" % (name, l), [128, 16, 128], stack=st2)
                LR, LI, LS, DT, AR, AI, T1, T2, ZR, ZI, BR, BI = [G(n) for n in
                                                                  ("LR", "LI", "LS", "DT", "AR", "AI", "T1", "T2", "ZR", "ZI", "BR", "BI")]
                S.dma("sp", BR[:], s5Bt[:, l, 0, :, :], writes=[BR.d[0]])
                S.dma("sp", BI[:], s5Bt[:, l, 1, :, :], writes=[BI.d[0]])
                for d in range(2):
                    for buf, k in ((LR, 0), (LI, 1), (LS, 2)):
                        S.dma("sp", fl(buf), s5rep[:, l, d, k, :], writes=[buf.d[0]])
                    S.op("act", lambda e: e.activation(fl(DT), fl(LS), AF.Exp), reads=[LS.d[0]], writes=[DT.d[0]])
                    S.op("dve", lambda e: e.tensor_tensor(fl(AR), fl(LR), fl(DT), ALU.mult), reads=[LR.d[0], DT.d[0]], writes=[AR.d[0]])
                    S.op("act", lambda e: e.activation(fl(AR), fl(AR), AF.Exp), reads=[AR.d[0]], writes=[AR.d[0]])
                    S.op("dve", lambda e: e.scalar_tensor_tensor(fl(AI), fl(LI), INV2PI, fl(DT), ALU.mult, ALU.mult),
                         reads=[LI.d[0], DT.d[0]], writes=[AI.d[0]])
                    ybd = [AI.d[0]]
                    sincos(fl(AI), ZI, ZR, T1, T2)
                    S.op("dve", lambda e: e.tensor_tensor(fl(AI), fl(AR), fl(ZI), ALU.mult), reads=[AR.d[0], ZI.d[0]], writes=[AI.d[0]])
                    S.op("dve", lambda e: e.tensor_tensor(fl(AR), fl(AR), fl(ZR), ALU.mult), reads=[AR.d[0], ZR.d[0]], writes=[AR.d[0]])
                    S.op("dve", lambda e: e.tensor_scalar(fl(AR), fl(AR), -1.0, None, ALU.add), reads=[AR.d[0]], writes=[AR.d[0]])
                    S.op("dve", lambda e: e.tensor_tensor(fl(T1), fl(LR), fl(LR), ALU.mult), reads=[LR.d[0]], writes=[T1.d[0]])
                    S.op("dve", lambda e: e.tensor_tensor(fl(T2), fl(LI), fl(LI), ALU.mult), reads=[LI.d[0]], writes=[T2.d[0]])
                    S.op("dve", lambda e: e.tensor_tensor(fl(T1), fl(T1), fl(T2), ALU.add), reads=[T1.d[0], T2.d[0]], writes=[T1.d[0]])
                    S.op("dve", lambda e: e.reciprocal(fl(T1), fl(T1)), reads=[T1.d[0]], writes=[T1.d[0]])
                    S.op("dve", lambda e: e.tensor_tensor(fl(ZR), fl(AR), fl(LR), ALU.mult), reads=[AR.d[0], LR.d[0]], writes=[ZR.d[0]])
                    S.op("dve", lambda e: e.tensor_tensor(fl(T2), fl(AI), fl(LI), ALU.mult), reads=[AI.d[0], LI.d[0]], writes=[T2.d[0]])
                    S.op("dve", lambda e: e.tensor_tensor(fl(ZR), fl(ZR), fl(T2), ALU.add), reads=[ZR.d[0], T2.d[0]], writes=[ZR.d[0]])
                    S.op("dve", lambda e: e.tensor_tensor(fl(ZR), fl(ZR), fl(T1), ALU.mult), reads=[ZR.d[0], T1.d[0]], writes=[ZR.d[0]])
                    S.op("dve", lambda e: e.tensor_tensor(fl(ZI), fl(AI), fl(LR), ALU.mult), reads=[AI.d[0], LR.d[0]], writes=[ZI.d[0]])
                    S.op("dve", lambda e: e.tensor_tensor(fl(T2), fl(AR), fl(LI), ALU.mult), reads=[AR.d[0], LI.d[0]], writes=[T2.d[0]])
                    S.op("dve", lambda e: e.tensor_tensor(fl(ZI), fl(ZI), fl(T2), ALU.subtract), reads=[ZI.d[0], T2.d[0]], writes=[ZI.d[0]])
                    S.op("dve", lambda e: e.tensor_tensor(fl(ZI), fl(ZI), fl(T1), ALU.mult), reads=[ZI.d[0], T1.d[0]], writes=[ZI.d[0]])
                    bre, bim = Bp[d]
                    S.op("dve", lambda e: e.tensor_tensor(fl(bre), fl(ZR), fl(BR), ALU.mult), reads=[ZR.d[0], BR.d[0]], writes=[bre.d[0]])
                    S.op("dve", lambda e: e.tensor_tensor(fl(T2), fl(ZI), fl(BI), ALU.mult), reads=[ZI.d[0], BI.d[0]], writes=[T2.d[0]])
                    S.op("dve", lambda e: e.tensor_tensor(fl(bre), fl(bre), fl(T2), ALU.subtract), reads=[bre.d[0], T2.d[0]], writes=[bre.d[0]])
                    S.op("dve", lambda e: e.tensor_tensor(fl(bim), fl(ZR), fl(BI), ALU.mult), reads=[ZR.d[0], BI.d[0]], writes=[bim.d[0]])
                    S.op("dve", lambda e: e.tensor_tensor(fl(T2), fl(ZI), fl(BR), ALU.mult), reads=[ZI.d[0], BR.d[0]], writes=[T2.d[0]])
                    S.op("dve", lambda e: e.tensor_tensor(fl(bim), fl(bim), fl(T2), ALU.add), reads=[bim.d[0], T2.d[0]], writes=[bim.d[0]])
                    S.op("dve", lambda e: e.tensor_tensor(AI[:], cst[:, 7:8, :].to_broadcast([128, 16, 128]),
                                                          thy[:, d, :].unsqueeze(2).to_broadcast([128, 16, 128]), ALU.mult),
                         reads=[cdep, thy.d[0]], writes=[AI.d[0]])
                    ybd = [AI.d[0]]
                    sincos(fl(AI), sinT[d], cosT[d], T1, T2)
                    S.op("dve", lambda e: e.tensor_copy(rmat[d][:], rsm[:, d, :].unsqueeze(2).to_broadcast([128, 16, 128])),
                         reads=[rsm.d[0]], writes=[rmat[d].d[0]])
                    fpos = 0 if d == 0 else 127
                    S.op("dve", lambda e: e.memset(rmat[d][:, :, fpos:fpos + 1], 0.0), writes=[rmat[d].d[0]])
                S.barrier()
            uTr = F("uT", [128, 4, 128], 2)
            xr_r, xi_r = F("xr", [128, 16, 128], 2), F("xi", [128, 16, 128], 2)
            tr = [F("t%d" % i, [128, 4, 128], 2) for i in range(4)]
            smallH = F("sH", [128, 2, 16], 2)
            y5r = F("y5", [128, 4, 128], 2)
            ev = [0]
            for it in range(NTT):
                def body(d, it=it):
                    c = it if d == 0 else NTT - 1 - it
                    first = (c % 2 == 0) if d == 0 else (c % 2 == 1)
                    slot = c // 2
                    t0 = c * 128
                    Hd = H[d]
                    fpos = 0 if d == 0 else 127
                    lpos = 127 if d == 0 else 0

                    def dv(ap3):
                        return ap3 if d == 0 else ap3[:, :, ::-1]
                    if first and it > 0:
                        S.op("dve", lambda e: e.tensor_scalar(Hd[:], Hd[:], keep[:, 0:1], None, ALU.mult),
                             reads=[Hd.d[0], keep.d[0]], writes=[Hd.d[0]])
                    uT = uTr.next()
                    S.dma("sp", uT[:], s5_uT[:, :, t0:t0 + 128], writes=[uT.d[0]])
                    yield
                    xr, xi = xr_r.next(), xi_r.next()
                    gr, gi = xr, xi
                    for q4 in range(4):
                        yield
                        pre, pim = psr.next(), psr.next()
                        for j4 in range(4):
                            j = q4 * 4 + j4
                            for pp, ri in ((pre, 0), (pim, 1)):
                                S.op("pe", lambda e, pp=pp, ri=ri, j=j, j4=j4: e.matmul(
                                    pp[:, j4 * 128:(j4 + 1) * 128], f32r(Bp[d][ri][:, j, :]), f32r(uT[:, j // 4, :]), start=True, stop=True),
                                    reads=[Bp[d][ri].d[0], uT.d[0]], writes=[pp.d[0]])
                        yield
                        cv, sv = dv(cosT[d][:, q4 * 4:q4 * 4 + 4, :]), dv(sinT[d][:, q4 * 4:q4 * 4 + 4, :])
                        pr3 = pre[:, :].rearrange("p (a t) -> p a t", a=4)
                        pi3 = pim[:, :].rearrange("p (a t) -> p a t", a=4)
                        t1, t2, t3, t4 = [r_.next() for r_ in tr]
                        tdep = [cosT[d].d[0], sinT[d].d[0]]
                        S.op("dve", lambda e: e.tensor_tensor(t1[:], pr3, cv, ALU.mult), reads=[pre.d[0]] + tdep, writes=[t1.d[0]])
                        S.op("dve", lambda e: e.tensor_tensor(t2[:], pi3, sv, ALU.mult), reads=[pim.d[0]] + tdep, writes=[t2.d[0]])
                        S.op("pool", lambda e: e.tensor_tensor(xr[:, q4 * 4:q4 * 4 + 4, :], t1[:], t2[:], ALU.add),
                             reads=[t1.d[0], t2.d[0]], writes=[xr.d[0]])
                        S.op("dve", lambda e: e.tensor_tensor(t3[:], pi3, cv, ALU.mult), reads=[pim.d[0]] + tdep, writes=[t3.d[0]])
                        S.op("dve", lambda e: e.tensor_tensor(t4[:], pr3, sv, ALU.mult), reads=[pre.d[0]] + tdep, writes=[t4.d[0]])
                        S.op("pool", lambda e: e.tensor_tensor(xi[:, q4 * 4:q4 * 4 + 4, :], t3[:], t4[:], ALU.subtract),
                             reads=[t3.d[0], t4.d[0]], writes=[xi.d[0]])
                    yield
                    sH = smallH.next()
                    S.op("dve", lambda e: e.tensor_tensor(sH[:], Hd[:], rsm[:, d:d + 1, :].to_broadcast([128, 2, 16]), ALU.mult),
                         reads=[Hd.d[0], rsm.d[0]], writes=[sH.d[0]])
                    for xb_, ri in ((xr, 0), (xi, 1)):
                        S.op("dve", lambda e, xb_=xb_, ri=ri: e.tensor_tensor(
                            xb_[:, :, fpos:fpos + 1], xb_[:, :, fpos:fpos + 1], sH[:, ri, :].unsqueeze(2), ALU.add),
                            reads=[xb_.d[0], sH.d[0]], writes=[xb_.d[0]])

                    def fv(b):
                        v = fl(b)
                        return v if d == 0 else v[:, ::-1]
                    yield
                    for xb_, gb_ in ((xr, gr), (xi, gi)):
                        S.op("dve", lambda e, xb_=xb_, gb_=gb_: e.tensor_tensor_scan(
                            fv(gb_), fv(rmat[d]), fv(xb_), 0.0, ALU.mult, ALU.add),
                            reads=[xb_.d[0], rmat[d].d[0]], writes=[gb_.d[0]])
                    for q4 in range(4):
                        yield
                        sl_ = slice(q4 * 4, q4 * 4 + 4)
                        cv, sv = dv(cosT[d][:, sl_, :]), dv(sinT[d][:, sl_, :])
                        t1, t2, t3, t4 = [r_.next() for r_ in tr]
                        tdep = [cosT[d].d[0], sinT[d].d[0]]
                        S.op("pool", lambda e: e.tensor_tensor(t1[:], gr[:, sl_, :], cv, ALU.mult), reads=[gr.d[0]] + tdep, writes=[t1.d[0]])
                        S.op("pool", lambda e: e.tensor_tensor(t2[:], gi[:, sl_, :], sv, ALU.mult), reads=[gi.d[0]] + tdep, writes=[t2.d[0]])
                        S.op("pool", lambda e: e.tensor_tensor(t3[:], gi[:, sl_, :], cv, ALU.mult), reads=[gi.d[0]] + tdep, writes=[t3.d[0]])
                        S.op("dve", lambda e: e.tensor_tensor(t4[:], gr[:, sl_, :], sv, ALU.mult), reads=[gr.d[0]] + tdep, writes=[t4.d[0]])
                        S.op("dve", lambda e: e.tensor_tensor(xr[:, sl_, :], t1[:], t2[:], ALU.subtract), reads=[t1.d[0], t2.d[0]], writes=[xr.d[0]])
                        S.op("dve", lambda e: e.tensor_tensor(xi[:, sl_, :], t3[:], t4[:], ALU.add), reads=[t3.d[0], t4.d[0]], writes=[xi.d[0]])
                    for xb_, ri in ((xr, 0), (xi, 1)):
                        S.op("act", lambda e, xb_=xb_, ri=ri: e.copy(Hd[:, ri, :].unsqueeze(2), xb_[:, :, lpos:lpos + 1]),
                             reads=[xb_.d[0]], writes=[Hd.d[0]])
                    yield
                    py = psr.next()
                    for kq in range(4):
                        n_ = 0
                        for j in range(kq * 4, kq * 4 + 4):
                            for hb_, ri in ((xr, 0), (xi, 1)):
                                S.op("pe", lambda e, j=j, hb_=hb_, ri=ri, kq=kq, n_=n_: e.matmul(
                                    py[:, kq * 128:(kq + 1) * 128], f32r(Ct[ri][:, j, :]), f32r(hb_[:, j, :]), start=(n_ == 0), stop=(n_ == 7)),
                                    reads=[Ct[ri].d[0], hb_.d[0]], writes=[py.d[0]])
                                n_ += 1
                    yield
                    y5 = y5r.next()
                    ev[0] += 1
                    evac(ev[0], y5[:].rearrange("p a t -> p (a t)"), py[:, :], [py.d[0]], [y5.d[0]])
                    S.dma("pool", y_s5T[d, :, :, t0:t0 + 128], y5[:], reads=[y5.d[0]])
                    if not first:
                        S.dma("pool", st_s5[l, slot, d], Hd[:], reads=[Hd.d[0]])
                yield [body(0), body(1)]

    for l in range(nlayers):
        S.barrier()
        stL = ExitStack()
        hT = sb("hT%d" % l, [128, KT, T], BF16, n=NBLK, stack=stL)
        hTh[0] = hT
        with ExitStack() as st:
            xbr = Ring([sb("xb%d_%d" % (l, i), [128, KT, 512], stack=st) for i in range(2)])
            tmpr = Ring([sb("xtmp%d_%d" % (l, i), [128, KT, 512], stack=st) for i in range(2)])
            if l == 0:
                xtr = Ring([sb("xtok%d" % i, [128, D], stack=st) for i in range(4)])
                posr = Ring([sb("pos%d" % i, [128, KT, 512], stack=st) for i in range(2)])
            for blk in range(NBLK):
                xb = xbr.next()
                tmp = tmpr.next()
                if l == 0:
                    pb = posr.next()
                    S.dma("sp", pb[:], pos_in[:, :, blk * 512:(blk + 1) * 512], writes=[pb.d[0]])
                    xts = []
                    for tt in range(4):
                        xt = xtr.next()
                        r0 = blk * 512 + tt * 128
                        S.dma("sp", xt[:], x_in[r0:r0 + 128, :], writes=[xt.d[0]])
                        xts.append(xt)
                    for kt in range(KT):
                        pt = psr.next()
                        for tt in range(4):
                            S.op("pe", lambda e, tt=tt, kt=kt, pt=pt: e.transpose(
                                pt[:, tt * 128:(tt + 1) * 128], xts[tt][:, kt * 128:(kt + 1) * 128], ident),
                                reads=[xts[tt].d[0], cdep], writes=[pt.d[0]])
                        S.op("dve", lambda e, kt=kt, pt=pt: e.tensor_tensor(xb[:, kt, :], pt[:, :], pb[:, kt, :], ALU.add),
                             reads=[pt.d[0], pb.d[0]], writes=[xb.d[0]])
                    S.dma("pool", xT_scr[:, :, blk * 512:(blk + 1) * 512], xb[:], reads=[xb.d[0]])
                else:
                    S.dma("sp", xb[:], xT_scr[:, :, blk * 512:(blk + 1) * 512], writes=[xb.d[0]])
                rmsnorm_block(st, xb, l, 0, blk, tmp)
            if hT_dbg is not None and l == 0:
                S.dma("pool", hT_dbg, hT[:], reads=hT.d)

        S.barrier()
        with ExitStack() as st:
            wr = Ring([sb("wA%d_%d" % (l, i), [128, KT, 512], BF16, stack=st) for i in range(2)])
            stg = Ring([sb("stgA%d_%d" % (l, i), [128, 512], stack=st) for i in range(4)])
            wsrc = w_in[l].rearrange("(kt p) c -> p kt c", p=128)
            ecount = [0]

            wfr = Ring([sb("wAf%d_%d" % (l, i), [128, KT, 512], F32, stack=st) for i in range(2)])

            def load_w(c0, ncol=512, w=None, o=0):
                wf = wfr.next()
                if w is None:
                    w = wr.next()
                S.dma("sp", wf[:, :, 0:ncol], wsrc[:, :, c0:c0 + ncol], writes=[wf.d[0]])
                ecount[0] += 1
                evac(ecount[0], w[:, :, o:o + ncol], wf[:, :, 0:ncol], [wf.d[0]], [w.d[0]])
                return w

            def feat_major(w, dst, scale=None):
                for ct in range(4):
                    for blk in range(NBLK):
                        pq = psr.next()
                        for kt in range(KT):
                            S.op("pe", lambda e, kt=kt, ct=ct, blk=blk, pq=pq: e.matmul(
                                pq[:, :], w[:, kt, ct * 128:(ct + 1) * 128], hT[:, kt, blk * 512:(blk + 1) * 512],
                                start=(kt == 0), stop=(kt == KT - 1)),
                                reads=[w.d[0], hT.d[blk]], writes=[pq.d[0]])
                        sg = stg.next()
                        ecount[0] += 1
                        evac(ecount[0], sg[:], pq[:, :], [pq.d[0]], [sg.d[0]], scale)
                        S.dma("sp", dst[:, ct, blk * 512:(blk + 1) * 512], sg[:], reads=[sg.d[0]])

            def tok_major(w, ncol, consume):
                for tt in range(NTT):
                    pq = psr.next()
                    blk = tt // 4
                    for kt in range(KT):
                        S.op("pe", lambda e, kt=kt, tt=tt, pq=pq: e.matmul(
                            pq[:, 0:ncol], hT[:, kt, tt * 128:(tt + 1) * 128], w[:, kt, 0:ncol],
                            start=(kt == 0), stop=(kt == KT - 1)),
                            reads=[w.d[0], hT.d[blk]], writes=[pq.d[0]])
                    consume(tt, pq)

            w = load_w(C_MQ)
            if dbg and "wdbg" in dbg and l == 0:
                wdbg = dscr("wdbg", [128, KT, 512], BF16)
                S.dma("sp", wdbg, w[:], reads=[w.d[0]])
            feat_major(w, ml_qT)
            w = load_w(C_MK)
            ksc = 128 ** -0.5
            feat_major(w, ml_kT, ksc)

            def cons_k(tt, pq):
                sg = stg.next()
                ecount[0] += 1
                evac(ecount[0], sg[:], pq[:, :], [pq.d[0]], [sg.d[0]], ksc)
                S.dma("sp", ml_k[tt * 128:(tt + 1) * 128, :], sg[:], reads=[sg.d[0]])
            tok_major(w, 512, cons_k)
            w = load_w(C_MV)
            vstg = Ring([sb("vstg%d_%d" % (l, i), [128, 4, 129], stack=st) for i in range(2)])
            for vb in vstg.bufs:
                S.op("pool", lambda e, vb=vb: e.memset(vb[:], 1.0), writes=[vb.d[0]])

            def cons_v(tt, pq):
                sg = vstg.next()
                ecount[0] += 1
                evac(ecount[0], sg[:, :, 0:128], pq[:, :].rearrange("p (h d) -> p h d", h=4), [pq.d[0]], [sg.d[0]])
                S.dma("sp", ml_v[tt * 128:(tt + 1) * 128, :, :], sg[:], reads=[sg.d[0]])
            tok_major(w, 512, cons_v)
            w = load_w(C_SU)
            feat_major(w, s5_uT)
            w = load_w(C_MI, 16)
            load_w(C_GA, 16, w=w, o=16)
            pg = [psr.next(), psr.next()]
            for tt in range(NTT):
                pq = pg[tt // 16]
                o = (tt % 16) * 32
                for kt in range(KT):
                    S.op("pe", lambda e, kt=kt, tt=tt, pq=pq, o=o: e.matmul(
                        pq[:, o:o + 32], hT[:, kt, tt * 128:(tt + 1) * 128], w[:, kt, 0:32],
                        start=(kt == 0), stop=(kt == KT - 1)),
                        reads=[w.d[0], hT.d[tt // 4]], writes=[pq.d[0]])
            gtmp = sb("gtmp%d" % l, [128, 16, 8], stack=st)
            gd_ = gates.d[0]
            for half in range(2):
                pq = pg[half]
                pv = pq[:, :].rearrange("p (t c) -> p t c", c=32)
                gv = gates[:, half * 16:(half + 1) * 16, :]

                def bias(k):
                    return gb[:, l, k * 8:(k + 1) * 8].unsqueeze(1).to_broadcast([128, 16, 8])
                S.op("dve", lambda e: e.tensor_tensor(gv[:, :, 0:8], pv[:, :, 0:8], bias(0), ALU.add),
                     reads=[pq.d[0], gb.d[0]], writes=[gd_])
                S.op("dve", lambda e: e.tensor_tensor(gtmp[:], pv[:, :, 8:16], bias(1), ALU.add),
                     reads=[pq.d[0], gb.d[0]], writes=[gtmp.d[0]])
                S.op("act", lambda e: e.activation(gtmp[:], gtmp[:], AF.Exp, scale=-1.0),
                     reads=[gtmp.d[0]], writes=[gtmp.d[0]])
                S.op("act", lambda e: e.activation(gtmp[:], gtmp[:], AF.Ln, bias=1.0),
                     reads=[gtmp.d[0]], writes=[gtmp.d[0]])
                S.op("dve", lambda e: e.tensor_scalar(gv[:, :, 8:16], gtmp[:], -1.0, None, ALU.mult),
                     reads=[gtmp.d[0]], writes=[gd_])
                S.op("dve", lambda e: e.tensor_tensor(gtmp[:], pv[:, :, 16:24], bias(2), ALU.add),
                     reads=[pq.d[0], gb.d[0]], writes=[gtmp.d[0]])
                S.op("act", lambda e: e.activation(gtmp[:], gtmp[:], AF.Exp), reads=[gtmp.d[0]], writes=[gtmp.d[0]])
                S.op("act", lambda e: e.activation(gtmp[:], gtmp[:], AF.Ln, bias=1.0),
                     reads=[gtmp.d[0]], writes=[gtmp.d[0]])
                S.op("dve", lambda e: e.tensor_tensor(gv[:, :, 16:24], gtmp[:], negA[:, l, :].unsqueeze(1).to_broadcast(
                    [128, 16, 8]), ALU.mult), reads=[gtmp.d[0], negA.d[0]], writes=[gd_])
                S.op("act", lambda e: e.activation(gv[:, :, 24:32], pv[:, :, 24:32], AF.Sigmoid),
                     reads=[pq.d[0]], writes=[gd_])
            if gates_dbg is not None and l == 0:
                S.dma("sp", gates_dbg, gates[:], reads=[gd_])

            dgw = sb("dgw%d" % l, [128, 60, 128], stack=st)
            S.op("dve", lambda e: e.tensor_tensor(
                dgw[:], cst[:, 0:1, :].to_broadcast([128, 60, 128]),
                cw[:, l, :, :].rearrange("p c j -> p (c j)").unsqueeze(2).to_broadcast([128, 60, 128]), ALU.mult),
                reads=[cdep, cw.d[0]], writes=[dgw.d[0]])
            xpr = Ring([sb("xpad%d_%d" % (l, i), [128, 4, 260], stack=st) for i in range(2)])
            yallr = Ring([sb("yall%d_%d" % (l, i), [128, 4, 256], stack=st) for i in range(2)])
            sqr = Ring([sb("csq%d_%d" % (l, i), [128, 4, 256], stack=st) for i in range(2)])
            wq3 = [load_w(C_GQKV + g3 * 512) for g3 in range(2)]

            def conv_item(g3, slot, w):
                dT, dtok = ((gd_qT, gd_q), (gd_kT, gd_k), (None, gd_v))[g3]
                t0 = slot * SLOT
                lo = t0 - 2 if slot > 0 else t0
                hi = t0 + SLOT + 2 if slot < NSLOT - 1 else t0 + SLOT
                off = lo - (t0 - 2)
                n = hi - lo
                hdeps = [hT.d[b] for b in range(lo // 512, (hi - 1) // 512 + 1)]
                pqs = []
                for ct in range(4):
                    pq = psr.next()
                    pqs.append(pq)
                    for kt in range(KT):
                        S.op("pe", lambda e, kt=kt, ct=ct, pq=pq: e.matmul(
                            pq[:, off:off + n], w[:, kt, ct * 128:(ct + 1) * 128], hT[:, kt, lo:hi],
                            start=(kt == 0), stop=(kt == KT - 1)),
                            reads=[w.d[0]] + hdeps, writes=[pq.d[0]])
                yield
                xp = xpr.next()
                for ct in range(4):
                    ecount[0] += 1
                    evac(ecount[0], xp[:, ct, off:off + n], pqs[ct][:, off:off + n], [pqs[ct].d[0]], [xp.d[0]])
                for (a, b, inside) in ((0, 2, slot > 0), (258, 260, slot < NSLOT - 1)):
                    if inside:
                        S.op("dve", lambda e, a=a, b=b: e.tensor_scalar(xp[:, :, a:b], xp[:, :, a:b], keep[:, 0:1], None, ALU.mult),
                             reads=[xp.d[0], keep.d[0]], writes=[xp.d[0]])
                    else:
                        S.op("dve", lambda e, a=a, b=b: e.memset(xp[:, :, a:b], 0.0), writes=[xp.d[0]])
                yield
                pcs_ = [psr.next(), psr.next()]
                for ct in range(4):
                    pc = pcs_[ct // 2]
                    o_ = (ct % 2) * 256
                    for j in range(5):
                        S.op("pe", lambda e, ct=ct, j=j, pc=pc, o_=o_: e.matmul(
                            pc[:, o_:o_ + 256], f32r(dgw[:, (g3 * 4 + ct) * 5 + j, :]), f32r(xp[:, ct, j:j + 256]),
                            start=(j == 0), stop=(j == 4)), reads=[dgw.d[0], xp.d[0]], writes=[pc.d[0]])
                yield
                yall = yallr.next()
                for half in range(2):
                    S.op("act", lambda e, half=half: e.activation(
                        yall[:, half * 2:half * 2 + 2, :], pcs_[half][:, :].rearrange("p (c t) -> p c t", c=2), AF.Silu),
                        reads=[pcs_[half].d[0]], writes=[yall.d[0]])
                if g3 < 2:
                    yield
                    sq = sqr.next()
                    S.op("pool", lambda e: e.tensor_tensor(sq[:], yall[:], yall[:], ALU.mult), reads=[yall.d[0]], writes=[sq.d[0]])
                    pns = [psr.next(), psr.next()]
                    for ct in range(4):
                        S.op("pe", lambda e, ct=ct: e.matmul(pns[ct // 2][:, (ct % 2) * 256:(ct % 2 + 1) * 256], ones, sq[:, ct, :],
                                                             start=True, stop=True), reads=[sq.d[0], cdep], writes=[pns[ct // 2].d[0]])
                    yield
                    rs = sq
                    for half in range(2):
                        S.op("act", lambda e, half=half: e.activation(
                            rs[:, half * 2:half * 2 + 2, :], pns[half][:, :].rearrange("p (c t) -> p c t", c=2), AF.Sqrt,
                            bias=epsb[:, 0:1]), reads=[pns[half].d[0], epsb.d[0]], writes=[rs.d[0]])
                    yield
                    S.op("dve", lambda e: e.reciprocal(rs[:], rs[:]), reads=[rs.d[0]], writes=[rs.d[0]])
                    qs = (128 ** -0.5) if g3 == 0 else 1.0
                    S.op("dve", lambda e: e.scalar_tensor_tensor(yall[:], yall[:], qs, rs[:], ALU.mult, ALU.mult),
                         reads=[yall.d[0], rs.d[0]], writes=[yall.d[0]])
                    S.dma("sp", dT[:, :, t0:t0 + SLOT], yall[:], reads=[yall.d[0]])
                yield
                pts = []
                for t2 in range(2):
                    pt = psr.next()
                    pts.append(pt)
                    for ct in range(4):
                        S.op("pe", lambda e, ct=ct, pt=pt, t2=t2: e.transpose(
                            pt[:, ct * 128:(ct + 1) * 128], yall[:, ct, t2 * 128:(t2 + 1) * 128], ident),
                            reads=[yall.d[0], cdep], writes=[pt.d[0]])
                yield
                for t2 in range(2):
                    sg = stg.next()
                    ecount[0] += 1
                    evac(ecount[0], sg[:], pts[t2][:, :], [pts[t2].d[0]], [sg.d[0]])
                    S.dma("sp", dtok[t0 + t2 * 128:t0 + (t2 + 1) * 128, :], sg[:], reads=[sg.d[0]])

            def conv_items():
                for g3 in range(3):
                    w = wq3[g3] if g3 < 2 else load_w(C_GQKV + 2 * 512)
                    for slot in range(NSLOT):
                        yield conv_item(g3, slot, w)
            drive_window(conv_items(), 2)
        S.barrier()
        stL.close()
        hTh[0] = None

        if dbg and "stopA" in dbg:
            continue
        S.barrier()
        for gens in phase_gdn(l):
            drive(gens)
        S.barrier()
        it5, itm = phase_s5(l), phase_mlstm(l)
        for _ in range(NTT):
            g5 = next(it5)
            gm = next(itm)
            drive(g5 + gm)
        for _ in itm:
            pass
        for _ in it5:
            pass
        S.barrier()
        with ExitStack() as st:
            xbr = Ring([sb("cxb%d_%d" % (l, i), [128, KT, 512], stack=st) for i in range(1)])
            tmpr = Ring([sb("ctmp%d_%d" % (l, i), [128, KT, 512], stack=st) for i in range(1)])
            h2r = Ring([sb("h2_%d_%d" % (l, i), [128, KT, 512], BF16, stack=st) for i in range(1)])
            actr = Ring([sb("actT%d_%d" % (l, i), [128, FT, 512], BF16, stack=st) for i in range(1)])
            wgb = Ring([sb("wgb%d_%d" % (l, i), [128, KT, 256], BF16, stack=st) for i in range(5)])
            h1 = sb("h1_%d" % l, [128, KT, 512], BF16, stack=st)
            ysT = [sb("ysT%d_%d" % (l, n), [128, 4, 512], BF16, stack=st) for n in range(3)]
            mg = sb("mg%d" % l, [128, KT, 512], BF16, stack=st)
            maccs = [sb("macc%d_%d" % (l, i), [128, 512], stack=st) for i in range(2)]
            s5u = sb("s5u%d" % l, [128, 4, 512], stack=st)
            gbf = sb("gbf%d" % l, [128, 4, 512], BF16, stack=st)
            tkr = Ring([sb("tk%d_%d" % (l, i), [128, 512], stack=st) for i in range(4)])
            ssr = Ring([sb("ss%d_%d" % (l, i), [128, 4], stack=st) for i in range(2)])
            nwm = sb("nwm%d" % l, [128, 512], stack=st)
            nwg = sb("nwg%d" % l, [128, 128], stack=st)
            dbt = sb("dbt%d" % l, [128, 2, 4], stack=st)
            S.dma("sp", nwm[:], mlnw[:, l, :], writes=[nwm.d[0]])
            S.dma("sp", nwg[:], gdnw[:, l, :], writes=[nwg.d[0]])
            S.dma("sp", dbt[:], s5DbT[:, l, :, :], writes=[dbt.d[0]])
            win_src = w_in[l].rearrange("(kt p) c -> p kt c", p=128)
            wout_src = w_out[l].rearrange("(kt p) c -> p kt c", p=128)
            glu_src = glu_w[l].rearrange("(kt p) c -> p kt c", p=128)
            wbr_src = [w_branch[l, n].rearrange("(kt p) c -> p kt c", p=128) for n in range(3)]
            wdb = Ring([sb("wdb%d_%d" % (l, i), [128, FT, 128], BF16, stack=st) for i in range(2)])
            sgr = Ring([sb("csg%d_%d" % (l, i), [128, 512], stack=st) for i in range(3)])
            last = (l == DEPTH - 1)
            if last:
                ytr = Ring([sb("ytok%d" % i, [128, 512], stack=st) for i in range(1)])
            wg_src = w_gate[l].rearrange("(kt p) c -> p kt c", p=128)
            wu_src = w_up[l].rearrange("(kt p) c -> p kt c", p=128)
            wd_src = w_down[l].rearrange("(ft p) c -> p ft c", p=128)
            ec = [0]
            wgs_keep = {}
            wcache = {}

            def load_cast(src, fr, br, nk=None, key=None):
                shp = list(br.bufs[0].t.shape)
                if key is not None and key in wcache:
                    cd, cdep_ = wcache[key]
                    wb_ = br.next()
                    if nk is None:
                        S.dma("sp", wb_[:], cd, reads=[cdep_], writes=[wb_.d[0]])
                    else:
                        S.dma("sp", wb_[:, 0:nk, :], cd[:, 0:nk, :], reads=[cdep_], writes=[wb_.d[0]])
                    return wb_
                wb_ = _load_cast(src, fr, br, nk)
                if key is not None:
                    cd = nc.dram_tensor("wc_%d_%s" % (l, key), shp, BF16, kind="Internal").ap()
                    cdep_ = Dep()
                    if nk is None:
                        S.dma("pool", cd, wb_[:], reads=[wb_.d[0]], writes=[cdep_])
                    else:
                        S.dma("pool", cd[:, 0:nk, :], wb_[:, 0:nk, :], reads=[wb_.d[0]], writes=[cdep_])
                    wcache[key] = (cd, cdep_)
                return wb_

            def _load_cast(src, fr, br, nk=None):
                wf_ = fr.next()
                wb_ = br.next()
                if nk is None:
                    S.dma("sp", wf_[:], src, writes=[wf_.d[0]])
                    ec[0] += 1
                    evac(ec[0], wb_[:], wf_[:], [wf_.d[0]], [wb_.d[0]])
                else:
                    S.dma("sp", wf_[:, 0:nk, :], src, writes=[wf_.d[0]])
                    ec[0] += 1
                    evac(ec[0], wb_[:, 0:nk, :], wf_[:, 0:nk, :], [wf_.d[0]], [wb_.d[0]])
                return wb_

            def headnorm_gate(blk, src2, nw_ap, gate_c0, gate_fn, dstT):
                wA = load_cast(win_src[:, :, gate_c0:gate_c0 + 256], wgf, wgb, key="in%d" % gate_c0)
                wB = load_cast(win_src[:, :, gate_c0 + 256:gate_c0 + 512], wgf, wgb, key="in%d" % (gate_c0 + 256))
                for tt in range(4):
                    r0 = blk * 512 + tt * 128
                    ha, hb, sq, gs = tkr.next(), tkr.next(), tkr.next(), tkr.next()
                    S.dma("sp", ha[:], src2[0, r0:r0 + 128, :], writes=[ha.d[0]])
                    S.dma("sp", hb[:], src2[1, r0:r0 + 128, :], writes=[hb.d[0]])
                    S.op("pool", lambda e: e.tensor_tensor(ha[:], ha[:], hb[:], ALU.add), reads=[ha.d[0], hb.d[0]], writes=[ha.d[0]])
                    S.op("pool", lambda e: e.tensor_tensor(sq[:], ha[:], ha[:], ALU.mult), reads=[ha.d[0]], writes=[sq.d[0]])
                    ss = ssr.next()
                    S.op("dve", lambda e: e.tensor_reduce(ss[:], sq[:].rearrange("p (h d) -> p h d", h=4), AX.X, ALU.add),
                         reads=[sq.d[0]], writes=[ss.d[0]])
                    S.op("dve", lambda e: e.tensor_scalar(ss[:], ss[:], 1.0 / 128, EPS, ALU.mult, ALU.add), reads=[ss.d[0]], writes=[ss.d[0]])
                    S.op("act", lambda e: e.activation(ss[:], ss[:], AF.Sqrt), reads=[ss.d[0]], writes=[ss.d[0]])
                    S.op("dve", lambda e: e.reciprocal(ss[:], ss[:]), reads=[ss.d[0]], writes=[ss.d[0]])
                    S.op("dve", lambda e: e.tensor_tensor(
                        ha[:].rearrange("p (h d) -> p h d", h=4), ha[:].rearrange("p (h d) -> p h d", h=4),
                        ss[:].unsqueeze(2).to_broadcast([128, 4, 128]), ALU.mult), reads=[ha.d[0], ss.d[0]], writes=[ha.d[0]])
                    S.op("pool", lambda e: e.tensor_tensor(
                        ha[:].rearrange("p (h d) -> p h d", h=4), ha[:].rearrange("p (h d) -> p h d", h=4), nw_ap, ALU.mult),
                        reads=[ha.d[0], nwm.d[0], nwg.d[0]], writes=[ha.d[0]])
                    pg = psr.next()
                    for half, wX in ((0, wA), (1, wB)):
                        for kt in range(KT):
                            S.op("pe", lambda e, half=half, wX=wX, kt=kt: e.matmul(
                                pg[:, half * 256:(half + 1) * 256], h1[:, kt, tt * 128:(tt + 1) * 128], wX[:, kt, :],
                                start=(kt == 0), stop=(kt == KT - 1)), reads=[h1.d[0], wX.d[0]], writes=[pg.d[0]])
                    S.op("act", lambda e: e.activation(gs[:], pg[:, :], gate_fn), reads=[pg.d[0]], writes=[gs.d[0]])
                    S.op("dve", lambda e: e.tensor_tensor(ha[:], ha[:], gs[:], ALU.mult), reads=[ha.d[0], gs.d[0]], writes=[ha.d[0]])
                    pt = psr.next()
                    for ct in range(4):
                        S.op("pe", lambda e, ct=ct: e.transpose(pt[:, ct * 128:(ct + 1) * 128], ha[:, ct * 128:(ct + 1) * 128], ident),
                             reads=[ha.d[0], cdep], writes=[pt.d[0]])
                    ec[0] += 1
                    evac(ec[0], dstT[:, :, tt * 128:(tt + 1) * 128], pt[:, :].rearrange("p (c t) -> p c t", c=4), [pt.d[0]], [dstT.d[0]])

            st0 = ExitStack()
            wgf = Ring([sb("wgf%d_%d" % (l, i), [128, KT, 256], stack=st0) for i in range(2)])
            wdf = Ring([sb("wdf%d_%d" % (l, i), [128, FT, 128], stack=st0) for i in range(2)])
            for blk in range(NBLK):
                if blk == 1:
                    S.barrier()
                    st0.close()
                    xbr.bufs.append(sb("cxb%d_b" % l, [128, KT, 512], stack=st))
                    tmpr.bufs.append(sb("ctmp%d_b" % l, [128, KT, 512], stack=st))
                xb = xbr.next()
                tmp = tmpr.next()
                h2 = h2r.next()
                aT = actr.next()
                S.dma("sp", xb[:], xT_scr[:, :, blk * 512:(blk + 1) * 512], writes=[xb.d[0]])
                bs = slice(blk * 512, (blk + 1) * 512)
                rmsnorm_block(st, xb, l, 0, blk, tmp, out=h1)
                headnorm_gate(blk, h_ml, nwm[:].rearrange("p (h d) -> p h d", h=4), C_MO, AF.Sigmoid, ysT[0])
                headnorm_gate(blk, o_gd, nwg[:].unsqueeze(1).to_broadcast([128, 4, 128]), C_GZ, AF.Silu, ysT[2])
                y0, y1 = tmp[:, 0:4, :], tmp[:, 4:8, :]
                td = tmp.d[0]
                S.dma("sp", y0, y_s5T[0, :, :, bs], writes=[td])
                S.dma("sp", y1, y_s5T[1, :, :, bs], writes=[td])
                S.dma("sp", s5u[:], s5_uT[:, :, bs], writes=[s5u.d[0]])
                S.op("pool", lambda e: e.tensor_tensor(y0, y0, y1, ALU.add), reads=[td], writes=[td])
                for k4 in range(4):
                    S.op("dve", lambda e, k4=k4: e.scalar_tensor_tensor(
                        tmp[:, k4, :], s5u[:, k4, :], dbt[:, 0, k4:k4 + 1], tmp[:, k4, :], ALU.mult, ALU.add),
                        reads=[td, s5u.d[0], dbt.d[0]], writes=[td])
                S.op("pool", lambda e: e.tensor_tensor(y1, y0, y0, ALU.mult), reads=[td], writes=[td])
                S.op("dve", lambda e: e.tensor_scalar(y1, y1, 0.044715, 1.0, ALU.mult, ALU.add), reads=[td], writes=[td])
                S.op("pool", lambda e: e.tensor_tensor(y1, y1, y0, ALU.mult), reads=[td], writes=[td])
                S.op("act", lambda e: e.activation(y1, y1, AF.Sigmoid, scale=1.5957691216), reads=[td], writes=[td])
                S.op("dve", lambda e: e.tensor_tensor(y0, y0, y1, ALU.mult), reads=[td], writes=[td])
                S.op("act", lambda e: e.copy(gbf[:], y0), reads=[td], writes=[gbf.d[0]])
                gwA = load_cast(glu_src[:, :, 0:256], wgf, wgb, nk=4, key="gluA")
                gwB = load_cast(glu_src[:, :, 256:512], wgf, wgb, nk=4, key="gluB")
                for ct in range(4):
                    gw = gwA if ct < 2 else gwB
                    pq = psr.next()
                    for k4 in range(4):
                        S.op("pe", lambda e, k4=k4, gw=gw, ct=ct: e.matmul(
                            pq[:, :], gw[:, k4, (ct % 2) * 128:(ct % 2 + 1) * 128], gbf[:, k4, :], start=(k4 == 0), stop=(k4 == 3)),
                            reads=[gw.d[0], gbf.d[0]], writes=[pq.d[0]])
                    sg = sgr.next()
                    S.op("act", lambda e, ct=ct, sg=sg, pq=pq: e.activation(sg[:], pq[:, :], AF.Sigmoid, bias=dbt[:, 1, ct:ct + 1]),
                         reads=[pq.d[0], dbt.d[0]], writes=[sg.d[0]])
                    S.op("dve", lambda e, ct=ct, sg=sg: e.tensor_tensor(ysT[1][:, ct, :], tmp[:, ct, :], sg[:], ALU.mult),
                         reads=[td, sg.d[0]], writes=[ysT[1].d[0]])
                for dp in range(4):
                    for n in range(3):
                        wbr = load_cast(wbr_src[n][:, :, dp * 256:(dp + 1) * 256], wgf, wgb, nk=4, key="br%d_%d" % (n, dp))
                        wgn = load_cast(win_src[:, :, C_GATES + n * 1024 + dp * 256:C_GATES + n * 1024 + (dp + 1) * 256], wgf, wgb, key="g%d_%d" % (n, dp))
                        for d2 in range(2):
                            mac = maccs[d2]
                            pP, pG = psr.next(), psr.next()
                            for k4 in range(4):
                                S.op("pe", lambda e, k4=k4, pP=pP: e.matmul(
                                    pP[:, :], wbr[:, k4, d2 * 128:(d2 + 1) * 128], ysT[n][:, k4, :], start=(k4 == 0), stop=(k4 == 3)),
                                    reads=[wbr.d[0], ysT[n].d[0]], writes=[pP.d[0]])
                            for kt in range(KT):
                                S.op("pe", lambda e, kt=kt, pG=pG: e.matmul(
                                    pG[:, :], wgn[:, kt, d2 * 128:(d2 + 1) * 128], h1[:, kt, :], start=(kt == 0), stop=(kt == KT - 1)),
                                    reads=[wgn.d[0], h1.d[0]], writes=[pG.d[0]])
                            sg = sgr.next()
                            S.op("act", lambda e, sg=sg, pG=pG: e.activation(sg[:], pG[:, :], AF.Sigmoid), reads=[pG.d[0]], writes=[sg.d[0]])
                            if n == 0:
                                S.op("dve", lambda e, sg=sg, pP=pP, mac=mac: e.tensor_tensor(mac[:], sg[:], pP[:, :], ALU.mult),
                                     reads=[sg.d[0], pP.d[0]], writes=[mac.d[0]])
                            else:
                                S.op("dve", lambda e, sg=sg, pP=pP: e.tensor_tensor(sg[:], sg[:], pP[:, :], ALU.mult),
                                     reads=[sg.d[0], pP.d[0]], writes=[sg.d[0]])
                                S.op("pool", lambda e, sg=sg, mac=mac: e.tensor_tensor(mac[:], mac[:], sg[:], ALU.add),
                                     reads=[sg.d[0], mac.d[0]], writes=[mac.d[0]])
                    for d2 in range(2):
                        S.op("act", lambda e, d2=d2: e.copy(mg[:, dp * 2 + d2, :], maccs[d2][:]), reads=[maccs[d2].d[0]], writes=[mg.d[0]])
                for oc in range(4):
                    wo = load_cast(wout_src[:, :, oc * 256:(oc + 1) * 256], wgf, wgb, key="wo%d" % oc)
                    for o2 in range(2):
                        ot = oc * 2 + o2
                        pd = psr.next()
                        for kt in range(KT):
                            S.op("pe", lambda e, kt=kt, pd=pd, o2=o2, wo=wo: e.matmul(
                                pd[:, :], wo[:, kt, o2 * 128:(o2 + 1) * 128], mg[:, kt, :], start=(kt == 0), stop=(kt == KT - 1)),
                                reads=[wo.d[0], mg.d[0]], writes=[pd.d[0]])
                        S.op("dve", lambda e, ot=ot, pd=pd: e.scalar_tensor_tensor(
                            xb[:, ot, :], pd[:, :], modT[:, l, 16 + ot:17 + ot], xb[:, ot, :], ALU.mult, ALU.add),
                            reads=[pd.d[0], modT.d[0], xb.d[0]], writes=[xb.d[0]])
                rmsnorm_block(st, xb, l, 1, blk, tmp, out=h2)
                for fc in range(FT // 2):
                    wg = load_cast(wg_src[:, :, fc * 256:(fc + 1) * 256], wgf, wgb, key="fg%d" % fc)
                    wu = load_cast(wu_src[:, :, fc * 256:(fc + 1) * 256], wgf, wgb, key="fu%d" % fc)
                    for f2 in range(2):
                        ft = fc * 2 + f2
                        pgate = psr.next()
                        pup = psr.next()
                        for (pp, ww) in ((pgate, wg), (pup, wu)):
                            for kt in range(KT):
                                S.op("pe", lambda e, kt=kt, pp=pp, ww=ww, f2=f2: e.matmul(
                                    pp[:, :], ww[:, kt, f2 * 128:(f2 + 1) * 128], h2[:, kt, :],
                                    start=(kt == 0), stop=(kt == KT - 1)),
                                    reads=[ww.d[0], h2.d[0]], writes=[pp.d[0]])
                        sg = sgr.next()
                        S.op("act", lambda e, sg=sg, pgate=pgate: e.activation(sg[:], pgate[:, :], AF.Silu),
                             reads=[pgate.d[0]], writes=[sg.d[0]])
                        S.op("dve", lambda e, sg=sg, pup=pup, ft=ft: e.tensor_tensor(aT[:, ft, :], sg[:], pup[:, :], ALU.mult),
                             reads=[sg.d[0], pup.d[0]], writes=[aT.d[0]])
                for oc in range(8):
                    wd = load_cast(wd_src[:, :, oc * 128:(oc + 1) * 128], wdf, wdb, key="fd%d" % oc)
                    for o2 in range(1):
                        ot = oc
                        pd = psr.next()
                        for ft in range(FT):
                            S.op("pe", lambda e, ft=ft, pd=pd, o2=o2: e.matmul(
                                pd[:, :], wd[:, ft, o2 * 128:(o2 + 1) * 128], aT[:, ft, :],
                                start=(ft == 0), stop=(ft == FT - 1)),
                                reads=[wd.d[0], aT.d[0]], writes=[pd.d[0]])
                        S.op("dve", lambda e, ot=ot, pd=pd: e.scalar_tensor_tensor(
                            xb[:, ot, :], pd[:, :], modT[:, l, 40 + ot:41 + ot], xb[:, ot, :], ALU.mult, ALU.add),
                            reads=[pd.d[0], modT.d[0], xb.d[0]], writes=[xb.d[0]])
                if not last:
                    S.dma("pool", xT_scr[:, :, blk * 512:(blk + 1) * 512], xb[:], reads=[xb.d[0]])
                else:
                    yb = tmp
                    rmsnorm_block(st, xb, l, 2, blk, tmp, out=yb)
                    for tt in range(4):
                        for half in range(2):
                            yt = ytr.next()
                            pt = psr.next()
                            for k4 in range(4):
                                kt = half * 4 + k4
                                S.op("pe", lambda e, kt=kt, k4=k4, tt=tt, pt=pt: e.transpose(
                                    pt[:, k4 * 128:(k4 + 1) * 128], yb[:, kt, tt * 128:(tt + 1) * 128], ident),
                                    reads=[yb.d[0], cdep], writes=[pt.d[0]])
                            ec[0] += 1
                            evac(ec[0], yt[:], pt[:, :], [pt.d[0]], [yt.d[0]])
                            r0 = blk * 512 + tt * 128
                            S.dma("pool", y_out[r0:r0 + 128, half * 512:(half + 1) * 512], yt[:], reads=[yt.d[0]])
    S.finish([])
    return nc, S


def _unused():
    pass


def _consts():
    c = np.zeros((128, 8, 128), np.float32)
    i = np.arange(128)
    c[:, 0, :] = np.eye(128)
    c[:, 1, :] = 1.0
    c[:, 2, :] = (i[:, None] <= i[None, :])
    c[:, 3, :] = (i[:, None] >= i[None, :])
    c[:, 4, :] = (i[:, None] < i[None, :])
    c[:, 5, :] = (i[:, None] > i[None, :])
    c[:, 6, :] = (i[:, None] // 32 == i[None, :] // 32)
    c[:, 7, :] = (i[None, :] + 1).astype(np.float32)
    return c


def _pos_embed_T():
    t = np.arange(T)
    quarter = D // 4
    omega = (1.0 / (10000.0 ** (np.arange(quarter, dtype=np.float32) / quarter))).astype(np.float32)

    def enc(pos):
        ang = pos.astype(np.float32)[:, None] * omega[None, :]
        return np.concatenate([np.sin(ang), np.cos(ang)], axis=-1)
    pe = np.concatenate([enc(t // 64), enc(t % 64)], axis=-1).astype(np.float32)
    return np.ascontiguousarray(pe.T.reshape(KT, 128, T).transpose(1, 0, 2))


def _fm(v):
    v = np.asarray(v, np.float32)
    lead = v.shape[:-1]
    return np.ascontiguousarray(np.moveaxis(v.reshape(lead + (KT, 128)), -1, 0))


def _prompt_map():
    m = []
    counts = [6, 6, 5, 5, 5, 5]
    s = 0
    for ci, n in enumerate(counts):
        for j in range(n):
            m.append((2 + ci, j))
            s += 1
    return m


def prep_inputs(inp):
    f = lambda k: np.asarray(inp[k], np.float32)
    pm = _prompt_map()
    shared = {
        "w_branch": f("w_branch"), "w_out": f("w_out"), "glu_w": f("s5_glu_w"),
        "s5DbT": np.ascontiguousarray(np.stack([f("s5_D"), f("s5_glu_b")], axis=1).reshape(DEPTH, 2, 4, 128).transpose(3, 0, 1, 2)),
        "mlnw": np.ascontiguousarray(np.broadcast_to(f("ml_norm_w")[None], (128, DEPTH, 512))),
        "gdnw": np.ascontiguousarray(np.broadcast_to(f("gd_norm_w")[None], (128, DEPTH, 128))),
        "ada_w": f("ada_w"), "w_in": f("w_in"), "w_gate": f("w_gate"), "w_up": f("w_up"), "w_down": f("w_down"),
        "ada_bT": np.ascontiguousarray(f("ada_b").reshape(DEPTH, 48, 128).transpose(2, 0, 1)),
        "n1wT": _fm(f("norm1_w")), "n2wT": _fm(f("norm2_w")), "fnwT": _fm(f("final_norm_w")),
        "consts": _consts(),
    }
    gbias = np.concatenate([f("ml_i_bias").reshape(DEPTH, 8), f("ml_f_bias").reshape(DEPTH, 8),
                            f("gd_dt_bias").reshape(DEPTH, 8), f("gd_A_log").reshape(DEPTH, 8)], axis=1)
    shared["gbias"] = np.ascontiguousarray(np.broadcast_to(gbias[None], (128, DEPTH, 32)))
    cwv = f("gd_conv_w")
    shared["convw"] = np.ascontiguousarray(cwv.reshape(DEPTH, 5, 12, 128).transpose(3, 0, 2, 1))
    lre, lim, lst = f("s5_lam_re"), f("s5_lam_im"), f("s5_log_step")
    lst_e = np.broadcast_to(lst[..., None], lre.shape)
    flat3 = np.stack([lre, lim, lst_e], axis=2).reshape(DEPTH, 2, 3, 2048)
    shared["s5rep"] = np.ascontiguousarray(np.broadcast_to(flat3[None], (128, DEPTH, 2, 3, 2048)))
    shared["s5sm"] = np.ascontiguousarray(flat3.reshape(DEPTH, 2, 3, 16, 128).transpose(4, 0, 1, 2, 3))
    Bt = np.zeros((128, DEPTH, 2, 16, 128), np.float32)
    Ctt = np.zeros((128, DEPTH, 2, 16, 128), np.float32)
    for ri, (Bk, Ck) in enumerate((("s5_B_re", "s5_C_re"), ("s5_B_im", "s5_C_im"))):
        Bv, Cv = f(Bk), f(Ck)
        for g in range(32):
            j, half, gr_ = g // 2, g % 2, g % 8
            Bt[gr_ * 16:(gr_ + 1) * 16, :, ri, j, half * 64:(half + 1) * 64] = Bv[:, g].transpose(2, 0, 1)
            Ctt[half * 64:(half + 1) * 64, :, ri, j, gr_ * 16:(gr_ + 1) * 16] = Cv[:, g].transpose(2, 0, 1)
    shared["s5Bt"], shared["s5Ct"] = Bt, Ctt
    h0r, h0i = f("state_s5_re"), f("state_s5_im")
    posT = _pos_embed_T()
    C0, n0, m0 = f("state_mlstm_C"), f("state_mlstm_n"), f("state_mlstm_m")
    maps = []
    for c in range(8):
        m = dict(shared)
        x = np.zeros((T, D), np.float32)
        if c < 2:
            x[:] = f("x_sample")[c]
            m["cvec"] = _fm(f("c")[c])
            m["keep"] = np.ones((128, 1), np.float32)
            m["posT"] = posT
            caug = np.concatenate([C0[c], n0[c][..., None]], axis=-1)
            m["mlC0"] = np.ascontiguousarray(caug.transpose(3, 0, 1, 2, 4))
            m["m0rep"] = np.ascontiguousarray(np.broadcast_to(m0[c][None], (128, DEPTH, 2, 4)))
            m["m0h"] = np.ascontiguousarray(m0[c].reshape(DEPTH, 2, 4, 1))
            m["gdS0"] = np.ascontiguousarray(f("state_gdn_S")[c].transpose(3, 0, 1, 2, 4))
            hh = np.stack([h0r[c], h0i[c]], axis=2).reshape(DEPTH, 2, 2, 16, 128)
            m["s5h0"] = np.ascontiguousarray(hh.transpose(4, 0, 1, 2, 3))
        else:
            m["mlC0"] = np.zeros((128, DEPTH, 2, 4, 129), np.float32)
            m["m0rep"] = np.zeros((128, DEPTH, 2, 4), np.float32)
            m["m0h"] = np.zeros((DEPTH, 2, 4, 1), np.float32)
            m["gdS0"] = np.zeros((128, DEPTH, 2, 4, 128), np.float32)
            m["s5h0"] = np.zeros((128, DEPTH, 2, 2, 16), np.float32)
            for si, (cc, slot) in enumerate(pm):
                if cc == c:
                    x[slot * SLOT:(slot + 1) * SLOT] = f("x_prompt")[si]
            m["cvec"] = _fm(f("c_ctx"))
            m["keep"] = np.zeros((128, 1), np.float32)
            m["posT"] = np.zeros_like(posT)
        m["x"] = x
        maps.append(m)
    return maps


def kernel(**inputs):
    nc, S = build_program()
    maps = prep_inputs(inputs)
    res = run_bass_kernel_spmd(nc, maps, core_ids=list(range(8)))
    R = res.results
    ys = [np.asarray(r["y"], np.float32) for r in R]
    y_sample = np.stack([ys[0], ys[1]], axis=0)
    nb = 32
    y_prompt = np.zeros((nb, SLOT, D), np.float32)
    o_C = np.zeros((nb, DEPTH, 2, 4, 128, 128), np.float32)
    o_n = np.zeros((nb, DEPTH, 2, 4, 128), np.float32)
    o_m = np.zeros((nb, DEPTH, 2, 4), np.float32)
    o_re = np.zeros((nb, DEPTH, 2, 32, 64), np.float32)
    o_im = np.zeros((nb, DEPTH, 2, 32, 64), np.float32)
    o_S = np.zeros((nb, DEPTH, 2, 4, 128, 128), np.float32)
    for si, (c, slot) in enumerate(_prompt_map()):
        r = R[c]
        y_prompt[si] = ys[c][slot * SLOT:(slot + 1) * SLOT]
        ca = np.asarray(r["st_mlC"])[:, slot]
        o_C[si] = ca[..., :128].transpose(0, 1, 3, 2, 4)
        o_n[si] = ca[..., 128].transpose(0, 1, 3, 2)
        o_m[si] = np.asarray(r["st_mlm"])[:, slot, :, :, 0]
        s5 = np.asarray(r["st_s5"])[:, slot]
        o_re[si] = s5[:, :, :, 0, :].transpose(0, 1, 3, 2).reshape(DEPTH, 2, 32, 64)
        o_im[si] = s5[:, :, :, 1, :].transpose(0, 1, 3, 2).reshape(DEPTH, 2, 32, 64)
        o_S[si] = np.asarray(r["st_gdS"])[:, slot].transpose(0, 1, 3, 2, 4)
    return (y_prompt, y_sample, o_C, o_n, o_m, o_re, o_im, o_S)
```

```python
import math
from contextlib import ExitStack
import numpy as np
import concourse.bass as bass
import concourse.mybir as mybir
from concourse.bass_utils import run_bass_kernel_spmd

F32 = mybir.dt.float32
BF16 = mybir.dt.bfloat16
F32R = mybir.dt.float32r


def f32r(ap):
    return ap
AF = mybir.ActivationFunctionType
ALU = mybir.AluOpType
AX = mybir.AxisListType

D = 1024
KT = 8
T = 4096
NSLOT = 16
SLOT = 256
NTT = 32
NBLK = 8
DEPTH = 2
D_IN = 7712
D_FF = 2816
FT = 22
EPS = 1e-6
C_MQ, C_MK, C_MV, C_MO, C_MI, C_MF, C_SU, C_GQKV, C_GZ, C_GA, C_GB, C_GATES = (
    0, 512, 1024, 1536, 2048, 2056, 2064, 2576, 4112, 4624, 4632, 4640)


class Dep:
    __slots__ = ("w", "r")

    def __init__(self):
        self.w = None
        self.r = set()


class _Rec:
    def __init__(self):
        self.call = None

    def __getattr__(self, name):
        def f(*a, **k):
            self.call = (name, a, k)
            return self
        return f


def _ap_free_elems(ap):
    try:
        sh = tuple(ap.shape)
        n = 1
        for v in sh[1:]:
            n *= int(v)
        return max(n, 1)
    except Exception:
        return 256


class Sched:
    KDMA = 32
    LAT = 3.0

    def __init__(self, nc, es):
        self.nc = nc
        self.engs = {"pe": nc.tensor, "act": nc.scalar, "dve": nc.vector, "pool": nc.gpsimd, "sp": nc.sync}
        self.sem = {k: es.enter_context(nc.semaphore("sem_" + k)) for k in self.engs}
        self.cnt = {k: 0 for k in self.engs}
        self.waited = {k: {} for k in self.engs}
        self.dsem = {q: [es.enter_context(nc.semaphore("dsem_%s%d" % (q, i))) for i in range(self.KDMA)]
                     for q in ("sp", "pool")}
        self.dn = {q: 0 for q in self.dsem}
        self.nwait = 0
        self.ops = []
        self.base = 0
        self.ev = {}
        self.free_t = {k: 0.0 for k in self.engs}

    def _deps(self, reads, writes):
        deps = set()
        for d in reads:
            if d.w is not None:
                deps.add(d.w)
        for d in writes:
            if d.w is not None:
                deps.add(d.w)
            deps.update(d.r)
        return deps

    def _record(self, oid, reads, writes):
        for d in writes:
            d.w = oid
            d.r = set()
        for d in reads:
            d.r.add(oid)

    def op(self, eng, fn, reads=(), writes=()):
        rec = _Rec()
        fn(rec)
        name, a, k = rec.call
        out = k.get("out", a[0] if a else None)
        n = _ap_free_elems(out) if out is not None else 256
        if eng == "pe":
            lhs = a[1] if len(a) > 1 else None
            slow = 4.0 if (lhs is not None and getattr(lhs, "dtype", None) == F32) else 1.0
            cost = 0.10 + slow * n / 1400.0
        elif eng == "act":
            cost = 0.25 + n / 1200.0
        elif eng == "dve":
            cost = 0.12 + n / (480.0 if name == "tensor_tensor_scan" else 960.0)
        else:
            cost = 0.3 + n / 480.0
        oid = self.base + len(self.ops)
        self.ops.append(dict(eng=eng, call=rec.call, deps=self._deps(reads, writes), cost=cost, lat=cost, dma=False))
        self._record(oid, reads, writes)

    def dma(self, q, out, in_, reads=(), writes=()):
        n = _ap_free_elems(out)
        oid = self.base + len(self.ops)
        self.ops.append(dict(eng=q, call=(out, in_), deps=self._deps(reads, writes), cost=0.08, lat=2.0 + n / 400.0, dma=True))
        self._record(oid, reads, writes)

    def _semh(self, key):
        if isinstance(key, tuple):
            return self.dsem[key[1]][key[2]]
        return self.sem[key]

    def _wait(self, eng, deps):
        e = self.engs[eng]
        best = {}
        for (k, v) in deps:
            if best.get(k, 0) < v:
                best[k] = v
        for k, v in best.items():
            if k == "pe" and eng == "pe":
                continue
            if self.waited[eng].get(k, 0) < v:
                e.wait_ge(self._semh(k), v)
                self.waited[eng][k] = v
                self.nwait += 1

    def flush(self):
        import heapq
        ops = self.ops
        n = len(ops)
        if n == 0:
            return
        base = self.base
        succ = [[] for _ in range(n)]
        indeg = [0] * n
        rt = [0.0] * n
        for i, o in enumerate(ops):
            for d in o["deps"]:
                if d >= base:
                    succ[d - base].append(i)
                    indeg[i] += 1
        heap = [(0.0, i) for i in range(n) if indeg[i] == 0]
        heapq.heapify(heap)
        free_t = {k: 0.0 for k in self.engs}
        emitted = 0
        while heap:
            t, i = heapq.heappop(heap)
            o = ops[i]
            eng = o["eng"]
            start = max(t, free_t[eng])
            free_t[eng] = start + o["cost"]
            fin = start + o["lat"]
            self._emit(base + i, o)
            emitted += 1
            for s_ in succ[i]:
                lat = 0.05 if (ops[s_]["eng"] == eng and not o["dma"]) else self.LAT
                if rt[s_] < fin + lat:
                    rt[s_] = fin + lat
                indeg[s_] -= 1
                if indeg[s_] == 0:
                    heapq.heappush(heap, (rt[s_], s_))
        assert emitted == n, "dependency cycle in scheduler"
        self.base += n
        self.ops = []

    def _emit(self, oid, o):
        eng = o["eng"]
        deps = [self.ev[d] for d in o["deps"]]
        if o["dma"]:
            idx = self.dn[eng]
            slot = idx % self.KDMA
            key = ("dma", eng, slot)
            if idx >= self.KDMA:
                deps.append((key, 16 * (idx // self.KDMA)))
            self._wait(eng, deps)
            out, in_ = o["call"]
            inst = self.engs[eng].dma_start(out=out, in_=in_)
            inst.then_inc(self.dsem[eng][slot], 16)
            self.dn[eng] += 1
            self.ev[oid] = (key, 16 * (idx // self.KDMA + 1))
        else:
            self._wait(eng, deps)
            name, a, k = o["call"]
            inst = getattr(self.engs[eng], name)(*a, **k)
            self.cnt[eng] += 1
            inst.then_inc(self.sem[eng], 1)
            self.ev[oid] = (eng, self.cnt[eng])

    def _all_done_deps(self):
        deps = [(k, v) for k, v in self.cnt.items() if v > 0]
        for q in self.dsem:
            for slot in range(self.KDMA):
                n = (self.dn[q] - slot + self.KDMA - 1) // self.KDMA if self.dn[q] > slot else 0
                if n > 0:
                    deps.append((("dma", q, slot), 16 * n))
        return deps

    def barrier(self):
        self.flush()
        deps = self._all_done_deps()
        for eng in self.engs:
            self._wait(eng, [d for d in deps if d[0] != eng])

    def finish(self, deps_list):
        self.flush()
        self._wait("sp", self._all_done_deps())


class Buf:
    def __init__(self, t, n=1):
        self.t = t
        self.d = [Dep() for _ in range(n)]

    def __getitem__(self, k):
        return self.t[k]


class Ring:
    def __init__(self, bufs):
        self.bufs = bufs
        self.i = 0

    def next(self):
        b = self.bufs[self.i % len(self.bufs)]
        self.i += 1
        return b


def build_program(dbg=None, nlayers=DEPTH):
    nc = bass.Bass("TRN2", target_bir_lowering=False)
    es = ExitStack()
    S = Sched(nc, es)

    def din(name, shape, dt=F32):
        return nc.dram_tensor(name, list(shape), dt, kind="ExternalInput").ap()

    def dout(name, shape, dt=F32):
        return nc.dram_tensor(name, list(shape), dt, kind="ExternalOutput").ap()

    def dscr(name, shape, dt=F32):
        kind = "ExternalOutput" if (dbg and name in dbg) else "Internal"
        return nc.dram_tensor(name, list(shape), dt, kind=kind).ap()

    def sb(name, shape, dt=F32, n=1, stack=None):
        t = (stack or es).enter_context(nc.sbuf_tensor("s_" + name, list(shape), dt))
        return Buf(t, n)

    x_in = din("x", [T, D])
    pos_in = din("posT", [128, KT, T])
    cvec = din("cvec", [128, KT])
    keep_in = din("keep", [128, 1])
    ada_w = din("ada_w", [DEPTH, D, 6 * D])
    ada_bT = din("ada_bT", [128, DEPTH, 48])
    n1wT = din("n1wT", [128, DEPTH, KT])
    n2wT = din("n2wT", [128, DEPTH, KT])
    fnwT = din("fnwT", [128, KT])
    w_in = din("w_in", [DEPTH, D, D_IN])
    gbias = din("gbias", [128, DEPTH, 32])
    convw = din("convw", [128, DEPTH, 12, 5])
    w_branch = din("w_branch", [DEPTH, 3, 512, D])
    w_out = din("w_out", [DEPTH, D, D])
    glu_w = din("glu_w", [DEPTH, 512, 512])
    s5DbT = din("s5DbT", [128, DEPTH, 2, 4])
    mlnw = din("mlnw", [128, DEPTH, 512])
    gdnw = din("gdnw", [128, DEPTH, 128])
    w_gate = din("w_gate", [DEPTH, D, D_FF])
    w_up = din("w_up", [DEPTH, D, D_FF])
    w_down = din("w_down", [DEPTH, D_FF, D])
    consts = din("consts", [128, 8, 128])
    y_out = dout("y", [T, D])

    xT_scr = dscr("xT_scr", [128, KT, T])
    ml_qT = dscr("ml_qT", [128, 4, T])
    ml_kT = dscr("ml_kT", [128, 4, T])
    ml_k = dscr("ml_k", [T, 512])
    ml_v = dscr("ml_v", [T, 4, 129])
    s5_uT = dscr("s5_uT", [128, 4, T])
    gd_qT = dscr("gd_qT", [128, 4, T])
    gd_kT = dscr("gd_kT", [128, 4, T])
    gd_q = dscr("gd_q", [T, 512])
    gd_k = dscr("gd_k", [T, 512])
    gd_v = dscr("gd_v", [T, 512])
    h_ml = dscr("h_ml", [2, T, 512])
    o_gd = dscr("o_gd", [2, T, 512])
    y_s5T = dscr("y_s5T", [2, 128, 4, T])
    mlC0 = din("mlC0", [128, DEPTH, 2, 4, 129])
    m0rep = din("m0rep", [128, DEPTH, 2, 4])
    m0h = din("m0h", [DEPTH, 2, 4, 1])
    s5rep = din("s5rep", [128, DEPTH, 2, 3, 2048])
    s5sm = din("s5sm", [128, DEPTH, 2, 3, 16])
    s5Bt = din("s5Bt", [128, DEPTH, 2, 16, 128])
    s5Ct = din("s5Ct", [128, DEPTH, 2, 16, 128])
    s5h0 = din("s5h0", [128, DEPTH, 2, 2, 16])
    st_s5 = dout("st_s5", [DEPTH, NSLOT, 2, 128, 2, 16])
    gdS0 = din("gdS0", [128, DEPTH, 2, 4, 128])
    st_gdS = dout("st_gdS", [DEPTH, NSLOT, 2, 128, 4, 128])
    st_mlC = dout("st_mlC", [DEPTH, NSLOT, 2, 128, 4, 129])
    st_mlm = dout("st_mlm", [DEPTH, NSLOT, 2, 4, 1])
    gates_dbg = dscr("gates_dbg", [128, NTT, 32]) if (dbg and "gates_dbg" in dbg) else None
    hT_dbg = dscr("hT_dbg", [128, KT, T], BF16) if (dbg and "hT_dbg" in dbg) else None

    cst = sb("cst", [128, 8, 128])
    ident = cst[:, 0, :]
    ones = cst[:, 1, :]
    keep = sb("keep", [128, 1])
    modT = sb("modT", [128, DEPTH, 48])
    cv = sb("cv", [128, KT])
    n1w = sb("n1w", [128, DEPTH, KT])
    n2w = sb("n2w", [128, DEPTH, KT])
    fnw = sb("fnw", [128, KT])
    gb = sb("gb", [128, DEPTH, 32])
    cw = sb("cw", [128, DEPTH, 12, 5])
    w1 = sb("w1", [128, DEPTH, 2, KT])
    negA = sb("negA", [128, DEPTH, 8])
    epsb = sb("epsb", [128, 1])
    onesb = sb("onesb", [128, 128], BF16)
    hTh = [None]
    gates = sb("gates", [128, NTT, 32], n=1)

    ps = [Buf(es.enter_context(nc.psum_tensor("ps%d" % i, [128, 512], F32))) for i in range(8)]
    psr = Ring(ps)

    cdep = cst.d[0]
    S.dma("sp", cst[:], consts, writes=[cdep])
    for (b_, src) in ((keep, keep_in), (cv, cvec), (n1w, n1wT), (n2w, n2wT), (fnw, fnwT), (gb, gbias),
                      (cw, convw)):
        S.dma("sp", b_[:], src, writes=[b_.d[0]])

    S.op("dve", lambda e: e.memset(epsb[:], EPS), writes=[epsb.d[0]])
    S.op("dve", lambda e: e.memset(onesb[:], 1.0), writes=[onesb.d[0]])
    S.op("act", lambda e: e.activation(negA[:], gb[:, :, 24:32], AF.Exp), reads=[gb.d[0]], writes=[negA.d[0]])
    S.op("dve", lambda e: e.tensor_scalar(negA[:], negA[:], -1.0, None, ALU.mult), reads=[negA.d[0]], writes=[negA.d[0]])
    with ExitStack() as st:
        sc = sb("silu_c", [128, KT], stack=st)
        S.op("act", lambda e: e.activation(sc[:], cv[:], AF.Silu), reads=[cv.d[0]], writes=[sc.d[0]])
        abT = sb("abT", [128, DEPTH, 48], stack=st)
        S.dma("sp", abT[:], ada_bT, writes=[abT.d[0]])
        awr = Ring([sb("aw%d" % i, [128, KT, 512], stack=st) for i in range(2)])
        for l in range(DEPTH):
            pm = psr.next()
            for cg in range(12):
                aw = awr.next()
                S.dma("sp", aw[:], ada_w[l].rearrange("(kt p) c -> p kt c", p=128)[:, :, cg * 512:(cg + 1) * 512],
                      writes=[aw.d[0]])
                for c4 in range(4):
                    j = cg * 4 + c4
                    for kt in range(KT):
                        S.op("pe", lambda e, aw=aw, c4=c4, kt=kt, j=j, pm=pm: e.matmul(
                            pm[:, j:j + 1], aw[:, kt, c4 * 128:(c4 + 1) * 128], sc[:, kt:kt + 1],
                            start=(kt == 0), stop=(kt == KT - 1)),
                            reads=[aw.d[0], sc.d[0]], writes=[pm.d[0]])
            S.op("dve", lambda e, l=l, pm=pm: e.tensor_tensor(modT[:, l, :], pm[:, 0:48], abT[:, l, :], ALU.add),
                 reads=[pm.d[0], abT.d[0]], writes=[modT.d[0]])
            for which, nw, m in ((0, n1w, 1), (1, n2w, 4)):
                S.op("dve", lambda e, l=l, which=which, nw=nw, m=m: e.scalar_tensor_tensor(
                    w1[:, l, which, :], modT[:, l, m * 8:(m + 1) * 8], 1.0, nw[:, l, :], ALU.add, ALU.mult),
                    reads=[modT.d[0], nw.d[0]], writes=[w1.d[0]])

    S.barrier()

    def evac(i, out, in_, reads, writes, scale=None):
        if i % 2 == 0:
            if scale is None:
                S.op("act", lambda e: e.copy(out, in_), reads=reads, writes=writes)
            else:
                S.op("act", lambda e: e.mul(out, in_, scale), reads=reads, writes=writes)
        else:
            if scale is None:
                S.op("dve", lambda e: e.tensor_copy(out, in_), reads=reads, writes=writes)
            else:
                S.op("dve", lambda e: e.tensor_scalar(out, in_, scale, None, ALU.mult), reads=reads, writes=writes)

    def rmsnorm_block(st, xb, l, which, blk, tmp, out=None):
        hT = hTh[0]
        sh_m = 0 if which == 0 else 3
        sq = tmp
        sqb = tmp[:].bitcast(BF16)[:, :, 0:512]
        S.op("act", lambda e: e.activation(sqb, xb[:], AF.Square), reads=[xb.d[0]], writes=[sq.d[0]])
        pq = psr.next()
        for kt in range(KT):
            S.op("pe", lambda e, kt=kt: e.matmul(pq[:, :], onesb[:], sqb[:, kt, :], start=(kt == 0), stop=(kt == KT - 1)),
                 reads=[sq.d[0], onesb.d[0]], writes=[pq.d[0]])
        rs = rs_ring.next()
        S.op("dve", lambda e: e.tensor_scalar(rs[:], pq[:, :], 1.0 / D, EPS, ALU.mult, ALU.add),
             reads=[pq.d[0]], writes=[rs.d[0]])
        S.op("act", lambda e: e.activation(rs[:], rs[:], AF.Sqrt), reads=[rs.d[0]], writes=[rs.d[0]])
        S.op("dve", lambda e: e.reciprocal(rs[:], rs[:]), reads=[rs.d[0]], writes=[rs.d[0]])
        S.op("dve", lambda e: e.tensor_tensor(sq[:], xb[:], rs[:].unsqueeze(1).to_broadcast([128, KT, 512]), ALU.mult),
             reads=[xb.d[0], rs.d[0]], writes=[sq.d[0]])
        for kt in range(KT):
            if which == 2:
                S.op("act", lambda e, kt=kt: e.mul(out[:, kt, :], sq[:, kt, :], fnw[:, kt:kt + 1]),
                     reads=[sq.d[0], fnw.d[0]], writes=[out.d[0]])
                continue
            dst = hT[:, kt, blk * 512:(blk + 1) * 512] if out is None else out[:, kt, :]
            S.op("act", lambda e, kt=kt, dst=dst: e.activation(
                dst, sq[:, kt, :], AF.Identity,
                bias=modT[:, l, sh_m * 8 + kt:sh_m * 8 + kt + 1], scale=w1[:, l, which, kt:kt + 1]),
                reads=[sq.d[0], modT.d[0], w1.d[0]], writes=[hT.d[blk] if out is None else out.d[0]])

    rs_ring = Ring([sb("rs%d" % i, [128, 512]) for i in range(2)])


    def drive_window(items, width):
        active = []
        items = iter(items)
        done = False
        while True:
            while not done and len(active) < width:
                try:
                    active.append(next(items))
                except StopIteration:
                    done = True
            if not active:
                break
            for g in list(active):
                try:
                    next(g)
                except StopIteration:
                    active.remove(g)

    def drive(gens):
        gens = list(gens)
        while gens:
            for g in list(gens):
                try:
                    next(g)
                except StopIteration:
                    gens.remove(g)

    def phase_mlstm(l):
        psr = Ring(ps[4:8])
        with ExitStack() as st:
            def R(name, shape, n, dt=F32):
                return Ring([sb("%s%d_%d" % (name, l, i), shape, dt, stack=st) for i in range(n)])
            Cst = [sb("mC%d_%d" % (l, d), [128, 4, 129], stack=st) for d in range(2)]
            mst = [sb("mm%d_%d" % (l, d), [4, 1], stack=st) for d in range(2)]
            em0 = sb("em0_%d" % l, [128, 2, 4], stack=st)
            S.dma("sp", em0[:], m0rep[:, l, :, :], writes=[em0.d[0]])
            S.op("act", lambda e: e.activation(em0[:], em0[:], AF.Exp), reads=[em0.d[0]], writes=[em0.d[0]])
            for d in range(2):
                S.dma("sp", Cst[d][:], mlC0[:, l, d, :, :], writes=[Cst[d].d[0]])
                S.op("dve", lambda e, d=d: e.tensor_tensor(
                    Cst[d][:], Cst[d][:], em0[:, d, :].unsqueeze(2).to_broadcast([128, 4, 129]), ALU.mult),
                    reads=[Cst[d].d[0], em0.d[0]], writes=[Cst[d].d[0]])
                S.dma("sp", mst[d][:], m0h[l, d], writes=[mst[d].d[0]])
            qTr, kTr, ktr = R("mqT", [128, 4, 128], 2), R("mkT", [128, 4, 128], 2), R("mkt", [128, 4, 128], 2)
            var = R("mva", [128, 4, 129], 2)
            smr, dgr, Er = R("msm", [128, 28], 3), R("mdg", [128, 4, 128], 2), R("mE", [128, 4, 128], 2)
            STr, tmr, nmr, dnr, kwr = (R("mST", [128, 128], 2), R("mtm", [128, 129], 2), R("mnm", [128, 129], 2),
                                       R("mdn", [128, 1], 3), R("mkw", [128, 128], 2))
            hor, smallr, cor = R("mho", [128, 4, 128], 2), R("msml", [128, 8], 4), R("mco", [128, 4, 129], 1)
            for it in range(NTT):
                def body(d, it=it):
                    c = it if d == 0 else NTT - 1 - it
                    first = (c % 2 == 0) if d == 0 else (c % 2 == 1)
                    slot = c // 2
                    t0 = c * 128
                    Cd, md = Cst[d], mst[d]
                    if first and it > 0:
                        S.op("dve", lambda e, Cd=Cd: e.tensor_scalar(Cd[:], Cd[:], keep[:, 0:1], None, ALU.mult),
                             reads=[Cd.d[0], keep.d[0]], writes=[Cd.d[0]])
                        S.op("dve", lambda e, md=md: e.tensor_scalar(md[:], md[:], keep[0:4, 0:1], None, ALU.mult),
                             reads=[md.d[0], keep.d[0]], writes=[md.d[0]])
                    qT, kT, kt_, va = qTr.next(), kTr.next(), ktr.next(), var.next()
                    S.dma("sp", qT[:], ml_qT[:, :, t0:t0 + 128], writes=[qT.d[0]])
                    S.dma("sp", kT[:], ml_kT[:, :, t0:t0 + 128], writes=[kT.d[0]])
                    S.dma("sp", kt_[:], ml_k[t0:t0 + 128, :].rearrange("p (h d) -> p h d", h=4), writes=[kt_.d[0]])
                    S.dma("sp", va[:], ml_v[t0:t0 + 128, :, :], writes=[va.d[0]])
                    yield
                    gi = gates[:, c, d * 4:(d + 1) * 4]
                    gl = gates[:, c, 8 + d * 4:8 + (d + 1) * 4]
                    gdp = gates.d[0]
                    Md = cst[:, 2 + d, :]
                    pcs = psr.next()
                    S.op("pe", lambda e: e.matmul(pcs[:, 0:4], Md, gl, start=True, stop=True), reads=[cdep, gdp], writes=[pcs.d[0]])
                    S.op("pe", lambda e: e.matmul(pcs[:, 4:8], ones, gl, start=True, stop=True), reads=[cdep, gdp], writes=[pcs.d[0]])
                    yield
                    sm = smr.next()
                    sd = sm.d[0]
                    S.op("dve", lambda e: e.tensor_copy(sm[:, 0:8], pcs[:, 0:8]), reads=[pcs.d[0]], writes=[sd])
                    S.op("act", lambda e: e.activation(sm[:, 8:12], sm[:, 0:4], AF.Exp), reads=[sd], writes=[sd])
                    S.op("act", lambda e: e.activation(sm[:, 12:16], gi, AF.Exp), reads=[gdp], writes=[sd])
                    S.op("dve", lambda e: e.tensor_tensor(sm[:, 16:20], sm[:, 4:8], sm[:, 0:4], ALU.subtract), reads=[sd], writes=[sd])
                    S.op("dve", lambda e: e.tensor_tensor(sm[:, 16:20], sm[:, 16:20], gi, ALU.add), reads=[sd, gdp], writes=[sd])
                    S.op("act", lambda e: e.activation(sm[:, 20:24], sm[:, 16:20], AF.Exp), reads=[sd], writes=[sd])
                    S.op("act", lambda e: e.activation(sm[:, 24:28], sm[:, 4:8], AF.Exp), reads=[sd], writes=[sd])
                    yield
                    dg = dgr.next()
                    for h in range(4):
                        S.op("act", lambda e, h=h: e.mul(dg[:, h, :], ident, sm[:, h:h + 1]), reads=[cdep, sd], writes=[dg.d[0]])
                    pbc = psr.next()
                    S.op("pe", lambda e: e.matmul(pbc[:, :], ones, dg[:].rearrange("p h t -> p (h t)"), start=True, stop=True),
                         reads=[cdep, dg.d[0]], writes=[pbc.d[0]])
                    yield
                    E = Er.next()
                    for h in range(4):
                        S.op("act", lambda e, h=h: e.activation(E[:, h, :], pbc[:, h * 128:(h + 1) * 128], AF.Abs,
                                                                bias=sm[:, h:h + 1], scale=-1.0),
                             reads=[pbc.d[0], sd], writes=[E.d[0]])
                    S.op("act", lambda e: e.activation(E[:], E[:], AF.Exp, scale=-1.0), reads=[E.d[0]], writes=[E.d[0]])
                    S.op("pool", lambda e: e.tensor_tensor(E[:], E[:], cst[:, 2 + d:3 + d, :].to_broadcast([128, 4, 128]), ALU.mult),
                         reads=[E.d[0], cdep], writes=[E.d[0]])
                    yield
                    ho = hor.next()
                    for h in range(4):
                        yield
                        pst = psr.next()
                        S.op("pe", lambda e: e.matmul(pst[:, 0:128], f32r(kT[:, h, :]), f32r(qT[:, h, :]), start=True, stop=True),
                             reads=[kT.d[0], qT.d[0]], writes=[pst.d[0]])
                        yield
                        ST = STr.next()
                        S.op("dve", lambda e: e.scalar_tensor_tensor(ST[:], pst[:, 0:128], sm[:, 12 + h:13 + h], E[:, h, :],
                                                                     ALU.mult, ALU.mult),
                             reads=[pst.d[0], sd, E.d[0]], writes=[ST.d[0]])
                        pn = psr.next()
                        S.op("pe", lambda e: e.matmul(pn[:, 0:129], f32r(ST[:]), f32r(va[:, h, :]), start=True, stop=True),
                             reads=[ST.d[0], va.d[0]], writes=[pn.d[0]])
                        S.op("pe", lambda e: e.matmul(pn[:, 256:385], f32r(qT[:, h, :]), f32r(Cd[:, h, :]), start=True, stop=True),
                             reads=[qT.d[0], Cd.d[0]], writes=[pn.d[0]])
                        yield
                        tm, nm, dn = tmr.next(), nmr.next(), dnr.next()
                        S.op("act", lambda e: e.mul(tm[:], pn[:, 256:385], sm[:, 8 + h:9 + h]), reads=[pn.d[0], sd], writes=[tm.d[0]])
                        S.op("dve", lambda e: e.tensor_tensor(nm[:], tm[:], pn[:, 0:129], ALU.add),
                             reads=[tm.d[0], pn.d[0]], writes=[nm.d[0]])
                        S.op("act", lambda e: e.activation(dn[:], nm[:, 128:129], AF.Abs), reads=[nm.d[0]], writes=[dn.d[0]])
                        S.op("dve", lambda e: e.tensor_scalar(dn[:], dn[:], 1.0, None, ALU.max), reads=[dn.d[0]], writes=[dn.d[0]])
                        S.op("dve", lambda e: e.reciprocal(dn[:], dn[:]), reads=[dn.d[0]], writes=[dn.d[0]])
                        S.op("act", lambda e: e.mul(ho[:, h, :], nm[:, 0:128], dn[:, 0:1]), reads=[nm.d[0], dn.d[0]], writes=[ho.d[0]])
                        yield
                        kw = kwr.next()
                        S.op("act", lambda e: e.mul(kw[:], kt_[:, h, :], sm[:, 20 + h:21 + h]),
                             reads=[kt_.d[0], sd], writes=[kw.d[0]])
                        pu = psr.next()
                        S.op("pe", lambda e: e.matmul(pu[:, 0:129], f32r(kw[:]), f32r(va[:, h, :]), start=True, stop=True),
                             reads=[kw.d[0], va.d[0]], writes=[pu.d[0]])
                        S.op("dve", lambda e: e.scalar_tensor_tensor(Cd[:, h, :], Cd[:, h, :], sm[:, 24 + h:25 + h], pu[:, 0:129],
                                                                     ALU.mult, ALU.add),
                             reads=[Cd.d[0], sd, pu.d[0]], writes=[Cd.d[0]])
                    S.dma("sp", h_ml[d, t0:t0 + 128, :].rearrange("p (h d) -> p h d", h=4), ho[:], reads=[ho.d[0]])
                    yield
                    ptm = psr.next()
                    S.op("pe", lambda e: e.transpose(ptm[0:4, 0:128], sm[:, 16:20], ident), reads=[sd, cdep], writes=[ptm.d[0]])
                    S.op("pe", lambda e: e.transpose(ptm[0:4, 128:256], sm[:, 4:8], ident), reads=[sd, cdep], writes=[ptm.d[0]])
                    yield
                    sl = smallr.next()
                    S.op("dve", lambda e: e.tensor_reduce(sl[0:4, 0:1], ptm[0:4, 0:128], AX.X, ALU.max), reads=[ptm.d[0]], writes=[sl.d[0]])
                    S.op("dve", lambda e: e.tensor_tensor(sl[0:4, 1:2], ptm[0:4, 128:129], md[:], ALU.add),
                         reads=[ptm.d[0], md.d[0]], writes=[sl.d[0]])
                    S.op("dve", lambda e: e.tensor_tensor(md[:], sl[0:4, 0:1], sl[0:4, 1:2], ALU.max), reads=[sl.d[0]], writes=[md.d[0]])
                    if not first:
                        S.op("act", lambda e: e.activation(sl[0:4, 2:3], md[:], AF.Exp, scale=-1.0), reads=[md.d[0]], writes=[sl.d[0]])
                        S.op("dve", lambda e: e.tensor_scalar(sl[0:4, 4:8], cst[0:4, 0, 0:4], sl[0:4, 2:3], None, ALU.mult),
                             reads=[sl.d[0], cdep], writes=[sl.d[0]])
                        pb = psr.next()
                        S.op("pe", lambda e: e.matmul(pb[:, 0:4], cst[0:4, 1, :], sl[0:4, 4:8], start=True, stop=True),
                             reads=[sl.d[0], cdep], writes=[pb.d[0]])
                        sl2 = smallr.next()
                        S.op("dve", lambda e: e.tensor_copy(sl2[:, 0:4], pb[:, 0:4]), reads=[pb.d[0]], writes=[sl2.d[0]])
                        co = cor.next()
                        S.op("dve", lambda e: e.tensor_tensor(co[:], Cd[:], sl2[:, 0:4].unsqueeze(2).to_broadcast([128, 4, 129]), ALU.mult),
                             reads=[Cd.d[0], sl2.d[0]], writes=[co.d[0]])
                        S.dma("sp", st_mlC[l, slot, d], co[:], reads=[co.d[0]])
                        S.dma("sp", st_mlm[l, slot, d], md[:], reads=[md.d[0]])
                yield [body(0), body(1)]

    def phase_gdn(l):
        with ExitStack() as st:
            cnt = [0]

            def T4(name, n=2):
                cnt[0] += 1
                return Ring([sb("g%s%d_%d_%d" % (name, l, cnt[0], i), [128, 4, 128], stack=st) for i in range(n)])
            Sst = [sb("gS%d_%d" % (l, d), [128, 4, 128], stack=st) for d in range(2)]
            for d in range(2):
                s0 = sb("gS0_%d_%d" % (l, d), [128, 4, 128], stack=st)
                S.dma("sp", s0[:], gdS0[:, l, d, :, :], writes=[s0.d[0]])
                S.op("dve", lambda e, d=d, s0=s0: e.tensor_copy(Sst[d][:].bitcast(F32R), s0[:]), reads=[s0.d[0]], writes=[Sst[d].d[0]])
            names = ["qTr", "kTr", "qT", "kT", "qt", "kt", "vt", "dg", "E", "EUi", "EUs", "ELs", "kb", "kbg", "vb", "qg", "kd", "kbT", "qgT",
                     "NT", "N", "aT", "AT", "Em", "wTn", "vn", "oo", "tmpS"]
            rg = {n: T4(n) for n in names}
            for k in (1, 2, 4, 8, 16):
                rg["X%d" % k] = T4("X%d" % k)
                rg["Y%d" % k] = T4("Y%d" % k)
            rg["P"] = T4("P", 4)
            rg["Q"] = T4("Q", 4)
            smr = Ring([sb("gsm%d_%d" % (l, i), [128, 24], stack=st) for i in range(3)])
            identb = cst[:, 0:1, :].to_broadcast([128, 4, 128])

            def flat(b):
                return b[:].rearrange("p h t -> p (h t)")

            def v4(pb):
                return pb[:, :].rearrange("p (h t) -> p h t", h=4)

            def mm4(lhs, rhs, rl, rr, red=False):
                pq = psr.next()
                for h in range(4):
                    a_, b_ = lhs[:, h, :], rhs[:, h, :]
                    if red:
                        a_, b_ = a_.bitcast(F32R), b_.bitcast(F32R)
                    S.op("pe", lambda e, h=h, a_=a_, b_=b_: e.matmul(pq[:, h * 128:(h + 1) * 128], a_, b_,
                                                                     start=True, stop=True), reads=[rl.d[0], rr.d[0]], writes=[pq.d[0]])
                return pq

            def R_(b):
                return b[:].bitcast(F32R)

            ev = [0]

            def copy4(dst, pq, scale=None, red=False):
                ev[0] += 1
                o_ = flat(dst).bitcast(F32R) if red else flat(dst)
                evac(ev[0], o_, pq[:, :], [pq.d[0]], [dst.d[0]], scale)

            for it in range(NTT):
                def body(d, it=it):
                    c = it if d == 0 else NTT - 1 - it
                    first = (c % 2 == 0) if d == 0 else (c % 2 == 1)
                    slot = c // 2
                    t0 = c * 128
                    Sd = Sst[d]
                    if first and it > 0:
                        S.op("dve", lambda e: e.tensor_scalar(R_(Sd), Sd[:], keep[:, 0:1], None, ALU.mult),
                             reads=[Sd.d[0], keep.d[0]], writes=[Sd.d[0]])
                    B = {n: r.next() for n, r in rg.items() if n not in ("P", "Q")}
                    S.dma("sp", B["qT"][:], gd_qT[:, :, t0:t0 + 128], writes=[B["qT"].d[0]])
                    S.dma("sp", B["kT"][:], gd_kT[:, :, t0:t0 + 128], writes=[B["kT"].d[0]])
                    for nm_, src in (("qt", gd_q), ("kt", gd_k), ("vt", gd_v)):
                        S.dma("sp", B[nm_][:], src[t0:t0 + 128, :].rearrange("p (h d) -> p h d", h=4), writes=[B[nm_].d[0]])
                    yield
                    for nm_ in ("qT", "kT"):
                        ev[0] += 1
                        evac(ev[0], R_(B[nm_ + "r"]), B[nm_][:], [B[nm_].d[0]], [B[nm_ + "r"].d[0]])
                    gg = gates[:, c, 16 + d * 4:20 + d * 4]
                    gbeta = gates[:, c, 24 + d * 4:28 + d * 4]
                    gdp = gates.d[0]
                    Md = cst[:, 2 + d, :]
                    pcs = psr.next()
                    S.op("pe", lambda e: e.matmul(pcs[:, 0:4], Md, gg, start=True, stop=True), reads=[cdep, gdp], writes=[pcs.d[0]])
                    S.op("pe", lambda e: e.matmul(pcs[:, 4:8], ones, gg, start=True, stop=True), reads=[cdep, gdp], writes=[pcs.d[0]])
                    yield
                    sm = smr.next()
                    sd = sm.d[0]
                    S.op("dve", lambda e: e.tensor_copy(sm[:, 0:8], pcs[:, 0:8]), reads=[pcs.d[0]], writes=[sd])
                    S.op("act", lambda e: e.activation(sm[:, 8:16], sm[:, 0:8], AF.Exp), reads=[sd], writes=[sd])
                    S.op("dve", lambda e: e.tensor_tensor(sm[:, 16:20], sm[:, 4:8], sm[:, 0:4], ALU.subtract), reads=[sd], writes=[sd])
                    S.op("act", lambda e: e.activation(sm[:, 16:20], sm[:, 16:20], AF.Exp), reads=[sd], writes=[sd])
                    S.op("dve", lambda e: e.tensor_tensor(sm[:, 20:24], sm[:, 8:12], gbeta, ALU.mult), reads=[sd, gdp], writes=[sd])

                    def bc(ap4):
                        return ap4.unsqueeze(2).to_broadcast([128, 4, 128])
                    yield
                    dg, E = B["dg"], B["E"]
                    for h in range(4):
                        S.op("act", lambda e, h=h: e.mul(dg[:, h, :], ident, sm[:, h:h + 1]), reads=[cdep, sd], writes=[dg.d[0]])
                    yield
                    pbc = psr.next()
                    S.op("pe", lambda e: e.matmul(pbc[:, :], ones, flat(dg), start=True, stop=True), reads=[cdep, dg.d[0]], writes=[pbc.d[0]])
                    yield
                    for h in range(4):
                        S.op("act", lambda e, h=h: e.activation(E[:, h, :], pbc[:, h * 128:(h + 1) * 128], AF.Abs,
                                                                bias=sm[:, h:h + 1], scale=-1.0),
                             reads=[pbc.d[0], sd], writes=[E.d[0]])
                    S.op("act", lambda e: e.activation(E[:], E[:], AF.Exp, scale=-1.0), reads=[E.d[0]], writes=[E.d[0]])
                    for nm_, ci in (("EUi", 2 + d), ("EUs", 4 + d), ("ELs", 5 - d)):
                        S.op("pool", lambda e, nm_=nm_, ci=ci: e.tensor_tensor(
                            B[nm_][:], E[:], cst[:, ci:ci + 1, :].to_broadcast([128, 4, 128]), ALU.mult),
                            reads=[E.d[0], cdep], writes=[B[nm_].d[0]])
                    yield
                    for i_, (nm_, src, sc_, scd) in enumerate((("kb", "kt", gbeta, gdp), ("kbg", "kt", sm[:, 20:24], sd),
                                                              ("vb", "vt", gbeta, gdp), ("qg", "qt", sm[:, 8:12], sd),
                                                              ("kd", "kt", sm[:, 16:20], sd))):
                        S.op("pool" if i_ % 2 == 0 else "dve", lambda e, nm_=nm_, src=src, sc_=sc_: e.tensor_tensor(
                            R_(B[nm_]), B[src][:], bc(sc_), ALU.mult), reads=[B[src].d[0], scd], writes=[B[nm_].d[0]])
                    for nm_, src in (("kbT", "kb"), ("qgT", "qg")):
                        pt = psr.next()
                        for h in range(4):
                            S.op("pe", lambda e, h=h, pt=pt, src=src: e.transpose(pt[:, h * 128:(h + 1) * 128], B[src][:, h, :], ident),
                                 reads=[B[src].d[0], cdep], writes=[pt.d[0]])
                        copy4(B[nm_], pt, red=True)
                    yield
                    kT, qT, kbT, qgT = B["kTr"], B["qTr"], B["kbT"], B["qgT"]
                    for nm_, lh, rh, msk in (("NT", kT, kbT, "EUs"), ("N", kbT, kT, "ELs"), ("aT", kT, qT, "EUi")):
                        pq = mm4(lh, rh, lh, rh, True)
                        S.op("dve", lambda e, nm_=nm_, pq=pq, msk=msk: e.tensor_tensor(R_(B[nm_]), v4(pq), B[msk][:], ALU.mult),
                             reads=[pq.d[0], B[msk].d[0]], writes=[B[nm_].d[0]])
                    yield
                    NT, N = B["NT"], B["N"]
                    blkb = cst[:, 6:7, :].to_broadcast([128, 4, 128])
                    X, Y = {1: B["X1"]}, {1: B["Y1"]}
                    S.op("pool", lambda e: e.tensor_tensor(R_(X[1]), N[:], blkb, ALU.mult), reads=[N.d[0], cdep], writes=[X[1].d[0]])
                    S.op("pool", lambda e: e.tensor_tensor(R_(Y[1]), NT[:], blkb, ALU.mult), reads=[NT.d[0], cdep], writes=[Y[1].d[0]])
                    P, Q = rg["P"].next(), rg["Q"].next()
                    S.op("dve", lambda e: e.tensor_tensor(R_(P), identb, X[1][:], ALU.subtract), reads=[X[1].d[0], cdep], writes=[P.d[0]])
                    S.op("pool", lambda e: e.tensor_tensor(R_(Q), identb, Y[1][:], ALU.subtract), reads=[Y[1].d[0], cdep], writes=[Q.d[0]])
                    AT = B["AT"]
                    S.op("pool", lambda e: e.tensor_tensor(R_(AT), NT[:], identb, ALU.add), reads=[NT.d[0], cdep], writes=[AT.d[0]])
                    yield
                    kp = 1
                    for k in (2, 4, 8, 16):
                        X[k], Y[k] = B["X%d" % k], B["Y%d" % k]
                        pX = mm4(Y[kp], X[kp], Y[kp], X[kp], True)
                        pY = mm4(X[kp], Y[kp], X[kp], Y[kp], True)
                        yield
                        copy4(X[k], pX, red=True)
                        copy4(Y[k], pY, red=True)
                        yield
                        pP = mm4(Q, X[k], Q, X[k], True)
                        pQ = mm4(P, Y[k], P, Y[k], True)
                        yield
                        Pn, Qn = rg["P"].next(), rg["Q"].next()
                        S.op("dve", lambda e, Pn=Pn, pP=pP, P=P: e.tensor_tensor(R_(Pn), v4(pP), P[:], ALU.add),
                             reads=[pP.d[0], P.d[0]], writes=[Pn.d[0]])
                        S.op("dve", lambda e, Qn=Qn, pQ=pQ, Q=Q: e.tensor_tensor(R_(Qn), v4(pQ), Q[:], ALU.add),
                             reads=[pQ.d[0], Q.d[0]], writes=[Qn.d[0]])
                        P, Q = Pn, Qn
                        kp = k
                    Em = B["Em"]
                    for step in range(2):
                        yield
                        pR = mm4(AT, P, AT, P, True)
                        S.op("dve", lambda e, pR=pR: e.tensor_tensor(R_(Em), identb, v4(pR), ALU.subtract),
                             reads=[pR.d[0], cdep], writes=[Em.d[0]])
                        yield
                        Qn = rg["Q"].next()
                        if step == 0:
                            pP = mm4(Q, Em, Q, Em, True)
                            Pn = rg["P"].next()
                            S.op("dve", lambda e, Pn=Pn, pP=pP, P=P: e.tensor_tensor(R_(Pn), v4(pP), P[:], ALU.add),
                                 reads=[pP.d[0], P.d[0]], writes=[Pn.d[0]])
                        pQ = mm4(Em, Q, Em, Q, True)
                        S.op("dve", lambda e, Qn=Qn, pQ=pQ, Q=Q: e.tensor_tensor(R_(Qn), v4(pQ), Q[:], ALU.add),
                             reads=[pQ.d[0], Q.d[0]], writes=[Qn.d[0]])
                        if step == 0:
                            P = Pn
                        Q = Qn
                    kbg, vb, kd, aT, wTn, vn, oo = B["kbg"], B["vb"], B["kd"], B["aT"], B["wTn"], B["vn"], B["oo"]
                    yield
                    pw = mm4(kbg, Q, kbg, Q, True)
                    yield
                    copy4(wTn, pw, -1.0, red=True)
                    pu = psr.next()
                    for h in range(4):
                        S.op("pe", lambda e, h=h: e.matmul(pu[:, h * 128:(h + 1) * 128], Q[:, h, :].bitcast(F32R), vb[:, h, :].bitcast(F32R), start=True, stop=False),
                             reads=[Q.d[0], vb.d[0]], writes=[pu.d[0]])
                        S.op("pe", lambda e, h=h: e.matmul(pu[:, h * 128:(h + 1) * 128], wTn[:, h, :].bitcast(F32R), Sd[:, h, :].bitcast(F32R), start=False, stop=True),
                             reads=[wTn.d[0], Sd.d[0]], writes=[pu.d[0]])
                    yield
                    copy4(vn, pu, red=True)
                    po = psr.next()
                    for h in range(4):
                        S.op("pe", lambda e, h=h: e.matmul(po[:, h * 128:(h + 1) * 128], qgT[:, h, :].bitcast(F32R), Sd[:, h, :].bitcast(F32R), start=True, stop=False),
                             reads=[qgT.d[0], Sd.d[0]], writes=[po.d[0]])
                        S.op("pe", lambda e, h=h: e.matmul(po[:, h * 128:(h + 1) * 128], aT[:, h, :].bitcast(F32R), vn[:, h, :].bitcast(F32R), start=False, stop=True),
                             reads=[aT.d[0], vn.d[0]], writes=[po.d[0]])
                    yield
                    copy4(oo, po)
                    S.dma("sp", o_gd[d, t0:t0 + 128, :].rearrange("p (h d) -> p h d", h=4), oo[:], reads=[oo.d[0]])
                    pS = mm4(kd, vn, kd, vn, True)
                    yield
                    tS = B["tmpS"]
                    S.op("pool", lambda e: e.tensor_tensor(tS[:], Sd[:], bc(sm[:, 12:16]), ALU.mult), reads=[Sd.d[0], sd], writes=[tS.d[0]])
                    S.op("dve", lambda e: e.tensor_tensor(R_(Sd), v4(pS), tS[:], ALU.add), reads=[pS.d[0], tS.d[0]], writes=[Sd.d[0]])
                    if not first:
                        S.dma("sp", st_gdS[l, slot, d], Sd[:], reads=[Sd.d[0]])
                yield [body(0), body(1)]

    MAGIC = 12582912.0
    INV2PI = 1.0 / (2.0 * math.pi)
    TWOPI_LO = 6.2831845

    def phase_s5(l):
        psr = Ring(ps[0:4])
        with ExitStack() as st:
            def F(name, shape, n=1):
                return Ring([sb("s5%s%d_%d" % (name, l, i), shape, stack=st) for i in range(n)])
            Bp = [[F("Bp%d%d" % (d, ri), [128, 16, 128]).next() for ri in range(2)] for d in range(2)]
            Ct = [F("Ct%d" % ri, [128, 16, 128]).next() for ri in range(2)]
            cosT = [F("cos%d" % d, [128, 16, 128]).next() for d in range(2)]
            sinT = [F("sin%d" % d, [128, 16, 128]).next() for d in range(2)]
            rmat = [F("rmat%d" % d, [128, 16, 128]).next() for d in range(2)]
            smp = F("smp", [128, 2, 3, 16]).next()
            rsm = F("rsm", [128, 2, 16]).next()
            thy = F("thy", [128, 2, 16]).next()
            H = [F("H%d" % d, [128, 2, 16]).next() for d in range(2)]

            def fl(b):
                return b[:].rearrange("p a t -> p (a t)")

            def sincos(ybuf, dst_sin, dst_cos, t1, t2):
                for dst, shift in ((dst_sin, 0.0), (dst_cos, 0.25)):
                    S.op("dve", lambda e: e.tensor_scalar(fl(t1), ybuf, shift, MAGIC, ALU.add, ALU.add), reads=ybd, writes=[t1.d[0]])
                    S.op("dve", lambda e: e.tensor_scalar(fl(t1), fl(t1), MAGIC, None, ALU.subtract), reads=[t1.d[0]], writes=[t1.d[0]])
                    S.op("dve", lambda e: e.scalar_tensor_tensor(fl(t2), ybuf, shift, fl(t1), ALU.add, ALU.subtract),
                         reads=ybd + [t1.d[0]], writes=[t2.d[0]])
                    S.op("act", lambda e: e.activation(fl(dst), fl(t2), AF.Sin, scale=TWOPI_LO), reads=[t2.d[0]], writes=[dst.d[0]])

            S.dma("sp", smp[:], s5sm[:, l, :, :, :], writes=[smp.d[0]])
            S.dma("sp", Ct[0][:], s5Ct[:, l, 0, :, :], writes=[Ct[0].d[0]])
            S.dma("sp", Ct[1][:], s5Ct[:, l, 1, :, :], writes=[Ct[1].d[0]])
            S.op("dve", lambda e: e.tensor_scalar(fl(Ct[1]), fl(Ct[1]), -1.0, None, ALU.mult), reads=[Ct[1].d[0]], writes=[Ct[1].d[0]])
            for d in range(2):
                S.dma("sp", H[d][:], s5h0[:, l, d, :, :], writes=[H[d].d[0]])
            dtm = F("dtm", [128, 2, 16]).next()
            S.op("act", lambda e: e.activation(dtm[:], smp[:, :, 2, :], AF.Exp), reads=[smp.d[0]], writes=[dtm.d[0]])
            S.op("dve", lambda e: e.tensor_tensor(rsm[:], smp[:, :, 0, :], dtm[:], ALU.mult), reads=[smp.d[0], dtm.d[0]], writes=[rsm.d[0]])
            S.op("act", lambda e: e.activation(rsm[:], rsm[:], AF.Exp), reads=[rsm.d[0]], writes=[rsm.d[0]])
            S.op("dve", lambda e: e.scalar_tensor_tensor(thy[:], smp[:, :, 1, :], INV2PI, dtm[:], ALU.mult, ALU.mult),
                 reads=[smp.d[0], dtm.d[0]], writes=[thy.d[0]])
            with ExitStack() as st2:
                def G(name):
                    return sb("s5t%s%d" % (name, l), [128, 16, 128], stack=st2)
                LR, LI, LS, DT, AR, AI, T1, T2, ZR, ZI, BR, BI = [G(n) for n in
                                                                  ("LR", "LI", "LS", "DT", "AR", "AI", "T1", "T2", "ZR", "ZI", "BR", "BI")]
                S.dma("sp", BR[:], s5Bt[:, l, 0, :, :], writes=[BR.d[0]])
                S.dma("sp", BI[:], s5Bt[:, l, 1, :, :], writes=[BI.d[0]])
                for d in range(2):
                    for buf, k in ((LR, 0), (LI, 1), (LS, 2)):
                        S.dma("sp", fl(buf), s5rep[:, l, d, k, :], writes=[buf.d[0]])
                    S.op("act", lambda e: e.activation(fl(DT), fl(LS), AF.Exp), reads=[LS.d[0]], writes=[DT.d[0]])
                    S.op("dve", lambda e: e.tensor_tensor(fl(AR), fl(LR), fl(DT), ALU.mult), reads=[LR.d[0], DT.d[0]], writes=[AR.d[0]])
                    S.op("act", lambda e: e.activation(fl(AR), fl(AR), AF.Exp), reads=[AR.d[0]], writes=[AR.d[0]])
                    S.op("dve", lambda e: e.scalar_tensor_tensor(fl(AI), fl(LI), INV2PI, fl(DT), ALU.mult, ALU.mult),
                         reads=[LI.d[0], DT.d[0]], writes=[AI.d[0]])
                    ybd = [AI.d[0]]
                    sincos(fl(AI), ZI, ZR, T1, T2)
                    S.op("dve", lambda e: e.tensor_tensor(fl(AI), fl(AR), fl(ZI), ALU.mult), reads=[AR.d[0], ZI.d[0]], writes=[AI.d[0]])
                    S.op("dve", lambda e: e.tensor_tensor(fl(AR), fl(AR), fl(ZR), ALU.mult), reads=[AR.d[0], ZR.d[0]], writes=[AR.d[0]])
                    S.op("dve", lambda e: e.tensor_scalar(fl(AR), fl(AR), -1.0, None, ALU.add), reads=[AR.d[0]], writes=[AR.d[0]])
                    S.op("dve", lambda e: e.tensor_tensor(fl(T1), fl(LR), fl(LR), ALU.mult), reads=[LR.d[0]], writes=[T1.d[0]])
                    S.op("dve", lambda e: e.tensor_tensor(fl(T2), fl(LI), fl(LI), ALU.mult), reads=[LI.d[0]], writes=[T2.d[0]])
                    S.op("dve", lambda e: e.tensor_tensor(fl(T1), fl(T1), fl(T2), ALU.add), reads=[T1.d[0], T2.d[0]], writes=[T1.d[0]])
                    S.op("dve", lambda e: e.reciprocal(fl(T1), fl(T1)), reads=[T1.d[0]], writes=[T1.d[0]])
                    S.op("dve", lambda e: e.tensor_tensor(fl(ZR), fl(AR), fl(LR), ALU.mult), reads=[AR.d[0], LR.d[0]], writes=[ZR.d[0]])
                    S.op("dve", lambda e: e.tensor_tensor(fl(T2), fl(AI), fl(LI), ALU.mult), reads=[AI.d[0], LI.d[0]], writes=[T2.d[0]])
                    S.op("dve", lambda e: e.tensor_tensor(fl(ZR), fl(ZR), fl(T2), ALU.add), reads=[ZR.d[0], T2.d[0]], writes=[ZR.d[0]])
                    S.op("dve", lambda e: e.tensor_tensor(fl(ZR), fl(ZR), fl(T1), ALU.mult), reads=[ZR.d[0], T1.d[0]], writes=[ZR.d[0]])
                    S.op("dve", lambda e: e.tensor_tensor(fl(ZI), fl(AI), fl(LR), ALU.mult), reads=[AI.d[0], LR.d[0]], writes=[ZI.d[0]])
                    S.op("dve", lambda e: e.tensor_tensor(fl(T2), fl(AR), fl(LI), ALU.mult), reads=[AR.d[0], LI.d[0]], writes=[T2.d[0]])
                    S.op("dve", lambda e: e.tensor_tensor(fl(ZI), fl(ZI), fl(T2), ALU.subtract), reads=[ZI.d[0], T2.d[0]], writes=[ZI.d[0]])
                    S.op("dve", lambda e: e.tensor_tensor(fl(ZI), fl(ZI), fl(T1), ALU.mult), reads=[ZI.d[0], T1.d[0]], writes=[ZI.d[0]])
                    bre, bim = Bp[d]
                    S.op("dve", lambda e: e.tensor_tensor(fl(bre), fl(ZR), fl(BR), ALU.mult), reads=[ZR.d[0], BR.d[0]], writes=[bre.d[0]])
                    S.op("dve", lambda e: e.tensor_tensor(fl(T2), fl(ZI), fl(BI), ALU.mult), reads=[ZI.d[0], BI.d[0]], writes=[T2.d[0]])
                    S.op("dve", lambda e: e.tensor_tensor(fl(bre), fl(bre), fl(T2), ALU.subtract), reads=[bre.d[0], T2.d[0]], writes=[bre.d[0]])
                    S.op("dve", lambda e: e.tensor_tensor(fl(bim), fl(ZR), fl(BI), ALU.mult), reads=[ZR.d[0], BI.d[0]], writes=[bim.d[0]])
                    S.op("dve", lambda e: e.tensor_tensor(fl(T2), fl(ZI), fl(BR), ALU.mult), reads=[ZI.d[0], BR.d[0]], writes=[T2.d[0]])
                    S.op("dve", lambda e: e.tensor_tensor(fl(bim), fl(bim), fl(T2), ALU.add), reads=[bim.d[0], T2.d[0]], writes=[bim.d[0]])
                    S.op("dve", lambda e: e.tensor_tensor(AI[:], cst[:, 7:8, :].to_broadcast([128, 16, 128]),
                                                          thy[:, d, :].unsqueeze(2).to_broadcast([128, 16, 128]), ALU.mult),
                         reads=[cdep, thy.d[0]], writes=[AI.d[0]])
                    ybd = [AI.d[0]]
                    sincos(fl(AI), sinT[d], cosT[d], T1, T2)
                    S.op("dve", lambda e: e.tensor_copy(rmat[d][:], rsm[:, d, :].unsqueeze(2).to_broadcast([128, 16, 128])),
                         reads=[rsm.d[0]], writes=[rmat[d].d[0]])
                    fpos = 0 if d == 0 else 127
                    S.op("dve", lambda e: e.memset(rmat[d][:, :, fpos:fpos + 1], 0.0), writes=[rmat[d].d[0]])
                S.barrier()
            uTr = F("uT", [128, 4, 128], 2)
            xr_r, xi_r = F("xr", [128, 16, 128], 2), F("xi", [128, 16, 128], 2)
            tr = [F("t%d" % i, [128, 4, 128], 2) for i in range(4)]
            smallH = F("sH", [128, 2, 16], 2)
            y5r = F("y5", [128, 4, 128], 2)
            ev = [0]
            for it in range(NTT):
                def body(d, it=it):
                    c = it if d == 0 else NTT - 1 - it
                    first = (c % 2 == 0) if d == 0 else (c % 2 == 1)
                    slot = c // 2
                    t0 = c * 128
                    Hd = H[d]
                    fpos = 0 if d == 0 else 127
                    lpos = 127 if d == 0 else 0

                    def dv(ap3):
                        return ap3 if d == 0 else ap3[:, :, ::-1]
                    if first and it > 0:
                        S.op("dve", lambda e: e.tensor_scalar(Hd[:], Hd[:], keep[:, 0:1], None, ALU.mult),
                             reads=[Hd.d[0], keep.d[0]], writes=[Hd.d[0]])
                    uT = uTr.next()
                    S.dma("sp", uT[:], s5_uT[:, :, t0:t0 + 128], writes=[uT.d[0]])
                    yield
                    xr, xi = xr_r.next(), xi_r.next()
                    gr, gi = xr, xi
                    for q4 in range(4):
                        yield
                        pre, pim = psr.next(), psr.next()
                        for j4 in range(4):
                            j = q4 * 4 + j4
                            for pp, ri in ((pre, 0), (pim, 1)):
                                S.op("pe", lambda e, pp=pp, ri=ri, j=j, j4=j4: e.matmul(
                                    pp[:, j4 * 128:(j4 + 1) * 128], f32r(Bp[d][ri][:, j, :]), f32r(uT[:, j // 4, :]), start=True, stop=True),
                                    reads=[Bp[d][ri].d[0], uT.d[0]], writes=[pp.d[0]])
                        yield
                        cv, sv = dv(cosT[d][:, q4 * 4:q4 * 4 + 4, :]), dv(sinT[d][:, q4 * 4:q4 * 4 + 4, :])
                        pr3 = pre[:, :].rearrange("p (a t) -> p a t", a=4)
                        pi3 = pim[:, :].rearrange("p (a t) -> p a t", a=4)
                        t1, t2, t3, t4 = [r_.next() for r_ in tr]
                        tdep = [cosT[d].d[0], sinT[d].d[0]]
                        S.op("dve", lambda e: e.tensor_tensor(t1[:], pr3, cv, ALU.mult), reads=[pre.d[0]] + tdep, writes=[t1.d[0]])
                        S.op("dve", lambda e: e.tensor_tensor(t2[:], pi3, sv, ALU.mult), reads=[pim.d[0]] + tdep, writes=[t2.d[0]])
                        S.op("pool", lambda e: e.tensor_tensor(xr[:, q4 * 4:q4 * 4 + 4, :], t1[:], t2[:], ALU.add),
                             reads=[t1.d[0], t2.d[0]], writes=[xr.d[0]])
                        S.op("dve", lambda e: e.tensor_tensor(t3[:], pi3, cv, ALU.mult), reads=[pim.d[0]] + tdep, writes=[t3.d[0]])
                        S.op("dve", lambda e: e.tensor_tensor(t4[:], pr3, sv, ALU.mult), reads=[pre.d[0]] + tdep, writes=[t4.d[0]])
                        S.op("pool", lambda e: e.tensor_tensor(xi[:, q4 * 4:q4 * 4 + 4, :], t3[:], t4[:], ALU.subtract),
                             reads=[t3.d[0], t4.d[0]], writes=[xi.d[0]])
                    yield
                    sH = smallH.next()
                    S.op("dve", lambda e: e.tensor_tensor(sH[:], Hd[:], rsm[:, d:d + 1, :].to_broadcast([128, 2, 16]), ALU.mult),
                         reads=[Hd.d[0], rsm.d[0]], writes=[sH.d[0]])
                    for xb_, ri in ((xr, 0), (xi, 1)):
                        S.op("dve", lambda e, xb_=xb_, ri=ri: e.tensor_tensor(
                            xb_[:, :, fpos:fpos + 1], xb_[:, :, fpos:fpos + 1], sH[:, ri, :].unsqueeze(2), ALU.add),
                            reads=[xb_.d[0], sH.d[0]], writes=[xb_.d[0]])

                    def fv(b):
                        v = fl(b)
                        return v if d == 0 else v[:, ::-1]
                    yield
                    for xb_, gb_ in ((xr, gr), (xi, gi)):
                        S.op("dve", lambda e, xb_=xb_, gb_=gb_: e.tensor_tensor_scan(
                            fv(gb_), fv(rmat[d]), fv(xb_), 0.0, ALU.mult, ALU.add),
                            reads=[xb_.d[0], rmat[d].d[0]], writes=[gb_.d[0]])
                    for q4 in range(4):
                        yield
                        sl_ = slice(q4 * 4, q4 * 4 + 4)
                        cv, sv = dv(cosT[d][:, sl_, :]), dv(sinT[d][:, sl_, :])
                        t1, t2, t3, t4 = [r_.next() for r_ in tr]
                        tdep = [cosT[d].d[0], sinT[d].d[0]]
                        S.op("pool", lambda e: e.tensor_tensor(t1[:], gr[:, sl_, :], cv, ALU.mult), reads=[gr.d[0]] + tdep, writes=[t1.d[0]])
                        S.op("pool", lambda e: e.tensor_tensor(t2[:], gi[:, sl_, :], sv, ALU.mult), reads=[gi.d[0]] + tdep, writes=[t2.d[0]])
                        S.op("pool", lambda e: e.tensor_tensor(t3[:], gi[:, sl_, :], cv, ALU.mult), reads=[gi.d[0]] + tdep, writes=[t3.d[0]])
                        S.op("dve", lambda e: e.tensor_tensor(t4[:], gr[:, sl_, :], sv, ALU.mult), reads=[gr.d[0]] + tdep, writes=[t4.d[0]])
                        S.op("dve", lambda e: e.tensor_tensor(xr[:, sl_, :], t1[:], t2[:], ALU.subtract), reads=[t1.d[0], t2.d[0]], writes=[xr.d[0]])
                        S.op("dve", lambda e: e.tensor_tensor(xi[:, sl_, :], t3[:], t4[:], ALU.add), reads=[t3.d[0], t4.d[0]], writes=[xi.d[0]])
                    for xb_, ri in ((xr, 0), (xi, 1)):
                        S.op("act", lambda e, xb_=xb_, ri=ri: e.copy(Hd[:, ri, :].unsqueeze(2), xb_[:, :, lpos:lpos + 1]),
                             reads=[xb_.d[0]], writes=[Hd.d[0]])
                    yield
                    py = psr.next()
                    for kq in range(4):
                        n_ = 0
                        for j in range(kq * 4, kq * 4 + 4):
                            for hb_, ri in ((xr, 0), (xi, 1)):
                                S.op("pe", lambda e, j=j, hb_=hb_, ri=ri, kq=kq, n_=n_: e.matmul(
                                    py[:, kq * 128:(kq + 1) * 128], f32r(Ct[ri][:, j, :]), f32r(hb_[:, j, :]), start=(n_ == 0), stop=(n_ == 7)),
                                    reads=[Ct[ri].d[0], hb_.d[0]], writes=[py.d[0]])
                                n_ += 1
                    yield
                    y5 = y5r.next()
                    ev[0] += 1
                    evac(ev[0], y5[:].rearrange("p a t -> p (a t)"), py[:, :], [py.d[0]], [y5.d[0]])
                    S.dma("sp", y_s5T[d, :, :, t0:t0 + 128], y5[:], reads=[y5.d[0]])
                    if not first:
                        S.dma("sp", st_s5[l, slot, d], Hd[:], reads=[Hd.d[0]])
                yield [body(0), body(1)]

    for l in range(nlayers):
        S.barrier()
        stL = ExitStack()
        hT = sb("hT%d" % l, [128, KT, T], BF16, n=NBLK, stack=stL)
        hTh[0] = hT
        with ExitStack() as st:
            xbr = Ring([sb("xb%d_%d" % (l, i), [128, KT, 512], stack=st) for i in range(2)])
            tmpr = Ring([sb("xtmp%d_%d" % (l, i), [128, KT, 512], stack=st) for i in range(2)])
            if l == 0:
                xtr = Ring([sb("xtok%d" % i, [128, D], stack=st) for i in range(4)])
                posr = Ring([sb("pos%d" % i, [128, KT, 512], stack=st) for i in range(2)])
            for blk in range(NBLK):
                xb = xbr.next()
                tmp = tmpr.next()
                if l == 0:
                    pb = posr.next()
                    S.dma("sp", pb[:], pos_in[:, :, blk * 512:(blk + 1) * 512], writes=[pb.d[0]])
                    xts = []
                    for tt in range(4):
                        xt = xtr.next()
                        r0 = blk * 512 + tt * 128
                        S.dma("sp", xt[:], x_in[r0:r0 + 128, :], writes=[xt.d[0]])
                        xts.append(xt)
                    for kt in range(KT):
                        pt = psr.next()
                        for tt in range(4):
                            S.op("pe", lambda e, tt=tt, kt=kt, pt=pt: e.transpose(
                                pt[:, tt * 128:(tt + 1) * 128], xts[tt][:, kt * 128:(kt + 1) * 128], ident),
                                reads=[xts[tt].d[0], cdep], writes=[pt.d[0]])
                        S.op("dve", lambda e, kt=kt, pt=pt: e.tensor_tensor(xb[:, kt, :], pt[:, :], pb[:, kt, :], ALU.add),
                             reads=[pt.d[0], pb.d[0]], writes=[xb.d[0]])
                    S.dma("sp", xT_scr[:, :, blk * 512:(blk + 1) * 512], xb[:], reads=[xb.d[0]])
                else:
                    S.dma("sp", xb[:], xT_scr[:, :, blk * 512:(blk + 1) * 512], writes=[xb.d[0]])
                rmsnorm_block(st, xb, l, 0, blk, tmp)
            if hT_dbg is not None and l == 0:
                S.dma("sp", hT_dbg, hT[:], reads=hT.d)

        S.barrier()
        with ExitStack() as st:
            wr = Ring([sb("wA%d_%d" % (l, i), [128, KT, 512], BF16, stack=st) for i in range(2)])
            stg = Ring([sb("stgA%d_%d" % (l, i), [128, 512], stack=st) for i in range(4)])
            wsrc = w_in[l].rearrange("(kt p) c -> p kt c", p=128)
            ecount = [0]

            wfr = Ring([sb("wAf%d_%d" % (l, i), [128, KT, 512], F32, stack=st) for i in range(2)])

            def load_w(c0, ncol=512, w=None, o=0):
                wf = wfr.next()
                if w is None:
                    w = wr.next()
                S.dma("sp", wf[:, :, 0:ncol], wsrc[:, :, c0:c0 + ncol], writes=[wf.d[0]])
                ecount[0] += 1
                evac(ecount[0], w[:, :, o:o + ncol], wf[:, :, 0:ncol], [wf.d[0]], [w.d[0]])
                return w

            def feat_major(w, dst, scale=None):
                for ct in range(4):
                    for blk in range(NBLK):
                        pq = psr.next()
                        for kt in range(KT):
                            S.op("pe", lambda e, kt=kt, ct=ct, blk=blk, pq=pq: e.matmul(
                                pq[:, :], w[:, kt, ct * 128:(ct + 1) * 128], hT[:, kt, blk * 512:(blk + 1) * 512],
                                start=(kt == 0), stop=(kt == KT - 1)),
                                reads=[w.d[0], hT.d[blk]], writes=[pq.d[0]])
                        sg = stg.next()
                        ecount[0] += 1
                        evac(ecount[0], sg[:], pq[:, :], [pq.d[0]], [sg.d[0]], scale)
                        S.dma("sp", dst[:, ct, blk * 512:(blk + 1) * 512], sg[:], reads=[sg.d[0]])

            def tok_major(w, ncol, consume):
                for tt in range(NTT):
                    pq = psr.next()
                    blk = tt // 4
                    for kt in range(KT):
                        S.op("pe", lambda e, kt=kt, tt=tt, pq=pq: e.matmul(
                            pq[:, 0:ncol], hT[:, kt, tt * 128:(tt + 1) * 128], w[:, kt, 0:ncol],
                            start=(kt == 0), stop=(kt == KT - 1)),
                            reads=[w.d[0], hT.d[blk]], writes=[pq.d[0]])
                    consume(tt, pq)

            w = load_w(C_MQ)
            if dbg and "wdbg" in dbg and l == 0:
                wdbg = dscr("wdbg", [128, KT, 512], BF16)
                S.dma("sp", wdbg, w[:], reads=[w.d[0]])
            feat_major(w, ml_qT)
            w = load_w(C_MK)
            ksc = 128 ** -0.5
            feat_major(w, ml_kT, ksc)

            def cons_k(tt, pq):
                sg = stg.next()
                ecount[0] += 1
                evac(ecount[0], sg[:], pq[:, :], [pq.d[0]], [sg.d[0]], ksc)
                S.dma("sp", ml_k[tt * 128:(tt + 1) * 128, :], sg[:], reads=[sg.d[0]])
            tok_major(w, 512, cons_k)
            w = load_w(C_MV)
            vstg = Ring([sb("vstg%d_%d" % (l, i), [128, 4, 129], stack=st) for i in range(2)])
            for vb in vstg.bufs:
                S.op("pool", lambda e, vb=vb: e.memset(vb[:], 1.0), writes=[vb.d[0]])

            def cons_v(tt, pq):
                sg = vstg.next()
                ecount[0] += 1
                evac(ecount[0], sg[:, :, 0:128], pq[:, :].rearrange("p (h d) -> p h d", h=4), [pq.d[0]], [sg.d[0]])
                S.dma("sp", ml_v[tt * 128:(tt + 1) * 128, :, :], sg[:], reads=[sg.d[0]])
            tok_major(w, 512, cons_v)
            w = load_w(C_SU)
            feat_major(w, s5_uT)
            w = load_w(C_MI, 16)
            load_w(C_GA, 16, w=w, o=16)
            pg = [psr.next(), psr.next()]
            for tt in range(NTT):
                pq = pg[tt // 16]
                o = (tt % 16) * 32
                for kt in range(KT):
                    S.op("pe", lambda e, kt=kt, tt=tt, pq=pq, o=o: e.matmul(
                        pq[:, o:o + 32], hT[:, kt, tt * 128:(tt + 1) * 128], w[:, kt, 0:32],
                        start=(kt == 0), stop=(kt == KT - 1)),
                        reads=[w.d[0], hT.d[tt // 4]], writes=[pq.d[0]])
            gtmp = sb("gtmp%d" % l, [128, 16, 8], stack=st)
            gd_ = gates.d[0]
            for half in range(2):
                pq = pg[half]
                pv = pq[:, :].rearrange("p (t c) -> p t c", c=32)
                gv = gates[:, half * 16:(half + 1) * 16, :]

                def bias(k):
                    return gb[:, l, k * 8:(k + 1) * 8].unsqueeze(1).to_broadcast([128, 16, 8])
                S.op("dve", lambda e: e.tensor_tensor(gv[:, :, 0:8], pv[:, :, 0:8], bias(0), ALU.add),
                     reads=[pq.d[0], gb.d[0]], writes=[gd_])
                S.op("dve", lambda e: e.tensor_tensor(gtmp[:], pv[:, :, 8:16], bias(1), ALU.add),
                     reads=[pq.d[0], gb.d[0]], writes=[gtmp.d[0]])
                S.op("act", lambda e: e.activation(gtmp[:], gtmp[:], AF.Exp, scale=-1.0),
                     reads=[gtmp.d[0]], writes=[gtmp.d[0]])
                S.op("act", lambda e: e.activation(gtmp[:], gtmp[:], AF.Ln, bias=1.0),
                     reads=[gtmp.d[0]], writes=[gtmp.d[0]])
                S.op("dve", lambda e: e.tensor_scalar(gv[:, :, 8:16], gtmp[:], -1.0, None, ALU.mult),
                     reads=[gtmp.d[0]], writes=[gd_])
                S.op("dve", lambda e: e.tensor_tensor(gtmp[:], pv[:, :, 16:24], bias(2), ALU.add),
                     reads=[pq.d[0], gb.d[0]], writes=[gtmp.d[0]])
                S.op("act", lambda e: e.activation(gtmp[:], gtmp[:], AF.Exp), reads=[gtmp.d[0]], writes=[gtmp.d[0]])
                S.op("act", lambda e: e.activation(gtmp[:], gtmp[:], AF.Ln, bias=1.0),
                     reads=[gtmp.d[0]], writes=[gtmp.d[0]])
                S.op("dve", lambda e: e.tensor_tensor(gv[:, :, 16:24], gtmp[:], negA[:, l, :].unsqueeze(1).to_broadcast(
                    [128, 16, 8]), ALU.mult), reads=[gtmp.d[0], negA.d[0]], writes=[gd_])
                S.op("act", lambda e: e.activation(gv[:, :, 24:32], pv[:, :, 24:32], AF.Sigmoid),
                     reads=[pq.d[0]], writes=[gd_])
            if gates_dbg is not None and l == 0:
                S.dma("sp", gates_dbg, gates[:], reads=[gd_])

            dgw = sb("dgw%d" % l, [128, 60, 128], stack=st)
            S.op("dve", lambda e: e.tensor_tensor(
                dgw[:], cst[:, 0:1, :].to_broadcast([128, 60, 128]),
                cw[:, l, :, :].rearrange("p c j -> p (c j)").unsqueeze(2).to_broadcast([128, 60, 128]), ALU.mult),
                reads=[cdep, cw.d[0]], writes=[dgw.d[0]])
            xpr = Ring([sb("xpad%d_%d" % (l, i), [128, 4, 260], stack=st) for i in range(2)])
            yallr = Ring([sb("yall%d_%d" % (l, i), [128, 4, 256], stack=st) for i in range(2)])
            sqr = Ring([sb("csq%d_%d" % (l, i), [128, 4, 256], stack=st) for i in range(2)])
            wq3 = [load_w(C_GQKV + g3 * 512) for g3 in range(2)]

            def conv_item(g3, slot, w):
                dT, dtok = ((gd_qT, gd_q), (gd_kT, gd_k), (None, gd_v))[g3]
                t0 = slot * SLOT
                lo = t0 - 2 if slot > 0 else t0
                hi = t0 + SLOT + 2 if slot < NSLOT - 1 else t0 + SLOT
                off = lo - (t0 - 2)
                n = hi - lo
                hdeps = [hT.d[b] for b in range(lo // 512, (hi - 1) // 512 + 1)]
                pqs = []
                for ct in range(4):
                    pq = psr.next()
                    pqs.append(pq)
                    for kt in range(KT):
                        S.op("pe", lambda e, kt=kt, ct=ct, pq=pq: e.matmul(
                            pq[:, off:off + n], w[:, kt, ct * 128:(ct + 1) * 128], hT[:, kt, lo:hi],
                            start=(kt == 0), stop=(kt == KT - 1)),
                            reads=[w.d[0]] + hdeps, writes=[pq.d[0]])
                yield
                xp = xpr.next()
                for ct in range(4):
                    ecount[0] += 1
                    evac(ecount[0], xp[:, ct, off:off + n], pqs[ct][:, off:off + n], [pqs[ct].d[0]], [xp.d[0]])
                for (a, b, inside) in ((0, 2, slot > 0), (258, 260, slot < NSLOT - 1)):
                    if inside:
                        S.op("dve", lambda e, a=a, b=b: e.tensor_scalar(xp[:, :, a:b], xp[:, :, a:b], keep[:, 0:1], None, ALU.mult),
                             reads=[xp.d[0], keep.d[0]], writes=[xp.d[0]])
                    else:
                        S.op("dve", lambda e, a=a, b=b: e.memset(xp[:, :, a:b], 0.0), writes=[xp.d[0]])
                yield
                pcs_ = [psr.next(), psr.next()]
                for ct in range(4):
                    pc = pcs_[ct // 2]
                    o_ = (ct % 2) * 256
                    for j in range(5):
                        S.op("pe", lambda e, ct=ct, j=j, pc=pc, o_=o_: e.matmul(
                            pc[:, o_:o_ + 256], f32r(dgw[:, (g3 * 4 + ct) * 5 + j, :]), f32r(xp[:, ct, j:j + 256]),
                            start=(j == 0), stop=(j == 4)), reads=[dgw.d[0], xp.d[0]], writes=[pc.d[0]])
                yield
                yall = yallr.next()
                for half in range(2):
                    S.op("act", lambda e, half=half: e.activation(
                        yall[:, half * 2:half * 2 + 2, :], pcs_[half][:, :].rearrange("p (c t) -> p c t", c=2), AF.Silu),
                        reads=[pcs_[half].d[0]], writes=[yall.d[0]])
                if g3 < 2:
                    yield
                    sq = sqr.next()
                    S.op("pool", lambda e: e.tensor_tensor(sq[:], yall[:], yall[:], ALU.mult), reads=[yall.d[0]], writes=[sq.d[0]])
                    pns = [psr.next(), psr.next()]
                    for ct in range(4):
                        S.op("pe", lambda e, ct=ct: e.matmul(pns[ct // 2][:, (ct % 2) * 256:(ct % 2 + 1) * 256], ones, sq[:, ct, :],
                                                             start=True, stop=True), reads=[sq.d[0], cdep], writes=[pns[ct // 2].d[0]])
                    yield
                    rs = sq
                    for half in range(2):
                        S.op("act", lambda e, half=half: e.activation(
                            rs[:, half * 2:half * 2 + 2, :], pns[half][:, :].rearrange("p (c t) -> p c t", c=2), AF.Sqrt,
                            bias=epsb[:, 0:1]), reads=[pns[half].d[0], epsb.d[0]], writes=[rs.d[0]])
                    yield
                    S.op("dve", lambda e: e.reciprocal(rs[:], rs[:]), reads=[rs.d[0]], writes=[rs.d[0]])
                    qs = (128 ** -0.5) if g3 == 0 else 1.0
                    S.op("dve", lambda e: e.scalar_tensor_tensor(yall[:], yall[:], qs, rs[:], ALU.mult, ALU.mult),
                         reads=[yall.d[0], rs.d[0]], writes=[yall.d[0]])
                    S.dma("sp", dT[:, :, t0:t0 + SLOT], yall[:], reads=[yall.d[0]])
                yield
                pts = []
                for t2 in range(2):
                    pt = psr.next()
                    pts.append(pt)
                    for ct in range(4):
                        S.op("pe", lambda e, ct=ct, pt=pt, t2=t2: e.transpose(
                            pt[:, ct * 128:(ct + 1) * 128], yall[:, ct, t2 * 128:(t2 + 1) * 128], ident),
                            reads=[yall.d[0], cdep], writes=[pt.d[0]])
                yield
                for t2 in range(2):
                    sg = stg.next()
                    ecount[0] += 1
                    evac(ecount[0], sg[:], pts[t2][:, :], [pts[t2].d[0]], [sg.d[0]])
                    S.dma("sp", dtok[t0 + t2 * 128:t0 + (t2 + 1) * 128, :], sg[:], reads=[sg.d[0]])

            def conv_items():
                for g3 in range(3):
                    w = wq3[g3] if g3 < 2 else load_w(C_GQKV + 2 * 512)
                    for slot in range(NSLOT):
                        yield conv_item(g3, slot, w)
            drive_window(conv_items(), 2)
        S.barrier()
        stL.close()
        hTh[0] = None

        if dbg and "stopA" in dbg:
            continue
        S.barrier()
        for gens in phase_gdn(l):
            drive(gens)
        S.barrier()
        it5, itm = phase_s5(l), phase_mlstm(l)
        for _ in range(NTT):
            g5 = next(it5)
            gm = next(itm)
            drive(g5 + gm)
        for _ in itm:
            pass
        for _ in it5:
            pass
        S.barrier()
        with ExitStack() as st:
            xbr = Ring([sb("cxb%d_%d" % (l, i), [128, KT, 512], stack=st) for i in range(1)])
            tmpr = Ring([sb("ctmp%d_%d" % (l, i), [128, KT, 512], stack=st) for i in range(1)])
            h2r = Ring([sb("h2_%d_%d" % (l, i), [128, KT, 512], BF16, stack=st) for i in range(1)])
            actr = Ring([sb("actT%d_%d" % (l, i), [128, FT, 512], BF16, stack=st) for i in range(1)])
            wgb = Ring([sb("wgb%d_%d" % (l, i), [128, KT, 256], BF16, stack=st) for i in range(5)])
            h1 = sb("h1_%d" % l, [128, KT, 512], BF16, stack=st)
            ysT = [sb("ysT%d_%d" % (l, n), [128, 4, 512], BF16, stack=st) for n in range(3)]
            mg = sb("mg%d" % l, [128, KT, 512], BF16, stack=st)
            maccs = [sb("macc%d_%d" % (l, i), [128, 512], stack=st) for i in range(2)]
            s5u = sb("s5u%d" % l, [128, 4, 512], stack=st)
            gbf = sb("gbf%d" % l, [128, 4, 512], BF16, stack=st)
            tkr = Ring([sb("tk%d_%d" % (l, i), [128, 512], stack=st) for i in range(4)])
            ssr = Ring([sb("ss%d_%d" % (l, i), [128, 4], stack=st) for i in range(2)])
            nwm = sb("nwm%d" % l, [128, 512], stack=st)
            nwg = sb("nwg%d" % l, [128, 128], stack=st)
            dbt = sb("dbt%d" % l, [128, 2, 4], stack=st)
            S.dma("sp", nwm[:], mlnw[:, l, :], writes=[nwm.d[0]])
            S.dma("sp", nwg[:], gdnw[:, l, :], writes=[nwg.d[0]])
            S.dma("sp", dbt[:], s5DbT[:, l, :, :], writes=[dbt.d[0]])
            win_src = w_in[l].rearrange("(kt p) c -> p kt c", p=128)
            wout_src = w_out[l].rearrange("(kt p) c -> p kt c", p=128)
            glu_src = glu_w[l].rearrange("(kt p) c -> p kt c", p=128)
            wbr_src = [w_branch[l, n].rearrange("(kt p) c -> p kt c", p=128) for n in range(3)]
            wdb = Ring([sb("wdb%d_%d" % (l, i), [128, FT, 128], BF16, stack=st) for i in range(2)])
            sgr = Ring([sb("csg%d_%d" % (l, i), [128, 512], stack=st) for i in range(3)])
            last = (l == DEPTH - 1)
            if last:
                ytr = Ring([sb("ytok%d" % i, [128, 512], stack=st) for i in range(1)])
            wg_src = w_gate[l].rearrange("(kt p) c -> p kt c", p=128)
            wu_src = w_up[l].rearrange("(kt p) c -> p kt c", p=128)
            wd_src = w_down[l].rearrange("(ft p) c -> p ft c", p=128)
            ec = [0]
            wgs_keep = {}
            wcache = {}

            def load_cast(src, fr, br, nk=None, key=None):
                shp = list(br.bufs[0].t.shape)
                if key is not None and key in wcache:
                    cd, cdep_ = wcache[key]
                    wb_ = br.next()
                    if nk is None:
                        S.dma("sp", wb_[:], cd, reads=[cdep_], writes=[wb_.d[0]])
                    else:
                        S.dma("sp", wb_[:, 0:nk, :], cd[:, 0:nk, :], reads=[cdep_], writes=[wb_.d[0]])
                    return wb_
                wb_ = _load_cast(src, fr, br, nk)
                if key is not None:
                    cd = nc.dram_tensor("wc_%d_%s" % (l, key), shp, BF16, kind="Internal").ap()
                    cdep_ = Dep()
                    if nk is None:
                        S.dma("sp", cd, wb_[:], reads=[wb_.d[0]], writes=[cdep_])
                    else:
                        S.dma("sp", cd[:, 0:nk, :], wb_[:, 0:nk, :], reads=[wb_.d[0]], writes=[cdep_])
                    wcache[key] = (cd, cdep_)
                return wb_

            def _load_cast(src, fr, br, nk=None):
                wf_ = fr.next()
                wb_ = br.next()
                if nk is None:
                    S.dma("sp", wf_[:], src, writes=[wf_.d[0]])
                    ec[0] += 1
                    evac(ec[0], wb_[:], wf_[:], [wf_.d[0]], [wb_.d[0]])
                else:
                    S.dma("sp", wf_[:, 0:nk, :], src, writes=[wf_.d[0]])
                    ec[0] += 1
                    evac(ec[0], wb_[:, 0:nk, :], wf_[:, 0:nk, :], [wf_.d[0]], [wb_.d[0]])
                return wb_

            def headnorm_gate(blk, src2, nw_ap, gate_c0, gate_fn, dstT):
                wA = load_cast(win_src[:, :, gate_c0:gate_c0 + 256], wgf, wgb, key="in%d" % gate_c0)
                wB = load_cast(win_src[:, :, gate_c0 + 256:gate_c0 + 512], wgf, wgb, key="in%d" % (gate_c0 + 256))
                for tt in range(4):
                    r0 = blk * 512 + tt * 128
                    ha, hb, sq, gs = tkr.next(), tkr.next(), tkr.next(), tkr.next()
                    S.dma("sp", ha[:], src2[0, r0:r0 + 128, :], writes=[ha.d[0]])
                    S.dma("sp", hb[:], src2[1, r0:r0 + 128, :], writes=[hb.d[0]])
                    S.op("pool", lambda e: e.tensor_tensor(ha[:], ha[:], hb[:], ALU.add), reads=[ha.d[0], hb.d[0]], writes=[ha.d[0]])
                    S.op("pool", lambda e: e.tensor_tensor(sq[:], ha[:], ha[:], ALU.mult), reads=[ha.d[0]], writes=[sq.d[0]])
                    ss = ssr.next()
                    S.op("dve", lambda e: e.tensor_reduce(ss[:], sq[:].rearrange("p (h d) -> p h d", h=4), AX.X, ALU.add),
                         reads=[sq.d[0]], writes=[ss.d[0]])
                    S.op("dve", lambda e: e.tensor_scalar(ss[:], ss[:], 1.0 / 128, EPS, ALU.mult, ALU.add), reads=[ss.d[0]], writes=[ss.d[0]])
                    S.op("act", lambda e: e.activation(ss[:], ss[:], AF.Sqrt), reads=[ss.d[0]], writes=[ss.d[0]])
                    S.op("dve", lambda e: e.reciprocal(ss[:], ss[:]), reads=[ss.d[0]], writes=[ss.d[0]])
                    S.op("dve", lambda e: e.tensor_tensor(
                        ha[:].rearrange("p (h d) -> p h d", h=4), ha[:].rearrange("p (h d) -> p h d", h=4),
                        ss[:].unsqueeze(2).to_broadcast([128, 4, 128]), ALU.mult), reads=[ha.d[0], ss.d[0]], writes=[ha.d[0]])
                    S.op("pool", lambda e: e.tensor_tensor(
                        ha[:].rearrange("p (h d) -> p h d", h=4), ha[:].rearrange("p (h d) -> p h d", h=4), nw_ap, ALU.mult),
                        reads=[ha.d[0], nwm.d[0], nwg.d[0]], writes=[ha.d[0]])
                    pg = psr.next()
                    for half, wX in ((0, wA), (1, wB)):
                        for kt in range(KT):
                            S.op("pe", lambda e, half=half, wX=wX, kt=kt: e.matmul(
                                pg[:, half * 256:(half + 1) * 256], h1[:, kt, tt * 128:(tt + 1) * 128], wX[:, kt, :],
                                start=(kt == 0), stop=(kt == KT - 1)), reads=[h1.d[0], wX.d[0]], writes=[pg.d[0]])
                    S.op("act", lambda e: e.activation(gs[:], pg[:, :], gate_fn), reads=[pg.d[0]], writes=[gs.d[0]])
                    S.op("dve", lambda e: e.tensor_tensor(ha[:], ha[:], gs[:], ALU.mult), reads=[ha.d[0], gs.d[0]], writes=[ha.d[0]])
                    pt = psr.next()
                    for ct in range(4):
                        S.op("pe", lambda e, ct=ct: e.transpose(pt[:, ct * 128:(ct + 1) * 128], ha[:, ct * 128:(ct + 1) * 128], ident),
                             reads=[ha.d[0], cdep], writes=[pt.d[0]])
                    ec[0] += 1
                    evac(ec[0], dstT[:, :, tt * 128:(tt + 1) * 128], pt[:, :].rearrange("p (c t) -> p c t", c=4), [pt.d[0]], [dstT.d[0]])

            st0 = ExitStack()
            wgf = Ring([sb("wgf%d_%d" % (l, i), [128, KT, 256], stack=st0) for i in range(2)])
            wdf = Ring([sb("wdf%d_%d" % (l, i), [128, FT, 128], stack=st0) for i in range(2)])
            for blk in range(NBLK):
                if blk == 1:
                    S.barrier()
                    st0.close()
                    xbr.bufs.append(sb("cxb%d_b" % l, [128, KT, 512], stack=st))
                    tmpr.bufs.append(sb("ctmp%d_b" % l, [128, KT, 512], stack=st))
                xb = xbr.next()
                tmp = tmpr.next()
                h2 = h2r.next()
                aT = actr.next()
                S.dma("sp", xb[:], xT_scr[:, :, blk * 512:(blk + 1) * 512], writes=[xb.d[0]])
                bs = slice(blk * 512, (blk + 1) * 512)
                rmsnorm_block(st, xb, l, 0, blk, tmp, out=h1)
                headnorm_gate(blk, h_ml, nwm[:].rearrange("p (h d) -> p h d", h=4), C_MO, AF.Sigmoid, ysT[0])
                headnorm_gate(blk, o_gd, nwg[:].unsqueeze(1).to_broadcast([128, 4, 128]), C_GZ, AF.Silu, ysT[2])
                y0, y1 = tmp[:, 0:4, :], tmp[:, 4:8, :]
                td = tmp.d[0]
                S.dma("sp", y0, y_s5T[0, :, :, bs], writes=[td])
                S.dma("sp", y1, y_s5T[1, :, :, bs], writes=[td])
                S.dma("sp", s5u[:], s5_uT[:, :, bs], writes=[s5u.d[0]])
                S.op("pool", lambda e: e.tensor_tensor(y0, y0, y1, ALU.add), reads=[td], writes=[td])
                for k4 in range(4):
                    S.op("dve", lambda e, k4=k4: e.scalar_tensor_tensor(
                        tmp[:, k4, :], s5u[:, k4, :], dbt[:, 0, k4:k4 + 1], tmp[:, k4, :], ALU.mult, ALU.add),
                        reads=[td, s5u.d[0], dbt.d[0]], writes=[td])
                S.op("pool", lambda e: e.tensor_tensor(y1, y0, y0, ALU.mult), reads=[td], writes=[td])
                S.op("dve", lambda e: e.tensor_scalar(y1, y1, 0.044715, 1.0, ALU.mult, ALU.add), reads=[td], writes=[td])
                S.op("pool", lambda e: e.tensor_tensor(y1, y1, y0, ALU.mult), reads=[td], writes=[td])
                S.op("act", lambda e: e.activation(y1, y1, AF.Sigmoid, scale=1.5957691216), reads=[td], writes=[td])
                S.op("dve", lambda e: e.tensor_tensor(y0, y0, y1, ALU.mult), reads=[td], writes=[td])
                S.op("act", lambda e: e.copy(gbf[:], y0), reads=[td], writes=[gbf.d[0]])
                gwA = load_cast(glu_src[:, :, 0:256], wgf, wgb, nk=4, key="gluA")
                gwB = load_cast(glu_src[:, :, 256:512], wgf, wgb, nk=4, key="gluB")
                for ct in range(4):
                    gw = gwA if ct < 2 else gwB
                    pq = psr.next()
                    for k4 in range(4):
                        S.op("pe", lambda e, k4=k4, gw=gw, ct=ct: e.matmul(
                            pq[:, :], gw[:, k4, (ct % 2) * 128:(ct % 2 + 1) * 128], gbf[:, k4, :], start=(k4 == 0), stop=(k4 == 3)),
                            reads=[gw.d[0], gbf.d[0]], writes=[pq.d[0]])
                    sg = sgr.next()
                    S.op("act", lambda e, ct=ct, sg=sg, pq=pq: e.activation(sg[:], pq[:, :], AF.Sigmoid, bias=dbt[:, 1, ct:ct + 1]),
                         reads=[pq.d[0], dbt.d[0]], writes=[sg.d[0]])
                    S.op("dve", lambda e, ct=ct, sg=sg: e.tensor_tensor(ysT[1][:, ct, :], tmp[:, ct, :], sg[:], ALU.mult),
                         reads=[td, sg.d[0]], writes=[ysT[1].d[0]])
                for dp in range(4):
                    for n in range(3):
                        wbr = load_cast(wbr_src[n][:, :, dp * 256:(dp + 1) * 256], wgf, wgb, nk=4, key="br%d_%d" % (n, dp))
                        wgn = load_cast(win_src[:, :, C_GATES + n * 1024 + dp * 256:C_GATES + n * 1024 + (dp + 1) * 256], wgf, wgb, key="g%d_%d" % (n, dp))
                        for d2 in range(2):
                            mac = maccs[d2]
                            pP, pG = psr.next(), psr.next()
                            for k4 in range(4):
                                S.op("pe", lambda e, k4=k4, pP=pP: e.matmul(
                                    pP[:, :], wbr[:, k4, d2 * 128:(d2 + 1) * 128], ysT[n][:, k4, :], start=(k4 == 0), stop=(k4 == 3)),
                                    reads=[wbr.d[0], ysT[n].d[0]], writes=[pP.d[0]])
                            for kt in range(KT):
                                S.op("pe", lambda e, kt=kt, pG=pG: e.matmul(
                                    pG[:, :], wgn[:, kt, d2 * 128:(d2 + 1) * 128], h1[:, kt, :], start=(kt == 0), stop=(kt == KT - 1)),
                                    reads=[wgn.d[0], h1.d[0]], writes=[pG.d[0]])
                            sg = sgr.next()
                            S.op("act", lambda e, sg=sg, pG=pG: e.activation(sg[:], pG[:, :], AF.Sigmoid), reads=[pG.d[0]], writes=[sg.d[0]])
                            if n == 0:
                                S.op("dve", lambda e, sg=sg, pP=pP, mac=mac: e.tensor_tensor(mac[:], sg[:], pP[:, :], ALU.mult),
                                     reads=[sg.d[0], pP.d[0]], writes=[mac.d[0]])
                            else:
                                S.op("dve", lambda e, sg=sg, pP=pP: e.tensor_tensor(sg[:], sg[:], pP[:, :], ALU.mult),
                                     reads=[sg.d[0], pP.d[0]], writes=[sg.d[0]])
                                S.op("pool", lambda e, sg=sg, mac=mac: e.tensor_tensor(mac[:], mac[:], sg[:], ALU.add),
                                     reads=[sg.d[0], mac.d[0]], writes=[mac.d[0]])
                    for d2 in range(2):
                        S.op("act", lambda e, d2=d2: e.copy(mg[:, dp * 2 + d2, :], maccs[d2][:]), reads=[maccs[d2].d[0]], writes=[mg.d[0]])
                for oc in range(4):
                    wo = load_cast(wout_src[:, :, oc * 256:(oc + 1) * 256], wgf, wgb, key="wo%d" % oc)
                    for o2 in range(2):
                        ot = oc * 2 + o2
                        pd = psr.next()
                        for kt in range(KT):
                            S.op("pe", lambda e, kt=kt, pd=pd, o2=o2, wo=wo: e.matmul(
                                pd[:, :], wo[:, kt, o2 * 128:(o2 + 1) * 128], mg[:, kt, :], start=(kt == 0), stop=(kt == KT - 1)),
                                reads=[wo.d[0], mg.d[0]], writes=[pd.d[0]])
                        S.op("dve", lambda e, ot=ot, pd=pd: e.scalar_tensor_tensor(
                            xb[:, ot, :], pd[:, :], modT[:, l, 16 + ot:17 + ot], xb[:, ot, :], ALU.mult, ALU.add),
                            reads=[pd.d[0], modT.d[0], xb.d[0]], writes=[xb.d[0]])
                rmsnorm_block(st, xb, l, 1, blk, tmp, out=h2)
                for fc in range(FT // 2):
                    wg = load_cast(wg_src[:, :, fc * 256:(fc + 1) * 256], wgf, wgb, key="fg%d" % fc)
                    wu = load_cast(wu_src[:, :, fc * 256:(fc + 1) * 256], wgf, wgb, key="fu%d" % fc)
                    for f2 in range(2):
                        ft = fc * 2 + f2
                        pgate = psr.next()
                        pup = psr.next()
                        for (pp, ww) in ((pgate, wg), (pup, wu)):
                            for kt in range(KT):
                                S.op("pe", lambda e, kt=kt, pp=pp, ww=ww, f2=f2: e.matmul(
                                    pp[:, :], ww[:, kt, f2 * 128:(f2 + 1) * 128], h2[:, kt, :],
                                    start=(kt == 0), stop=(kt == KT - 1)),
                                    reads=[ww.d[0], h2.d[0]], writes=[pp.d[0]])
                        sg = sgr.next()
                        S.op("act", lambda e, sg=sg, pgate=pgate: e.activation(sg[:], pgate[:, :], AF.Silu),
                             reads=[pgate.d[0]], writes=[sg.d[0]])
                        S.op("dve", lambda e, sg=sg, pup=pup, ft=ft: e.tensor_tensor(aT[:, ft, :], sg[:], pup[:, :], ALU.mult),
                             reads=[sg.d[0], pup.d[0]], writes=[aT.d[0]])
                for oc in range(8):
                    wd = load_cast(wd_src[:, :, oc * 128:(oc + 1) * 128], wdf, wdb, key="fd%d" % oc)
                    for o2 in range(1):
                        ot = oc
                        pd = psr.next()
                        for ft in range(FT):
                            S.op("pe", lambda e, ft=ft, pd=pd, o2=o2: e.matmul(
                                pd[:, :], wd[:, ft, o2 * 128:(o2 + 1) * 128], aT[:, ft, :],
                                start=(ft == 0), stop=(ft == FT - 1)),
                                reads=[wd.d[0], aT.d[0]], writes=[pd.d[0]])
                        S.op("dve", lambda e, ot=ot, pd=pd: e.scalar_tensor_tensor(
                            xb[:, ot, :], pd[:, :], modT[:, l, 40 + ot:41 + ot], xb[:, ot, :], ALU.mult, ALU.add),
                            reads=[pd.d[0], modT.d[0], xb.d[0]], writes=[xb.d[0]])
                if not last:
                    S.dma("sp", xT_scr[:, :, blk * 512:(blk + 1) * 512], xb[:], reads=[xb.d[0]])
                else:
                    yb = tmp
                    rmsnorm_block(st, xb, l, 2, blk, tmp, out=yb)
                    for tt in range(4):
                        for half in range(2):
                            yt = ytr.next()
                            pt = psr.next()
                            for k4 in range(4):
                                kt = half * 4 + k4
                                S.op("pe", lambda e, kt=kt, k4=k4, tt=tt, pt=pt: e.transpose(
                                    pt[:, k4 * 128:(k4 + 1) * 128], yb[:, kt, tt * 128:(tt + 1) * 128], ident),
                                    reads=[yb.d[0], cdep], writes=[pt.d[0]])
                            ec[0] += 1
                            evac(ec[0], yt[:], pt[:, :], [pt.d[0]], [yt.d[0]])
                            r0 = blk * 512 + tt * 128
                            S.dma("sp", y_out[r0:r0 + 128, half * 512:(half + 1) * 512], yt[:], reads=[yt.d[0]])
    S.finish([])
    return nc, S


def _unused():
    pass


def _consts():
    c = np.zeros((128, 8, 128), np.float32)
    i = np.arange(128)
    c[:, 0, :] = np.eye(128)
    c[:, 1, :] = 1.0
    c[:, 2, :] = (i[:, None] <= i[None, :])
    c[:, 3, :] = (i[:, None] >= i[None, :])
    c[:, 4, :] = (i[:, None] < i[None, :])
    c[:, 5, :] = (i[:, None] > i[None, :])
    c[:, 6, :] = (i[:, None] // 32 == i[None, :] // 32)
    c[:, 7, :] = (i[None, :] + 1).astype(np.float32)
    return c


def _pos_embed_T():
    t = np.arange(T)
    quarter = D // 4
    omega = (1.0 / (10000.0 ** (np.arange(quarter, dtype=np.float32) / quarter))).astype(np.float32)

    def enc(pos):
        ang = pos.astype(np.float32)[:, None] * omega[None, :]
        return np.concatenate([np.sin(ang), np.cos(ang)], axis=-1)
    pe = np.concatenate([enc(t // 64), enc(t % 64)], axis=-1).astype(np.float32)
    return np.ascontiguousarray(pe.T.reshape(KT, 128, T).transpose(1, 0, 2))


def _fm(v):
    v = np.asarray(v, np.float32)
    lead = v.shape[:-1]
    return np.ascontiguousarray(np.moveaxis(v.reshape(lead + (KT, 128)), -1, 0))


def _prompt_map():
    m = []
    counts = [6, 6, 5, 5, 5, 5]
    s = 0
    for ci, n in enumerate(counts):
        for j in range(n):
            m.append((2 + ci, j))
            s += 1
    return m


def prep_inputs(inp):
    f = lambda k: np.asarray(inp[k], np.float32)
    pm = _prompt_map()
    shared = {
        "w_branch": f("w_branch"), "w_out": f("w_out"), "glu_w": f("s5_glu_w"),
        "s5DbT": np.ascontiguousarray(np.stack([f("s5_D"), f("s5_glu_b")], axis=1).reshape(DEPTH, 2, 4, 128).transpose(3, 0, 1, 2)),
        "mlnw": np.ascontiguousarray(np.broadcast_to(f("ml_norm_w")[None], (128, DEPTH, 512))),
        "gdnw": np.ascontiguousarray(np.broadcast_to(f("gd_norm_w")[None], (128, DEPTH, 128))),
        "ada_w": f("ada_w"), "w_in": f("w_in"), "w_gate": f("w_gate"), "w_up": f("w_up"), "w_down": f("w_down"),
        "ada_bT": np.ascontiguousarray(f("ada_b").reshape(DEPTH, 48, 128).transpose(2, 0, 1)),
        "n1wT": _fm(f("norm1_w")), "n2wT": _fm(f("norm2_w")), "fnwT": _fm(f("final_norm_w")),
        "consts": _consts(),
    }
    gbias = np.concatenate([f("ml_i_bias").reshape(DEPTH, 8), f("ml_f_bias").reshape(DEPTH, 8),
                            f("gd_dt_bias").reshape(DEPTH, 8), f("gd_A_log").reshape(DEPTH, 8)], axis=1)
    shared["gbias"] = np.ascontiguousarray(np.broadcast_to(gbias[None], (128, DEPTH, 32)))
    cwv = f("gd_conv_w")
    shared["convw"] = np.ascontiguousarray(cwv.reshape(DEPTH, 5, 12, 128).transpose(3, 0, 2, 1))
    lre, lim, lst = f("s5_lam_re"), f("s5_lam_im"), f("s5_log_step")
    lst_e = np.broadcast_to(lst[..., None], lre.shape)
    flat3 = np.stack([lre, lim, lst_e], axis=2).reshape(DEPTH, 2, 3, 2048)
    shared["s5rep"] = np.ascontiguousarray(np.broadcast_to(flat3[None], (128, DEPTH, 2, 3, 2048)))
    shared["s5sm"] = np.ascontiguousarray(flat3.reshape(DEPTH, 2, 3, 16, 128).transpose(4, 0, 1, 2, 3))
    Bt = np.zeros((128, DEPTH, 2, 16, 128), np.float32)
    Ctt = np.zeros((128, DEPTH, 2, 16, 128), np.float32)
    for ri, (Bk, Ck) in enumerate((("s5_B_re", "s5_C_re"), ("s5_B_im", "s5_C_im"))):
        Bv, Cv = f(Bk), f(Ck)
        for g in range(32):
            j, half, gr_ = g // 2, g % 2, g % 8
            Bt[gr_ * 16:(gr_ + 1) * 16, :, ri, j, half * 64:(half + 1) * 64] = Bv[:, g].transpose(2, 0, 1)
            Ctt[half * 64:(half + 1) * 64, :, ri, j, gr_ * 16:(gr_ + 1) * 16] = Cv[:, g].transpose(2, 0, 1)
    shared["s5Bt"], shared["s5Ct"] = Bt, Ctt
    h0r, h0i = f("state_s5_re"), f("state_s5_im")
    posT = _pos_embed_T()
    C0, n0, m0 = f("state_mlstm_C"), f("state_mlstm_n"), f("state_mlstm_m")
    maps = []
    for c in range(8):
        m = dict(shared)
        x = np.zeros((T, D), np.float32)
        if c < 2:
            x[:] = f("x_sample")[c]
            m["cvec"] = _fm(f("c")[c])
            m["keep"] = np.ones((128, 1), np.float32)
            m["posT"] = posT
            caug = np.concatenate([C0[c], n0[c][..., None]], axis=-1)
            m["mlC0"] = np.ascontiguousarray(caug.transpose(3, 0, 1, 2, 4))
            m["m0rep"] = np.ascontiguousarray(np.broadcast_to(m0[c][None], (128, DEPTH, 2, 4)))
            m["m0h"] = np.ascontiguousarray(m0[c].reshape(DEPTH, 2, 4, 1))
            m["gdS0"] = np.ascontiguousarray(f("state_gdn_S")[c].transpose(3, 0, 1, 2, 4))
            hh = np.stack([h0r[c], h0i[c]], axis=2).reshape(DEPTH, 2, 2, 16, 128)
            m["s5h0"] = np.ascontiguousarray(hh.transpose(4, 0, 1, 2, 3))
        else:
            m["mlC0"] = np.zeros((128, DEPTH, 2, 4, 129), np.float32)
            m["m0rep"] = np.zeros((128, DEPTH, 2, 4), np.float32)
            m["m0h"] = np.zeros((DEPTH, 2, 4, 1), np.float32)
            m["gdS0"] = np.zeros((128, DEPTH, 2, 4, 128), np.float32)
            m["s5h0"] = np.zeros((128, DEPTH, 2, 2, 16), np.float32)
            for si, (cc, slot) in enumerate(pm):
                if cc == c:
                    x[slot * SLOT:(slot + 1) * SLOT] = f("x_prompt")[si]
            m["cvec"] = _fm(f("c_ctx"))
            m["keep"] = np.zeros((128, 1), np.float32)
            m["posT"] = np.zeros_like(posT)
        m["x"] = x
        maps.append(m)
    return maps


def kernel(**inputs):
    nc, S = build_program()
    maps = prep_inputs(inputs)
    res = run_bass_kernel_spmd(nc, maps, core_ids=list(range(8)))
    R = res.results
    ys = [np.asarray(r["y"], np.float32) for r in R]
    y_sample = np.stack([ys[0], ys[1]], axis=0)
    nb = 32
    y_prompt = np.zeros((nb, SLOT, D), np.float32)
    o_C = np.zeros((nb, DEPTH, 2, 4, 128, 128), np.float32)
    o_n = np.zeros((nb, DEPTH, 2, 4, 128), np.float32)
    o_m = np.zeros((nb, DEPTH, 2, 4), np.float32)
    o_re = np.zeros((nb, DEPTH, 2, 32, 64), np.float32)
    o_im = np.zeros((nb, DEPTH, 2, 32, 64), np.float32)
    o_S = np.zeros((nb, DEPTH, 2, 4, 128, 128), np.float32)
    for si, (c, slot) in enumerate(_prompt_map()):
        r = R[c]
        y_prompt[si] = ys[c][slot * SLOT:(slot + 1) * SLOT]
        ca = np.asarray(r["st_mlC"])[:, slot]
        o_C[si] = ca[..., :128].transpose(0, 1, 3, 2, 4)
        o_n[si] = ca[..., 128].transpose(0, 1, 3, 2)
        o_m[si] = np.asarray(r["st_mlm"])[:, slot, :, :, 0]
        s5 = np.asarray(r["st_s5"])[:, slot]
        o_re[si] = s5[:, :, :, 0, :].transpose(0, 1, 3, 2).reshape(DEPTH, 2, 32, 64)
        o_im[si] = s5[:, :, :, 1, :].transpose(0, 1, 3, 2).reshape(DEPTH, 2, 32, 64)
        o_S[si] = np.asarray(r["st_gdS"])[:, slot].transpose(0, 1, 3, 2, 4)
    return (y_prompt, y_sample, o_C, o_n, o_m, o_re, o_im, o_S)
```
